# Optimizing a Trainium2 kernel written in Bass

```python
import math
import jax, jax.numpy as jnp
from jax import lax
import numpy as np

D_MODEL = 1024
BATCH = 32
SEQ = 2048
DEPTH = 2
DEC_BATCH = 8
DEC_SEQ = 16
PAST_LEN = 1024

CHUNK = 64
HEAD_DIM = 64
SB_HEADS = 6
SB_WIDTH = SB_HEADS * HEAD_DIM
SB_BLOCK = 128
MLP_GROUPS = 4
MLP_WIDTH = MLP_GROUPS * HEAD_DIM
MLP_CHUNK = 128
HG_HEADS = 6
HG_KDIM = 64
HG_VDIM = 64
HG_WIDTH = HG_HEADS * HG_VDIM
HG_BLOCK = CHUNK
MIX_WIDTH = SB_WIDTH + MLP_WIDTH + HG_WIDTH
IN_WIDTH = 3 * SB_WIDTH + 2 * MLP_WIDTH + 2 * HG_HEADS * HG_KDIM + 2 * HG_WIDTH
D_FF = 4 * D_MODEL
ALPHA = (2 * DEPTH) ** 0.25
BETA_INIT = (8 * DEPTH) ** -0.25
LN_EPS = 1e-5
RMS_EPS = 1e-6

kernel_name = 'hybrid_stickbreak_gmlp_hgrn2_stream_step'


def layer_norm(x, g, b):
    xf = x.astype(jnp.float32)
    mu = jnp.mean(xf, axis=-1, keepdims=True)
    var = jnp.mean(jnp.square(xf - mu), axis=-1, keepdims=True)
    return ((xf - mu) * lax.rsqrt(var + LN_EPS) * g + b).astype(x.dtype)


def split_proj(p):
    sizes = (SB_WIDTH,) * 3 + (MLP_WIDTH,) * 2 + (HG_HEADS * HG_KDIM,) * 2 + (HG_WIDTH,) * 2
    out, off = [], 0
    for s in sizes:
        out.append(p[..., off:off + s])
        off += s
    return out


def stick_breaking(q, k, v, q_start):
    Lq, Lk = q.shape[1], k.shape[1]
    z = jnp.einsum('bqhd,bkhd->bhqk', q, k).astype(jnp.float32) / math.sqrt(HEAD_DIM)
    t_pos = q_start + jnp.arange(Lq)
    s_pos = jnp.arange(Lk)
    mask = s_pos[None, :] < t_pos[:, None]
    log_beta = jax.nn.log_sigmoid(z)
    log_keep = jnp.where(mask, jax.nn.log_sigmoid(-z), 0.0)
    later = lax.cumsum(log_keep, axis=3, reverse=True) - log_keep
    w = jnp.where(mask, jnp.exp(log_beta + later), 0.0)
    return jnp.einsum('bhqk,bkhd->bqhd', w.astype(v.dtype), v)


def stick_breaking_prompt(q, k, v):
    L = q.shape[1]
    outs = []
    for i in range(L // SB_BLOCK):
        lo, hi = i * SB_BLOCK, (i + 1) * SB_BLOCK
        outs.append(stick_breaking(q[:, lo:hi], k[:, :hi], v[:, :hi], lo))
    return jnp.concatenate(outs, axis=1)


def spatial_gate(u, v, ln_g, ln_b, w_s, b_s):
    B, L, _ = u.shape
    vn = layer_norm(v, ln_g, ln_b)
    nc = max(L // MLP_CHUNK, 1)
    cl = L // nc
    tri = jnp.tril(jnp.ones((cl, cl), dtype=bool))
    w = jnp.where(tri[None], w_s[:, :cl, :cl], 0.0)
    vc = vn.reshape(B, nc, cl, MLP_GROUPS, HEAD_DIM)
    mixed = jnp.einsum('gts,bcsgd->bctgd', w, vc) + jnp.transpose(b_s[:, :cl])[None, None, :, :, None]
    return u * mixed.reshape(B, L, MLP_WIDTH).astype(u.dtype), vn


def hgrn_lower_bounds(logits):
    g = jax.nn.softmax(logits.astype(jnp.float32), axis=0)
    c = jnp.cumsum(g, axis=0)
    return c - c[0:1]


def hgrn2_inputs(h_q, h_f, h_i, lb):
    B, L, _ = h_q.shape
    zf = h_f.astype(jnp.float32)
    log_f = jnp.logaddexp(jnp.log(lb), jnp.log1p(-lb) + jax.nn.log_sigmoid(zf))
    k = (1.0 - lb) * jax.nn.sigmoid(-zf)
    q = jax.nn.silu(h_q.astype(jnp.float32))
    v = h_i.astype(jnp.float32)
    heads = lambda a: a.reshape(B, L, HG_HEADS, -1).transpose(0, 2, 1, 3)
    return heads(q), heads(k), heads(v), heads(log_f)


def hgrn2_block(S, q, k, v, log_f):
    S = S.astype(jnp.float32)
    T = q.shape[2]
    b = jnp.cumsum(log_f, axis=2)
    causal = jnp.tril(jnp.ones((T, T), dtype=bool))
    rel = jnp.where(causal[None, None, :, :, None], b[:, :, :, None, :] - b[:, :, None, :, :], -jnp.inf)
    scores = jnp.einsum('bhtd,bhsd,bhtsd->bhts', q, k, jnp.exp(rel))
    o = jnp.einsum('bhts,bhse->bhte', scores, v) + jnp.einsum('bhtd,bhde->bhte', q * jnp.exp(b), S)
    b_last = b[:, :, -1:, :]
    S_new = jnp.exp(b_last[:, :, 0, :])[..., None] * S + jnp.einsum('bhsd,bhse->bhde', k * jnp.exp(b_last - b), v)
    return S_new, o


def hgrn2_prompt(q, k, v, log_f):
    B, H, L, _ = q.shape
    n = L // HG_BLOCK
    split = lambda a: jnp.moveaxis(a.reshape(B, H, n, HG_BLOCK, a.shape[-1]), 2, 0)
    S0 = jnp.zeros((B, H, HG_KDIM, HG_VDIM), jnp.float32)
    S, o = lax.scan(lambda s, xs: hgrn2_block(s, *xs), S0, (split(q), split(k), split(v), split(log_f)))
    return S, jnp.moveaxis(o, 0, 2).reshape(B, H, L, HG_VDIM)


def trunk(x, past_k, past_v, hg_state, w_in, w_out, mlp_ln_g, mlp_ln_b, mlp_ws, mlp_bs,
          hg_lb_logits, hg_norm_w, ln1_g, ln1_b, w_ff1, w_ff2, ln2_g, ln2_b):
    B, L, _ = x.shape
    prompt = past_k is None
    lbs = hgrn_lower_bounds(hg_lb_logits)
    new_k, new_v, new_s, new_mv = [], [], [], []
    for l in range(DEPTH):
        q_a, k_a, v_a, u_b, v_b, q_c, f_c, i_c, g_c = split_proj(x @ w_in[l])
        qa = q_a.reshape(B, L, SB_HEADS, HEAD_DIM)
        ka = k_a.reshape(B, L, SB_HEADS, HEAD_DIM)
        va = v_a.reshape(B, L, SB_HEADS, HEAD_DIM)
        if prompt:
            o_a = stick_breaking_prompt(qa, ka, va)
        else:
            o_a = stick_breaking(qa, jnp.concatenate([past_k[l].astype(ka.dtype), ka], axis=1),
                                 jnp.concatenate([past_v[l].astype(va.dtype), va], axis=1), past_k.shape[2])
        o_b, vn = spatial_gate(u_b, v_b, mlp_ln_g[l], mlp_ln_b[l], mlp_ws[l], mlp_bs[l])
        hq, hk, hv, hf = hgrn2_inputs(q_c, f_c, i_c, lbs[l])
        if prompt:
            S, o = hgrn2_prompt(hq, hk, hv, hf)
        else:
            S, o = hgrn2_block(hg_state[l], hq, hk, hv, hf)
        o = o.transpose(0, 2, 1, 3)
        o = o * lax.rsqrt(jnp.mean(o * o, axis=-1, keepdims=True) + RMS_EPS) * hg_norm_w[l].reshape(HG_HEADS, HG_VDIM)
        o_c = (o.reshape(B, L, HG_WIDTH) * jax.nn.sigmoid(g_c.astype(jnp.float32))).astype(x.dtype)
        mix = jnp.concatenate([o_a.reshape(B, L, SB_WIDTH).astype(x.dtype), o_b.astype(x.dtype), o_c], axis=-1)
        x = layer_norm(ALPHA * x + mix @ w_out[l], ln1_g[l], ln1_b[l])
        hdn = jnp.square(jax.nn.relu(x @ w_ff1[l]))
        x = layer_norm(ALPHA * x + hdn @ w_ff2[l], ln2_g[l], ln2_b[l])
        new_k.append(ka)
        new_v.append(va)
        new_s.append(S)
        new_mv.append(vn)
    mv = None if prompt else jnp.stack(new_mv)
    return x, jnp.stack(new_k), jnp.stack(new_v), jnp.stack(new_s), mv


def setup_inputs(seed: int = 0) -> dict:
    key = jax.random.key(seed)
    ks = jax.random.split(key, 20)
    nrm = lambda k, s, sc: jax.random.normal(k, s, jnp.float32) * sc
    return {
        'x_prompt': nrm(ks[0], (BATCH, SEQ, D_MODEL), 1.0),
        'x_sample': nrm(ks[1], (DEC_BATCH, DEC_SEQ, D_MODEL), 1.0),
        'cache_sb_k': nrm(ks[2], (DEPTH, DEC_BATCH, PAST_LEN, SB_HEADS, HEAD_DIM), 1.0),
        'cache_sb_v': nrm(ks[3], (DEPTH, DEC_BATCH, PAST_LEN, SB_HEADS, HEAD_DIM), 1.0),
        'state_hgrn': nrm(ks[4], (DEPTH, DEC_BATCH, HG_HEADS, HG_KDIM, HG_VDIM), 0.5),
        'w_in': nrm(ks[5], (DEPTH, D_MODEL, IN_WIDTH), D_MODEL ** -0.5),
        'w_out': nrm(ks[6], (DEPTH, MIX_WIDTH, D_MODEL), MIX_WIDTH ** -0.5 * BETA_INIT),
        'mlp_ln_g': 1.0 + nrm(ks[7], (DEPTH, MLP_WIDTH), 0.05),
        'mlp_ln_b': nrm(ks[8], (DEPTH, MLP_WIDTH), 0.02),
        'mlp_ws': nrm(ks[9], (DEPTH, MLP_GROUPS, MLP_CHUNK, MLP_CHUNK), MLP_CHUNK ** -0.5),
        'mlp_bs': 1.0 + nrm(ks[10], (DEPTH, MLP_GROUPS, MLP_CHUNK), 0.1),
        'hg_lb_logits': nrm(ks[11], (DEPTH, HG_HEADS * HG_KDIM), 0.5),
        'hg_norm_w': 1.0 + nrm(ks[12], (DEPTH, HG_WIDTH), 0.05),
        'ln1_g': 1.0 + nrm(ks[13], (DEPTH, D_MODEL), 0.05),
        'ln1_b': nrm(ks[14], (DEPTH, D_MODEL), 0.02),
        'w_ff1': nrm(ks[15], (DEPTH, D_MODEL, D_FF), D_MODEL ** -0.5),
        'w_ff2': nrm(ks[16], (DEPTH, D_FF, D_MODEL), D_FF ** -0.5 * BETA_INIT),
        'ln2_g': 1.0 + nrm(ks[17], (DEPTH, D_MODEL), 0.05),
        'ln2_b': nrm(ks[18], (DEPTH, D_MODEL), 0.02),
    }


def reference(x_prompt, x_sample, cache_sb_k, cache_sb_v, state_hgrn, w_in, w_out, mlp_ln_g, mlp_ln_b,
              mlp_ws, mlp_bs, hg_lb_logits, hg_norm_w, ln1_g, ln1_b, w_ff1, w_ff2, ln2_g, ln2_b):
    y_prompt, new_sb_k_prompt, new_sb_v_prompt, new_hgrn_prompt, _unused = trunk(
        x_prompt, None, None, None, w_in, w_out, mlp_ln_g, mlp_ln_b, mlp_ws, mlp_bs,
        hg_lb_logits, hg_norm_w, ln1_g, ln1_b, w_ff1, w_ff2, ln2_g, ln2_b)
    y_sample, new_sb_k_sample, new_sb_v_sample, new_hgrn_sample, new_mlp_v_sample = trunk(
        x_sample, cache_sb_k, cache_sb_v, state_hgrn, w_in, w_out, mlp_ln_g, mlp_ln_b, mlp_ws, mlp_bs,
        hg_lb_logits, hg_norm_w, ln1_g, ln1_b, w_ff1, w_ff2, ln2_g, ln2_b)
    return (y_prompt, y_sample, new_sb_k_prompt, new_sb_v_prompt, new_hgrn_prompt,
            new_sb_k_sample, new_sb_v_sample, new_hgrn_sample, new_mlp_v_sample)
```

```python
import contextlib
import math
import numpy as np
import concourse.bass as bass
import concourse.mybir as mybir
from concourse.bass_utils import run_bass_kernel_spmd

F32 = mybir.dt.float32
BF = mybir.dt.bfloat16
AF = mybir.ActivationFunctionType
ALU = mybir.AluOpType
AX = mybir.AxisListType

ENGS = ("pe", "act", "dve", "pool", "sp")
EPOCH = 24000


class Op:
    __slots__ = ("eng", "fn", "deps", "sig", "sigidx", "chan", "chan_idx", "fence")

    def __init__(self, eng, fn):
        self.eng = eng
        self.fn = fn
        self.deps = set()
        self.sig = False
        self.sigidx = 0
        self.chan = None
        self.chan_idx = 0
        self.fence = False


class Sched:
    def __init__(self):
        self.ops = {e: [] for e in ENGS}
        self.lastw = {}
        self.readers = {}
        self.chan_last = {}
        self.chan_cnt = {}
        self.aliases = {}

    def alias(self, ta, tb):
        for a in ta:
            self.aliases.setdefault(a, []).extend(tb)
        for b in tb:
            self.aliases.setdefault(b, []).extend(ta)

    def add(self, eng, fn, reads=(), writes=(), chan=None):
        op = Op(eng, fn)
        if chan is not None:
            op.chan = chan
            prev = self.chan_last.get(chan)
            if prev is not None:
                op.deps.add(prev)
            self.chan_cnt[chan] = self.chan_cnt.get(chan, 0) + 1
            op.chan_idx = self.chan_cnt[chan]
            self.chan_last[chan] = op
        lastw = self.lastw
        readers = self.readers
        for t in reads:
            w = lastw.get(t)
            if w is not None:
                op.deps.add(w)
        k = ("c", chan) if chan is not None else eng
        for t in writes:
            al = self.aliases.get(t)
            tl = (t,) if not al else [t] + al
            for t2 in tl:
                w = lastw.get(t2)
                if w is not None:
                    op.deps.add(w)
                r = readers.get(t2)
                if r:
                    op.deps.update(r.values())
        for t in reads:
            r = readers.get(t)
            if r is None:
                readers[t] = {k: op}
            else:
                r[k] = op
        for t in writes:
            lastw[t] = op
            readers[t] = {}
        op.deps.discard(op)
        self.ops[eng].append(op)
        return op

    def fence(self, tokens, engines=ENGS):
        deps = set()
        for t in tokens:
            w = self.lastw.get(t)
            if w is not None:
                deps.add(w)
            r = self.readers.get(t)
            if r:
                deps.update(r.values())
            self.lastw[t] = None
            self.readers[t] = {}
        for e in engines:
            op = Op(e, None)
            op.fence = True
            op.deps = set(deps)
            self.ops[e].append(op)

    def emit(self, nc, stack):
        for e in ENGS:
            for op in self.ops[e]:
                for d in op.deps:
                    if d.chan is None:
                        if d.eng == "pe" and e == "pe" and not op.fence:
                            continue
                        d.sig = True
        sems = {}
        for e in ENGS:
            c = 0
            for op in self.ops[e]:
                if op.chan is None and op.sig:
                    c += 1
                    op.sigidx = c
            n = (c + EPOCH - 1) // EPOCH
            sems[e] = [stack.enter_context(nc.semaphore(f"s_{e}_{i}")) for i in range(n)]
        csems = {}
        for i, ch in enumerate(self.chan_cnt):
            csems[ch] = stack.enter_context(nc.semaphore(f"c_{i}"))
        block = stack.enter_context(nc.Block())
        handles = {"pe": block.tensor, "act": block.scalar, "dve": block.vector,
                   "pool": block.gpsimd, "sp": block.sync}

        def make_prog(e):
            ops = self.ops[e]

            def prog(h):
                seen = {}
                for op in ops:
                    need = {}
                    for d in op.deps:
                        if d.chan is not None:
                            kk = ("c", d.chan)
                            v = d.chan_idx
                        else:
                            if d.eng == "pe" and e == "pe" and not op.fence:
                                continue
                            kk = d.eng
                            v = d.sigidx
                        if v > need.get(kk, 0):
                            need[kk] = v
                    for kk, v in need.items():
                        if seen.get(kk, 0) >= v:
                            continue
                        seen[kk] = v
                        if isinstance(kk, tuple):
                            h.wait_ge(csems[kk[1]], 16 * v)
                        else:
                            ep, val = divmod(v - 1, EPOCH)
                            h.wait_ge(sems[kk][ep], val + 1)
                    if op.fence:
                        continue
                    ins = op.fn(h)
                    if op.chan is not None:
                        ins.then_inc(csems[op.chan], 16)
                    elif op.sig:
                        ins.then_inc(sems[e][(op.sigidx - 1) // EPOCH], 1)
                for ch, cnt in self.chan_cnt.items():
                    if self.chan_last[ch].eng == e:
                        h.wait_ge(csems[ch], 16 * cnt)
            return prog

        for e in ENGS:
            if self.ops[e]:
                handles[e](make_prog(e))


DEPTH = 2
NH = 6
HD = 64
AW = 384
MW = 256
INW = 3200
MIXW = 1024
ALPHA = (2 * DEPTH) ** 0.25
LN_EPS = 1e-5
RMS_EPS = 1e-6
SECTIONS = [("qa", 0, 384), ("ka", 384, 384), ("va", 768, 384), ("ub", 1152, 256), ("vb", 1408, 256),
            ("qc", 1664, 384), ("fc", 2048, 384), ("ic", 2432, 384), ("gc", 2816, 384)]
OQ = 256
NCH = 8


class Cfg:
    def __init__(self, D=1024, DFF=4096, NSEQ=4, L=2048, LS=16, PAST=1024, HP=512, UT=4):
        self.D, self.DFF, self.NSEQ, self.L, self.LS, self.PAST, self.HP, self.UT = D, DFF, NSEQ, L, LS, PAST, HP, UT
        self.KC = D // 128
        self.NT = L // 128
        self.NP = DFF // HP
        self.HC = HP // 128
        self.NOQ = D // OQ
        self.LK = max(L, PAST + 128)
        self.KT = self.LK // 128


def make_consts():
    c = np.zeros((128, 1312), np.float32)
    s = np.arange(128)[:, None]
    t = np.arange(128)[None, :]
    c[:, 0:128] = (s == t)
    c[:, 128:256] = -(s >= t).astype(np.float32)
    c[:, 256:384] = -1.0
    c[:, 384:512] = (s < t)
    blk_s, blk_t = s // 64, t // 64
    same = (blk_s == blk_t)
    c[:, 512:640] = same & (s <= t)
    mid_t = blk_t * 64 + 31
    c[:, 640:768] = same * ((s <= t).astype(np.float32) - (s <= mid_t).astype(np.float32))
    m16 = np.zeros((128, 128), np.float32)
    m16[:16, :16] = (s[:16] <= t[:, :16])
    c[:, 768:896] = m16
    d16 = np.zeros((128, 128), np.float32)
    d16[:16, :16] = (s[:16] <= t[:, :16]).astype(np.float32) - (s[:16] <= 7).astype(np.float32)
    c[:, 896:1024] = d16
    c[:, 1024:1152] = (t <= s)
    sel = np.zeros((128, 8), np.float32)
    for b in range(2):
        inb = (s[:, 0] // 64 == b)
        mid = 64 * b + 31
        sel[:, 3 * b + 0] = inb & (s[:, 0] <= mid)
        sel[:, 3 * b + 1] = inb
        sel[:, 3 * b + 2] = inb & (s[:, 0] > mid)
    c[:, 1152:1160] = sel
    sel16 = np.zeros((128, 8), np.float32)
    sel16[:16, 0] = (s[:16, 0] <= 7)
    sel16[:16, 1] = 1.0
    sel16[:16, 2] = (s[:16, 0] > 7)
    c[:, 1160:1168] = sel16
    return c


def build(cfg):
    D, DFF, KC, NT, NSEQ, L, LS, PAST, HP, HC, NP_, NOQ = (cfg.D, cfg.DFF, cfg.KC, cfg.NT, cfg.NSEQ, cfg.L, cfg.LS,
                                                          cfg.PAST, cfg.HP, cfg.HC, cfg.NP, cfg.NOQ)
    KT, LK, UT = cfg.KT, cfg.LK, cfg.UT
    NU = UT * 128
    nc = bass.Bass("TRN2", target_bir_lowering=False)
    S = Sched()

    def din(name, shape):
        return nc.dram_tensor(name, list(shape), F32, kind="ExternalInput").ap()

    def dout(name, shape):
        return nc.dram_tensor(name, list(shape), F32, kind="ExternalOutput").ap()

    def dint(name, shape):
        return nc.dram_tensor(name, list(shape), BF, kind="Internal").ap()

    xp = din("xp", [NSEQ, L, D])
    xs = din("xs", [LS, D])
    ck = din("ck", [DEPTH, PAST, AW])
    cv = din("cv", [DEPTH, PAST, AW])
    sh = din("sh", [DEPTH, NH, HD, HD])
    w_in = din("w_in", [DEPTH, D, INW])
    w_out = din("w_out", [DEPTH, MIXW, D])
    w_ff1 = din("w_ff1", [DEPTH, D, DFF])
    w_ff2 = din("w_ff2", [DEPTH, DFF, D])
    mlp_ln_g = din("mlp_ln_g", [DEPTH, MW])
    mlp_ln_b = din("mlp_ln_b", [DEPTH, MW])
    mlp_ws = din("mlp_ws", [DEPTH, 4, 128, 128])
    mlp_bs = din("mlp_bs", [DEPTH, 4, 128])
    hg_lb = din("hg_lb_logits", [DEPTH, AW])
    hg_nw = din("hg_norm_w", [DEPTH, AW])
    ln_g = [din("ln1_g", [DEPTH, D]), din("ln2_g", [DEPTH, D])]
    ln_b = [din("ln1_b", [DEPTH, D]), din("ln2_b", [DEPTH, D])]
    cst = din("consts", [128, 1312])

    yp = dout("yp", [NSEQ, L, D])
    ys = dout("ys", [LS, D])
    kp = dout("kp", [DEPTH, NSEQ, L, AW])
    vp = dout("vp", [DEPTH, NSEQ, L, AW])
    hp = dout("hp", [DEPTH, NSEQ, NH, HD, HD])
    ks = dout("ks", [DEPTH, LS, AW])
    vs = dout("vs", [DEPTH, LS, AW])
    hs = dout("hs", [DEPTH, NH, HD, HD])
    mvs = dout("mvs", [DEPTH, LS, MW])

    wb_in = [[dint(f"wbin{l}_{i}", [128, KC, n]) for i, (_, _, n) in enumerate(SECTIONS)] for l in range(DEPTH)]
    wb_out = [[dint(f"wbout{l}_{q}", [128, NCH, OQ]) for q in range(NOQ)] for l in range(DEPTH)]
    wb_f1 = [[dint(f"wbf1{l}_{j}", [128, KC, HP]) for j in range(NP_)] for l in range(DEPTH)]
    wb_f2 = [[dint(f"wbf2{l}_{j}", [128, HC, D]) for j in range(NP_)] for l in range(DEPTH)]

    st = contextlib.ExitStack()
    with st:
        def sb(name, shape, dt):
            return st.enter_context(nc.sbuf_tensor(name, list(shape), dt))

        xres = sb("xres", [128, NT, D], F32)
        lnrow = sb("lnrow", [128, 2, D], F32)
        mlprow = sb("mlprow", [128, 2, MW], F32)
        hgrow = sb("hgrow", [128, DEPTH, 2, AW], F32)
        nwrow = sb("nwrow", [128, AW], F32)
        Sst = sb("Sst", [64, NH, HD], F32)
        cf = sb("cf", [128, 400], F32)
        cb = sb("cb", [128, 640], BF)
        mh16b = sb("mh16b", [128, 128], BF)
        WTl = sb("WTl", [128, DEPTH, 4, 128], BF)
        bsT = sb("bsT", [128, DEPTH, 4], F32)
        xb = sb("xb", [128, D], BF)
        small = sb("small", [128, 64], F32)

        identb = cb[:, 0:128]
        ntrib = cb[:, 128:256]
        nonesb = cb[:, 256:384]
        maskSb = cb[:, 384:512]
        maskHb = cb[:, 512:640]
        Dm = cf[:, 0:128]
        Dm16 = cf[:, 128:256]
        trilf = cf[:, 256:384]
        SelC = cf[:, 384:392]
        SelC16 = cf[:, 392:400]

        off = [0]
        arena_views = []

        def carve(nbytes):
            o = off[0]
            off[0] += (nbytes + 63) // 64 * 64
            return o

        lay = {}
        for name, nb in [("kT", 3 * LK * 2), ("vfull", KT * AW * 2), ("xTmix", NCH * NU * 2),
                         ("qT", 3 * NU * 2), ("u", UT * MW * 2), ("vn", UT * MW * 2), ("logf", UT * AW * 4),
                         ("qc", UT * AW * 2), ("vb", UT * AW * 2), ("gc", UT * AW * 2), ("kk", UT * AW * 2),
                         ("ring", 3 * 6144), ("stage", 4 * AW * 4), ("tmp", 14336 + 18688)]:
            lay[name] = (carve(nb), nb)
        mixer_bytes = off[0]
        off[0] = 0
        flay = {}
        TG = min(512, NT * 128)
        for name, nb in [("x1T", KC * NT * 128 * 2), ("hT", 2 * HC * NT * 128 * 2), ("w1", 2 * KC * HP * 2),
                         ("w2", 2 * HC * D * 2), ("r", 2 * 512 * 2)]:
            flay[name] = (carve(nb), nb)
        ffn_bytes = off[0]
        ARENA = max(mixer_bytes, ffn_bytes)
        arena = sb("arena", [128, ARENA // 2], BF)

        def view(layd, name, dt, pat=None, **kw):
            o, nb = layd[name]
            v = arena[:, o // 2:(o + nb) // 2]
            if dt == F32:
                v = v.bitcast(F32)
            if pat:
                v = v.rearrange(pat, **kw)
            return v

        kT = view(lay, "kT", BF, "p (m k) -> p m k", m=3)
        vfull = view(lay, "vfull", BF, "p (t c) -> p t c", c=AW)
        mixT = view(lay, "xTmix", BF)[:, 0:NCH * NU].rearrange("p (k n) -> p k n", k=NCH)
        qT = view(lay, "qT", BF, "p (m n) -> p m n", m=3)
        ub = view(lay, "u", BF, "p (t c) -> p t c", c=MW)
        vn = view(lay, "vn", BF, "p (t c) -> p t c", c=MW)
        logf = view(lay, "logf", F32, "p (t c) -> p t c", c=AW)
        qcb = view(lay, "qc", BF, "p (t c) -> p t c", c=AW)
        vbb = view(lay, "vb", BF, "p (t c) -> p t c", c=AW)
        gcb = view(lay, "gc", BF, "p (t c) -> p t c", c=AW)
        kkb = view(lay, "kk", BF, "p (t c) -> p t c", c=AW)
        ring = view(lay, "ring", BF, "p (s n) -> p s n", s=3)
        stage = view(lay, "stage", F32, "p (s c) -> p s c", c=AW)
        tmpo = lay["tmp"][0]

        def tview(o, nbytes, dt):
            v = arena[:, (tmpo + o) // 2:(tmpo + o + nbytes) // 2]
            return v.bitcast(F32) if dt == F32 else v

        att = []
        for c in range(2):
            b0 = c * 7168
            att.append(dict(
                ez=[tview(b0, 1024, BF), tview(b0 + 1024, 1024, BF)],
                sp=[tview(b0 + 2048, 1024, BF), tview(b0 + 3072, 1024, BF)],
                w=[tview(b0 + 4096, 1024, BF), tview(b0 + 5120, 1024, BF)],
                rs=tview(b0 + 6144, 1024, BF)))
        assert KC * NU * 2 <= 14336
        xT = tview(0, KC * NU * 2, BF).rearrange("p (k n) -> p k n", k=KC)
        ho = [14336]

        def hv(nbytes, dt):
            o = ho[0]
            ho[0] += nbytes
            return tview(o, nbytes, dt)
        eb = hv(1536, F32)
        enb = hv(1536, F32)
        tq = hv(1536, F32)
        qhat = hv(768, BF)
        khat = hv(768, BF)
        qhT = hv(1536, BF).rearrange("p (h t) -> p h t", h=NH)
        QAB = hv(3072, BF).rearrange("p (h b t) -> p h b t", h=NH, b=2)
        khT = hv(1536, BF).rearrange("p (h t) -> p h t", h=NH)
        sc = hv(1536, BF).rearrange("p (h t) -> p h t", h=NH)
        Stl = hv(1536, BF).rearrange("p (b h e) -> p b h e", b=2, h=NH)
        dtmp = hv(1536, F32)
        sq = eb
        sig = enb
        oc32 = tq
        ocb = hv(768, BF)
        vn32 = eb
        obb = hv(512, BF)
        esel = hv(256, F32)
        assert ho[0] <= 14336 + 18688, ho[0]
        ATT_T = ["ez00", "ez01", "ez10", "ez11", "sp00", "sp01", "sp10", "sp11", "w00", "w01", "w10", "w11", "rs0", "rs1"]
        HG_T = ["eb", "enb", "tq", "qhat", "khat", "qhT", "QAB", "khT", "sc", "Stl", "dtmp",
                "ocb", "obb", "esel"]
        XT_T = [("xT", i) for i in range(UT)]
        MIX_T = [("mixA", h) for h in range(NH)] + [("mixB", c, i) for c in range(5) for i in range(UT)]
        S.alias(XT_T, ATT_T)
        S.alias([("ps", 4)], ["psO0", "psO1"])

        x1T = view(flay, "x1T", BF, "p (k n) -> p k n", k=KC)
        hT = view(flay, "hT", BF, "p (s c n) -> p s c n", s=2, c=HC)
        w1r = view(flay, "w1", BF, "p (s k n) -> p s k n", s=2, k=KC)
        w2r = view(flay, "w2", BF, "p (s c n) -> p s c n", s=2, c=HC)
        rr = view(flay, "r", BF, "p (s n) -> p s n", s=2)

        MIXER_TOKS = (["kT%d_%d" % (m, u) for m in range(3) for u in range(KT)] + [("v", t) for t in range(KT)]
                      + XT_T + MIX_T + [("qT", m) for m in range(3)]
                      + [(nm, i) for nm in ("u", "vn", "logf", "qc", "vb", "gc", "kk") for i in range(UT)]
                      + [("ring", s_) for s_ in range(3)] + [("ringb", s_) for s_ in range(3)] + [("stage", s_) for s_ in range(4)] + ATT_T + HG_T)
        FFN_TOKS = ([("x1T", t) for t in range(NT)] + [("hT", s_, c) for s_ in range(2) for c in range(HC)]
                    + [("w1", s_) for s_ in range(2)] + [("w2", s_) for s_ in range(2)] + [("r", s_) for s_ in range(2)])

        pbanks = [st.enter_context(nc.psum_tensor(f"pb{i}", [128, 512], F32)) for i in range(8)]
        pcount = [0]

        def bank(i=None):
            if i is None:
                i = pcount[0] % 8
                pcount[0] += 1
            return pbanks[i], ("ps", i)

        def mm(out, lhsT, rhs, start, stop, reads, writes, sgc=False):
            if sgc:
                S.add("pe", lambda h: h.matmul(out, lhsT=lhsT, rhs=rhs, start=start, stop=stop, skip_group_check=True),
                      reads, writes)
            else:
                S.add("pe", lambda h: h.matmul(out, lhsT=lhsT, rhs=rhs, start=start, stop=stop), reads, writes)

        def tr(out, in_, ident, reads, writes):
            S.add("pe", lambda h: h.transpose(out=out, in_=in_, identity=ident), reads, writes)

        def act(out, in_, func, reads, writes, scale=None, bias=None):
            kw = {}
            if scale is not None:
                kw["scale"] = scale
            if bias is not None:
                kw["bias"] = bias
            S.add("act", lambda h: h.activation(out=out, in_=in_, func=func, **kw), reads, writes)

        def tt(eng, out, in0, in1, op, reads, writes):
            S.add(eng, lambda h: h.tensor_tensor(out=out, in0=in0, in1=in1, op=op), reads, writes)

        def tsc(eng, out, in0, s1, s2, op0, op1, reads, writes):
            if op1 is None:
                S.add(eng, lambda h: h.tensor_scalar(out=out, in0=in0, scalar1=s1, scalar2=None, op0=op0), reads, writes)
            else:
                S.add(eng, lambda h: h.tensor_scalar(out=out, in0=in0, scalar1=s1, scalar2=s2, op0=op0, op1=op1),
                      reads, writes)

        def stt(out, in0, scalar, in1, op0, op1, reads, writes):
            S.add("dve", lambda h: h.scalar_tensor_tensor(out=out, in0=in0, scalar=scalar, in1=in1, op0=op0, op1=op1),
                  reads, writes)

        def cp(eng, out, in_, reads, writes):
            if eng == "act":
                S.add("act", lambda h: h.activation(out=out, in_=in_, func=AF.Copy), reads, writes)
            else:
                S.add(eng, lambda h: h.tensor_copy(out=out, in_=in_), reads, writes)

        def recip(out, in_, reads, writes):
            S.add("dve", lambda h: h.reciprocal(out=out, in_=in_), reads, writes)

        def dma(eng, out, in_, chan, reads, writes, **kw):
            S.add(eng, lambda h: h.dma_start(out=out, in_=in_, **kw), reads, writes, chan=chan)

        dma("sp", cf[:, 0:128], cst[:, 640:768], "ld_c", [], ["cf"])
        dma("sp", cf[:, 128:256], cst[:, 896:1024], "ld_c", [], ["cf"])
        dma("sp", cf[:, 256:384], cst[:, 1024:1152], "ld_c", [], ["cf"])
        dma("sp", cf[:, 384:400], cst[:, 1152:1168], "ld_c", [], ["cf"])
        dma("pool", cb[:], cst[:, 0:640], "cv0", [], ["cb"])
        dma("pool", mh16b[:], cst[:, 768:896], "cv1", [], ["mh16b"])
        for l in range(DEPTH):
            dma("sp", bsT[:, l, :], mlp_bs[l].rearrange("g t -> t g"), "ld_c", [], ["bsT"],
                allow_slow_non_contiguous=True)
        L0 = eb[:, 0:AW]
        L1 = enb[:, 0:AW]
        T0 = tq[:, 0:AW]
        T1 = dtmp[:, 0:AW]
        dma("sp", L0, hg_lb[0:1, :].broadcast_to([128, AW]), "ld_c", [], ["eb"])
        dma("sp", L1, hg_lb[1:2, :].broadcast_to([128, AW]), "ld_c", [], ["enb"])
        tt("dve", T0, L0, L1, ALU.max, ["eb", "enb"], ["tq"])
        tt("dve", L0, L0, T0, ALU.subtract, ["eb", "tq"], ["eb"])
        tt("dve", L1, L1, T0, ALU.subtract, ["enb", "tq"], ["enb"])
        act(L0, L0, AF.Exp, ["eb"], ["eb"])
        act(L1, L1, AF.Exp, ["enb"], ["enb"])
        tt("dve", T0, L0, L1, ALU.add, ["eb", "enb"], ["tq"])
        recip(T0, T0, ["tq"], ["tq"])
        tt("dve", L0, L0, T0, ALU.mult, ["eb", "tq"], ["eb"])
        tt("dve", L1, L1, T0, ALU.mult, ["enb", "tq"], ["enb"])
        tt("dve", T1, L0, L1, ALU.add, ["eb", "enb"], ["dtmp"])
        tt("dve", hgrow[:, 0, 0, :], L0, L0, ALU.subtract, ["eb"], ["hgrow"])
        tt("dve", hgrow[:, 1, 0, :], T1, L0, ALU.subtract, ["dtmp", "eb"], ["hgrow"])
        for l in range(DEPTH):
            tsc("dve", hgrow[:, l, 1, :], hgrow[:, l, 0, :], -1.0, 1.0, ALU.mult, ALU.add, ["hgrow"], ["hgrow"])
        for l in range(DEPTH):
            wsrc = tview(14336, 2048, F32).rearrange("p (g s) -> p g s", g=4)
            dma("sp", wsrc, mlp_ws[l].rearrange("g t s -> t g s"), "ld_c", [], ["eb", "enb"])
            wsm = tview(14336 + 4608, 1024, BF).rearrange("p (g s) -> p g s", g=4)
            tt("dve", wsm, wsrc, trilf.unsqueeze(1).broadcast_to([128, 4, 128]), ALU.mult,
               ["eb", "enb", "cf"], ["qhat", "khat"])
            pb_, ptok = bank()
            pbb = pb_[:].bitcast(BF)
            for g in range(4):
                tr(pbb[:, g * 128:(g + 1) * 128], wsm[:, g, :], identb, ["qhat", "khat", "cb"], [ptok])
            cp("dve", WTl[:, l, :, :], pbb[:, 0:512].rearrange("p (g t) -> p g t", g=4), [ptok], ["WTl"])

        cvi = [0]

        def conv(out, in_, tok):
            dma("pool", out, in_, "cv%d" % (cvi[0] % 4), [], [tok])
            cvi[0] += 1

        def conv_layer_in(l):
            for i, (_, c0, n) in enumerate(SECTIONS):
                conv(wb_in[l][i], w_in[l][:, c0:c0 + n].rearrange("(k p) n -> p k n", p=128), ("wbin", l, i))
            for q in range(NOQ):
                conv(wb_out[l][q], w_out[l][:, q * OQ:(q + 1) * OQ].rearrange("(c p) n -> p c n", p=128), ("wbout", l, q))

        def conv_layer_ff(l):
            for j in range(NP_):
                conv(wb_f1[l][j], w_ff1[l][:, j * HP:(j + 1) * HP].rearrange("(k p) n -> p k n", p=128), ("wbf1", l, j))
                conv(wb_f2[l][j], w_ff2[l][j * HP:(j + 1) * HP, :].rearrange("(c p) n -> p c n", p=128), ("wbf2", l, j))

        conv_layer_in(0)
        conv_layer_ff(0)
        conv_layer_in(1)
        conv_layer_ff(1)

        NCK = (D + 511) // 512
        CW = D // NCK

        def layernorm_tile(P, t):
            xt_ = ("xres", t)
            stats = small[:, 0:6 * NCK]
            for i in range(NCK):
                S.add("dve", lambda h, i=i: h.bn_stats(out=small[:P, 6 * i:6 * i + 6], in_=xres[:P, t, i * CW:(i + 1) * CW]),
                      [xt_], ["small"])
            S.add("dve", lambda h: h.bn_aggr(out=small[:P, 32:34], in_=small[:P, 0:6 * NCK]), ["small"], ["small"])
            act(small[:P, 34:35], small[:P, 33:34], AF.Ln, ["small", "epsc"], ["small"], bias=epsln[:P, 0:1])
            act(small[:P, 35:36], small[:P, 34:35], AF.Exp, ["small"], ["small"], scale=-0.5)
            tsc("dve", small[:P, 36:37], small[:P, 32:33], small[:P, 35:36], -1.0, ALU.mult, ALU.mult, ["small"], ["small2"])
            act(xres[:P, t, :], xres[:P, t, :], AF.Identity, [xt_, "small", "small2"], [xt_],
                scale=small[:P, 35:36], bias=small[:P, 36:37])
            tt("dve", xres[:P, t, :], xres[:P, t, :], lnrow[:P, 0, :], ALU.mult, [xt_, "lnrow"], [xt_])
            tt("pool", xres[:P, t, :], xres[:P, t, :], lnrow[:P, 1, :], ALU.add, [xt_, "lnrowb"], [xt_])

        epsc = sb("epsc", [128, 2], F32)
        epsln = epsc[:, 0:1]
        epsrms = epsc[:, 1:2]
        S.add("pool", lambda h: h.memset(epsc[:, 0:1], LN_EPS), [], ["epsc"])
        S.add("pool", lambda h: h.memset(epsc[:, 1:2], RMS_EPS), ["epsc"], ["epsc"])

        def make_xT(P, t, dst, dst_tok, col0):
            xt_ = ("xres", t)
            cp("dve", xb[:P, :], xres[:P, t, :], [xt_], ["xb"])
            S.add("act", lambda h: h.mul(out=xres[:P, t, :], in_=xres[:P, t, :], mul=ALPHA), [xt_], [xt_])
            pb_, ptok = bank()
            pbb = pb_[:].bitcast(BF)
            for kc in range(KC):
                tr(pbb[:, kc * P:(kc + 1) * P], xb[:P, kc * 128:(kc + 1) * 128], identb[:P, :P], ["xb", "cb"], [ptok])
            cp("dve", dst[:, :, col0:col0 + P], pbb[:, 0:KC * P].rearrange("p (k t) -> p k t", k=KC), [ptok], [dst_tok])

        def run_sequence(kind, si):
            prompt = kind == "p"
            P = 128 if prompt else LS
            ntile = NT if prompt else 1
            kt0 = 0 if prompt else PAST // 128
            nblk, BL = (2, 64) if prompt else (1, LS)
            Dm_ = Dm if prompt else Dm16
            Sel_ = SelC if prompt else SelC16
            mH_ = maskHb if prompt else mh16b
            units = [list(range(u, min(u + UT, ntile))) for u in range(0, ntile, UT)]
            x_src = xp[si] if prompt else xs
            y_dst = yp[si] if prompt else ys

            for tiles in units:
                t0 = tiles[0]
                if prompt:
                    dma("sp", xres[:, t0:t0 + len(tiles), :],
                        x_src[t0 * 128:(t0 + len(tiles)) * 128, :].rearrange("(t p) d -> p t d", p=128),
                        "ld_x", [], [("xres", t) for t in tiles])
                else:
                    dma("sp", xres[:P, 0, :], x_src, "ld_x", [], [("xres", 0)])

            for l in range(DEPTH):
                S.fence(FFN_TOKS)
                dma("sp", lnrow[:, 0, :], ln_g[0][l:l + 1, :].broadcast_to([128, D]), "ld_ln0", [], ["lnrow"])
                dma("sp", lnrow[:, 1, :], ln_b[0][l:l + 1, :].broadcast_to([128, D]), "ld_ln1", [], ["lnrowb"])
                dma("sp", mlprow[:, 0, :], mlp_ln_g[l:l + 1, :].broadcast_to([128, MW]), "ld_ln2", [], ["mlprow"])
                dma("sp", mlprow[:, 1, :], mlp_ln_b[l:l + 1, :].broadcast_to([128, MW]), "ld_ln3", [], ["mlprowb"])
                dma("sp", nwrow[:, :], hg_nw[l:l + 1, :].broadcast_to([128, AW]), "ld_ln4", [], ["nwrow"])
                if prompt:
                    S.add("pool", lambda h: h.memset(Sst[:], 0.0), [], ["Sst"])
                else:
                    dma("sp", Sst[:], sh[l].rearrange("h d e -> d h e"), "ld_s", [], ["Sst"])
                    npt = PAST // 128
                    dma("pool", vfull[:, 0:npt, :], cv[l].rearrange("(t p) c -> p t c", p=128), "ld_cv", [],
                        [("v", t) for t in range(npt)])
                    kc_tmp = tview(0, npt * AW * 2, BF).rearrange("p (t c) -> p t c", c=AW)
                    dma("pool", kc_tmp, ck[l].rearrange("(t p) c -> p t c", p=128), "ld_ck", [], ATT_T)
                    for t in range(npt):
                        pb_, ptok = bank()
                        pbb = pb_[:].bitcast(BF)
                        for m in range(3):
                            tr(pbb[:, m * 128:(m + 1) * 128], kc_tmp[:, t, m * 128:(m + 1) * 128], identb,
                               ATT_T + ["cb"], [ptok])
                        cp("dve", kT[:, :, t * 128:(t + 1) * 128], pbb[:, 0:384].rearrange("p (m k) -> p m k", m=3),
                           [ptok], ["kT%d_%d" % (m, t) for m in range(3)])

                ringcnt = [0]

                def load_block(blk):
                    slot = ringcnt[0] % 3
                    ringcnt[0] += 1
                    if blk[0] == "in":
                        i = blk[1]
                        n = SECTIONS[i][2]
                        dst = ring[:, slot, 0:KC * n].rearrange("p (k n) -> p k n", k=KC)
                        dma("sp", dst, wb_in[l][i], "ring%d" % slot, [("wbin", l, i)], [("ring", slot), ("ringb", slot)])
                    else:
                        i = blk[1]
                        dst = ring[:, slot, 0:NCH * OQ].rearrange("p (k n) -> p k n", k=NCH)
                        dma("sp", dst, wb_out[l][i], "ring%d" % slot, [("wbout", l, i)], [("ring", slot), ("ringb", slot)])
                    return slot

                def outproj_block(q, slot, tiles_o):
                    Wo = ring[:, slot, 0:NCH * OQ].rearrange("p (k n) -> p k n", k=NCH)
                    rtok = ("ring", slot)
                    for tl, t in enumerate(tiles_o):
                        col = tl * 128
                        pb_, ptok = bank()
                        for pr in range(3):
                            mm(pb_[:P, 0:OQ], mixT[:, pr, col:col + P], Wo[:, pr, :], pr == 0, False,
                               [("mixA", 2 * pr), ("mixA", 2 * pr + 1), rtok], [ptok])
                        for c in range(5):
                            mm(pb_[:P, 0:OQ], mixT[:, 3 + c, col:col + P], Wo[:, 3 + c, :], False, c == 4,
                               [("mixB", c, tl), ("ringb", slot)], [ptok])
                        tt("dve", xres[:P, t, q * OQ:(q + 1) * OQ], xres[:P, t, q * OQ:(q + 1) * OQ], pb_[:P, 0:OQ], ALU.add,
                           [("xres", t), ptok], [("xres", t)])

                def run_blocks(seq_, fns):
                    pcs = [i for i, bk in enumerate(seq_) if bk[0] in ("in", "out")]
                    slot_of = {}
                    nl = [0]

                    def ensure(upto_rank):
                        while nl[0] <= upto_rank and nl[0] < len(pcs):
                            slot_of[pcs[nl[0]]] = load_block(seq_[pcs[nl[0]]])
                            nl[0] += 1
                    rank = 0
                    for i, bk in enumerate(seq_):
                        if bk[0] in ("in", "out"):
                            ensure(rank + 2)
                            fns[bk[0]](bk, slot_of[i])
                            rank += 1
                        else:
                            fns[bk[0]](bk, None)

                pend = None
                for tiles in units:
                    nt_u = len(tiles)
                    N = (nt_u - 1) * 128 + P
                    kcol0 = (kt0 + tiles[0]) * 128
                    for tl, t in enumerate(tiles):
                        make_xT(P, t, xT, ("xT", tl), tl * 128)
                    xT_reads = [("xT", tl) for tl in range(nt_u)]

                    def inproj_block(pi, slot):
                        sname, c0_, n = SECTIONS[pi]
                        W = ring[:, slot, 0:KC * n].rearrange("p (k n) -> p k n", k=KC)
                        rtok = ("ring", slot)
                        rtok2 = ("ringb", slot)
                        if sname in ("qa", "ka"):
                            for m in range(3):
                                pb_, ptok = bank()
                                for kc in range(KC):
                                    mm(pb_[:, 0:N], W[:, kc, m * 128:(m + 1) * 128], xT[:, kc, 0:N], kc == 0, kc == KC - 1,
                                       xT_reads + [rtok, rtok2], [ptok])
                                if sname == "qa":
                                    act(qT[:, m, 0:N], pb_[:, 0:N], AF.Copy, [ptok], [("qT", m)], scale=0.125)
                                else:
                                    cp("dve", kT[:, m, kcol0:kcol0 + N], pb_[:, 0:N], [ptok],
                                       ["kT%d_%d" % (m, kt0 + t) for t in tiles])
                        if sname != "qa":
                            for tl, t in enumerate(tiles):
                                pb_, ptok = bank()
                                for kc in range(KC):
                                    mm(pb_[:P, 0:n], xT[:, kc, tl * 128:tl * 128 + P], W[:, kc, 0:n], kc == 0, kc == KC - 1,
                                       [("xT", tl), rtok, rtok2], [ptok])
                                ps_ = pb_[:P, 0:n]
                                tok_lo, tok_hi = t * 128, t * 128 + P
                                if sname == "ka":
                                    ss = (t % 2)
                                    cp("dve", stage[:P, ss, :], ps_, [ptok], [("stage", ss)])
                                    dst = kp[l, si, tok_lo:tok_hi, :] if prompt else ks[l]
                                    dma("sp", dst, stage[:P, ss, :], "st_k%d" % ss, [("stage", ss)], [])
                                elif sname == "va":
                                    ss = 2 + (t % 2)
                                    cp("act", stage[:P, ss, :], ps_, [ptok], [("stage", ss)])
                                    dst = vp[l, si, tok_lo:tok_hi, :] if prompt else vs[l]
                                    dma("sp", dst, stage[:P, ss, :], "st_v%d" % (ss - 2), [("stage", ss)], [])
                                    cp("pool", vfull[:P, kt0 + t, :], stage[:P, ss, :], [("stage", ss)], [("v", kt0 + t)])
                                elif sname == "ub":
                                    cp("act", ub[:P, tl, :], ps_, [ptok], [("u", tl)])
                                elif sname == "vb":
                                    S.add("dve", lambda h, ps_=ps_: h.bn_stats(out=small[:P, 40:46], in_=ps_), [ptok], ["smallg"])
                                    S.add("dve", lambda h: h.bn_aggr(out=small[:P, 46:48], in_=small[:P, 40:46]),
                                          ["smallg"], ["smallg"])
                                    act(small[:P, 48:49], small[:P, 47:48], AF.Ln, ["smallg", "epsc"], ["smallg"],
                                        bias=epsln[:P, 0:1])
                                    act(small[:P, 49:50], small[:P, 48:49], AF.Exp, ["smallg"], ["smallg"], scale=-0.5)
                                    tsc("dve", vn32[:P, 0:MW], ps_, small[:P, 46:47], small[:P, 49:50], ALU.subtract, ALU.mult,
                                        [ptok, "smallg"], ["eb"])
                                    tt("pool", vn32[:P, 0:MW], vn32[:P, 0:MW], mlprow[:P, 0, :], ALU.mult,
                                       ["eb", "mlprow"], ["eb"])
                                    tt("pool", vn32[:P, 0:MW], vn32[:P, 0:MW], mlprow[:P, 1, :], ALU.add,
                                       ["eb", "mlprowb"], ["eb"])
                                    cp("act", vn[:P, tl, :], vn32[:P, 0:MW], ["eb"], [("vn", tl)])
                                    if not prompt:
                                        dma("sp", mvs[l], vn32[:P, 0:MW], "st_m", ["eb"], [])
                                elif sname == "qc":
                                    cp("act", qcb[:P, tl, :], ps_, [ptok], [("qc", tl)])
                                elif sname == "fc":
                                    tA, tB, tC = eb[:P, 0:AW], enb[:P, 0:AW], tq[:P, 0:AW]
                                    act(tA, ps_, AF.Exp, [ptok], ["eb"], scale=-1.0)
                                    act(tB, tA, AF.Ln, ["eb"], ["enb"], bias=1.0)
                                    act(tB, tB, AF.Exp, ["enb"], ["enb"], scale=-1.0)
                                    tt("dve", tC, tA, tB, ALU.mult, ["eb", "enb"], ["tq"])
                                    tt("pool", kkb[:P, tl, :], tC, hgrow[:P, l, 1, :], ALU.mult, ["tq", "hgrow"], [("kk", tl)])
                                    tt("dve", tB, tB, hgrow[:P, l, 1, :], ALU.mult, ["enb", "hgrow"], ["enb"])
                                    tt("dve", tB, tB, hgrow[:P, l, 0, :], ALU.add, ["enb", "hgrow"], ["enb"])
                                    act(logf[:P, tl, :], tB, AF.Ln, ["enb"], [("logf", tl)])
                                elif sname == "ic":
                                    cp("dve", vbb[:P, tl, :], ps_, [ptok], [("vb", tl)])
                                elif sname == "gc":
                                    cp("act", gcb[:P, tl, :], ps_, [ptok], [("gc", tl)])

                    seq_ = []
                    oq = 0
                    for pi in range(len(SECTIONS)):
                        if pend is not None and pi % 2 == 0 and oq < NOQ:
                            seq_.append(("out", oq))
                            oq += 1
                        seq_.append(("in", pi))
                    if pend is not None:
                        while oq < NOQ:
                            seq_.append(("out", oq))
                            oq += 1
                        last_out = max(i for i, bk in enumerate(seq_) if bk[0] == "out")
                        lns = [("ln", t) for t in pend]
                        rest = seq_[last_out + 1:]
                        half = len(lns) // 2
                        seq_ = seq_[:last_out + 1] + lns[:half] + rest + lns[half:]
                    pend_ = pend
                    run_blocks(seq_, {"in": lambda bk, sl: inproj_block(bk[1], sl),
                                      "out": lambda bk, sl: outproj_block(bk[1], sl, pend_),
                                      "ln": lambda bk, sl: layernorm_tile(P, bk[1])})

                    blocks = []
                    for jb in range(kt0 + tiles[-1], -1, -1):
                        if jb >= kt0 + tiles[0]:
                            tl = jb - (kt0 + tiles[0])
                            blocks.append((jb, P, True, tl * 128))
                        else:
                            blocks.append((jb, 128, False, 0))
                    nst = len(blocks)
                    A_th = []
                    H_th = []

                    def s1(c, hh, k, par):
                        jb, Pk, diag, c0 = blocks[k]
                        pr, j = hh // 2, hh % 2
                        n = N - c0
                        zb, ztok = bank(c)
                        mm(zb[:Pk, 0:n], kT[64 * j:64 * j + 64, pr, jb * 128:jb * 128 + Pk], qT[64 * j:64 * j + 64, pr, c0:N],
                           True, True, ["kT%d_%d" % (pr, jb), ("qT", pr)], [ztok])

                    def s2(c, hh, k, par):
                        jb, Pk, diag, c0 = blocks[k]
                        n = N - c0
                        zb, ztok = bank(c)
                        A = att[c]
                        spb = A["sp"][par]
                        sptok = "sp%d%d" % (c, par)
                        ezb = A["ez"][par]
                        eztok = "ez%d%d" % (c, par)
                        act(ezb[:Pk, 0:n], zb[:Pk, 0:n], AF.Exp, [ztok], [eztok])
                        act(spb[:Pk, 0:n], ezb[:Pk, 0:n], AF.Ln, [eztok], [sptok], bias=1.0)
                        if diag:
                            tt("pool", spb[:Pk, 0:P], spb[:Pk, 0:P], maskSb[:Pk, :P], ALU.mult, [sptok, "cb"], [sptok])

                    def s3(c, hh, k, par):
                        jb, Pk, diag, c0 = blocks[k]
                        pr, j = hh // 2, hh % 2
                        n = N - c0
                        A = att[c]
                        spb = A["sp"][par]
                        sptok = "sp%d%d" % (c, par)
                        cbk, ctok = bank(2 + c)
                        if k == 0:
                            S.add("pool", lambda h, c=c: h.memset(att[c]["rs"][:, 0:N], 0.0), [], ["rs%d" % c])
                        mm(cbk[:Pk, 0:n], ntrib[:Pk, :Pk], spb[:Pk, 0:n], True, k == 0, [sptok, "cb"], [ctok])
                        if k > 0:
                            mm(cbk[:Pk, 0:n], nonesb[:, :Pk], A["rs"][:, c0:N], False, True, ["rs%d" % c, "cb"], [ctok])
                        if k < nst - 1:
                            tt("dve", A["rs"][:Pk, c0:N], A["rs"][:Pk, c0:N], spb[:Pk, 0:n], ALU.add,
                               ["rs%d" % c, sptok], ["rs%d" % c])

                    def s4(c, hh, k, par):
                        jb, Pk, diag, c0 = blocks[k]
                        n = N - c0
                        A = att[c]
                        wb_ = A["w"][par]
                        wtok = "w%d%d" % (c, par)
                        cbk, ctok = bank(2 + c)
                        act(wb_[:Pk, 0:n], cbk[:Pk, 0:n], AF.Exp, [ctok], [wtok])
                        tt("dve", wb_[:Pk, 0:n], wb_[:Pk, 0:n], A["ez"][par][:Pk, 0:n], ALU.mult,
                           [wtok, "ez%d%d" % (c, par)], [wtok])
                        if diag:
                            tt("pool", wb_[:Pk, 0:P], wb_[:Pk, 0:P], maskSb[:Pk, :P], ALU.mult, [wtok, "cb"], [wtok])

                    def s5(c, hh, k, par):
                        jb, Pk, diag, c0 = blocks[k]
                        n = N - c0
                        A = att[c]
                        wb_ = A["w"][par]
                        wtok = "w%d%d" % (c, par)
                        ob_, otok = pbanks[4], "psO%d" % c
                        mm(ob_[64 * c:64 * c + 64, c0:N], vfull[:Pk, jb, hh * 64:(hh + 1) * 64], wb_[:Pk, 0:n], k == 0, k == nst - 1,
                           [("v", jb), wtok], [otok], sgc=True)
                        if k == nst - 1:
                            cp("dve", mixT[64 * c:64 * c + 64, hh // 2, 0:N], ob_[64 * c:64 * c + 64, 0:N], [otok], [("mixA", hh)])

                    items = [[(hh, k) for hh in range(c, NH, 2) for k in range(nst)] for c in range(2)]
                    nit = len(items[0])
                    for i_ in range(nit + 2):
                        for fn_, ii in ((s1, i_), (s2, i_), (s3, i_ - 1), (s4, i_ - 1), (s5, i_ - 2)):
                            if 0 <= ii < nit:
                                A_th.append(lambda fn_=fn_, ii=ii: [fn_(c, items[c][ii][0], items[c][ii][1], ii % 2) for c in range(2)])

                    def hg_tile(tl, t):
                        col = tl * 128
                        X = {}
                        es3 = esel[0:64, 0:48].rearrange("p (h c) -> p h c", h=NH)
                        bc = lambda col_: es3[:, :, col_:col_ + 1].broadcast_to([64, NH, HD])
                        d3 = lambda ap_: ap_.rearrange("p (h e) -> p h e", h=NH)

                        def t1():
                            pg, X["pg"] = bank(5)
                            for g in range(4):
                                mm(pg[:P, g * 64:(g + 1) * 64], WTl[:P, l, g, :P], vn[:P, tl, g * 64:(g + 1) * 64], True, True,
                                   ["WTl", ("vn", tl)], [X["pg"]])

                        def t3():
                            pg = pbanks[5]
                            for g in range(4):
                                stt(obb[:P, g * 64:(g + 1) * 64], pg[:P, g * 64:(g + 1) * 64], bsT[:P, l, g:g + 1],
                                    ub[:P, tl, g * 64:(g + 1) * 64], ALU.add, ALU.mult, [X["pg"], "bsT", ("u", tl)], ["obb"])

                        def t2():
                            pbr, tok = bank(5)
                            mm(pbr[:P, 0:AW], Dm_[:P, :P], logf[:P, tl, :], True, True, ["cf", ("logf", tl)], [tok])
                            for hh in range(NH):
                                mm(pbr[0:64, AW + hh * 8:AW + hh * 8 + 8], logf[:P, tl, hh * 64:(hh + 1) * 64], Sel_[:P, 0:8], True, True,
                                   ["cf", ("logf", tl)], [tok])

                        def t4():
                            pbr, tok = bank(5)
                            act(eb[:P, 0:AW], pbr[:P, 0:AW], AF.Exp, [tok], ["eb"])
                            act(enb[:P, 0:AW], pbr[:P, 0:AW], AF.Exp, [tok], ["enb"], scale=-1.0)
                            act(esel[0:64, 0:48], pbr[0:64, AW:AW + 48], AF.Exp, [tok], ["esel"])

                        def t6():
                            pt, tok = bank(5)
                            ptb = pt[:].bitcast(BF)
                            for c in range(2):
                                tr(ptb[:, c * P:(c + 1) * P], obb[:P, c * 128:(c + 1) * 128], identb[:P, :P], ["obb", "cb"], [tok])

                        def t8():
                            pt, tok = bank(5)
                            ptb = pt[:].bitcast(BF)
                            cp("dve", mixT[:, 3:5, col:col + P], ptb[:, 0:2 * P].rearrange("p (c t) -> p c t", c=2), [tok],
                               [("mixB", 0, tl), ("mixB", 1, tl)])

                        def t5():
                            act(tq[:P, 0:AW], qcb[:P, tl, :], AF.Exp, [("qc", tl)], ["tq"], scale=-1.0)
                            act(tq[:P, 0:AW], tq[:P, 0:AW], AF.Ln, ["tq"], ["tq"], bias=1.0)
                            act(tq[:P, 0:AW], tq[:P, 0:AW], AF.Exp, ["tq"], ["tq"], scale=-1.0)

                        def t7():
                            tt("pool", khat[:P, 0:AW], enb[:P, 0:AW], kkb[:P, tl, :], ALU.mult, ["enb", ("kk", tl)], ["khat"])
                            tt("dve", tq[:P, 0:AW], tq[:P, 0:AW], qcb[:P, tl, :], ALU.mult, ["tq", ("qc", tl)], ["tq"])
                            tt("dve", qhat[:P, 0:AW], tq[:P, 0:AW], eb[:P, 0:AW], ALU.mult, ["tq", "eb"], ["qhat"])

                        def t9(b):
                            def f():
                                pd, tok = bank(7 - b)
                                for hh in range(NH):
                                    mm(pd[0:64, hh * 64:(hh + 1) * 64], khat[b * BL:(b + 1) * BL, hh * 64:(hh + 1) * 64],
                                       vbb[b * BL:(b + 1) * BL, tl, hh * 64:(hh + 1) * 64], True, True, ["khat", ("vb", tl)], [tok])
                            return f

                        def t11(b):
                            def f():
                                pd, tok = bank(7 - b)
                                tt("dve", Stl[0:64, b, :, :], Sst[:, :, :], bc(3 * b + 0), ALU.mult, ["Sst", "esel"], ["Stl"])
                                tt("dve", d3(dtmp[0:64, 0:AW]), d3(pd[0:64, 0:AW]), bc(3 * b + 2), ALU.mult, [tok, "esel"], ["dtmp"])
                                tt("dve", Sst[:, :, :], Sst[:, :, :], bc(3 * b + 1), ALU.mult, ["Sst", "esel"], ["Sst"])
                                tt("dve", Sst[:, :, :], Sst[:, :, :], d3(dtmp[0:64, 0:AW]), ALU.add, ["Sst", "dtmp"], ["Sst"])
                            return f

                        def t10a():
                            pq, tok = bank(7)
                            pqb = pq[:].bitcast(BF)
                            for hh in range(NH):
                                tr(pqb[0:64, hh * P:(hh + 1) * P], qhat[:P, hh * 64:(hh + 1) * 64], identb[:P, :P], ["qhat", "cb"], [tok])

                        def t12a():
                            pq, tok = bank(7)
                            pq3 = pq[:].bitcast(BF)[0:64, 0:NH * P].rearrange("p (h t) -> p h t", h=NH)
                            cp("dve", qhT[0:64, :, 0:P], pq3, [tok], ["qhT"])
                            S.add("pool", lambda h: h.memset(QAB[0:64, :, :, :], 0.0), [], ["QAB"])
                            for b in range(nblk):
                                cp("dve", QAB[0:64, :, b, b * BL:(b + 1) * BL], pq3[:, :, b * BL:(b + 1) * BL], [tok], ["QAB"])

                        def t10b():
                            pk_, tok = bank(6)
                            pkb = pk_[:].bitcast(BF)
                            for hh in range(NH):
                                tr(pkb[0:64, hh * P:(hh + 1) * P], khat[:P, hh * 64:(hh + 1) * 64], identb[:P, :P], ["khat", "cb"], [tok])

                        def t12b():
                            pk_, tok = bank(6)
                            cp("act", khT[0:64, :, 0:P], pk_[:].bitcast(BF)[0:64, 0:NH * P].rearrange("p (h t) -> p h t", h=NH),
                               [tok], ["khT"])

                        def t13():
                            act(sig[:P, 0:AW], gcb[:P, tl, :], AF.Exp, [("gc", tl)], ["enb"], scale=-1.0)
                            act(sig[:P, 0:AW], sig[:P, 0:AW], AF.Ln, ["enb"], ["enb"], bias=1.0)
                            act(sig[:P, 0:AW], sig[:P, 0:AW], AF.Exp, ["enb"], ["enb"], scale=-1.0)

                        def t14(g):
                            def f():
                                pscb, tok = bank(7 - g)
                                for hh in range(3 * g, 3 * g + 3):
                                    mm(pscb[:P, (hh % 3) * 128:(hh % 3) * 128 + P], khT[0:64, hh, 0:P], qhT[0:64, hh, 0:P], True, True,
                                       ["khT", "qhT"], [tok])
                            return f

                        def t15(g):
                            def f():
                                pscb, tok = bank(7 - g)
                                tt("dve", sc[:P, 3 * g:3 * g + 3, 0:P],
                                   pscb[:P, 0:384].rearrange("p (h t) -> p h t", h=3)[:, :, 0:P],
                                   mH_[:P, 0:P].unsqueeze(1).broadcast_to([P, 3, P]), ALU.mult, [tok, "cb", "mh16b"], ["sc"])
                            return f

                        def t16():
                            po, tok = bank(7)
                            for hh in range(NH):
                                mm(po[:P, hh * 64:(hh + 1) * 64], sc[:P, hh, 0:P], vbb[:P, tl, hh * 64:(hh + 1) * 64], True, False,
                                   ["sc", ("vb", tl)], [tok])
                                for b in range(nblk):
                                    mm(po[:P, hh * 64:(hh + 1) * 64], QAB[0:64, hh, b, 0:P], Stl[0:64, b, hh, :], False, b == nblk - 1,
                                       ["QAB", "Stl"], [tok])

                        def t17():
                            po, tok = bank(7)
                            act(sq[:P, 0:AW], po[:P, 0:AW], AF.Square, [tok], ["eb"])
                            S.add("dve", lambda h: h.tensor_reduce(out=small[:P, 52:58], in_=d3(sq[:P, 0:AW]),
                                                                   axis=AX.X, op=ALU.add), ["eb"], ["smallh"])
                            act(small[:P, 52:58], small[:P, 52:58], AF.Ln, ["smallh", "epsc"], ["smallh"], scale=1.0 / HD,
                                bias=epsrms[:P, 0:1])
                            act(small[:P, 52:58], small[:P, 52:58], AF.Exp, ["smallh"], ["smallh"], scale=-0.5)

                        def t18():
                            po, tok = bank(7)
                            tt("dve", d3(oc32[:P, 0:AW]), d3(po[:P, 0:AW]),
                               small[:P, 52:58].unsqueeze(2).broadcast_to([P, NH, HD]), ALU.mult, [tok, "smallh"], ["tq"])
                            tt("pool", oc32[:P, 0:AW], oc32[:P, 0:AW], nwrow[:P, :], ALU.mult, ["tq", "nwrow"], ["tq"])
                            tt("pool", ocb[:P, 0:AW], oc32[:P, 0:AW], sig[:P, 0:AW], ALU.mult, ["tq", "enb"], ["ocb"])

                        def t19():
                            pt2, tok = bank(6)
                            pt2b = pt2[:].bitcast(BF)
                            for c in range(3):
                                tr(pt2b[:, c * P:(c + 1) * P], ocb[:P, c * 128:(c + 1) * 128], identb[:P, :P], ["ocb", "cb"], [tok])

                        def t20():
                            pt2, tok = bank(6)
                            cp("dve", mixT[:, 5:8, col:col + P], pt2[:].bitcast(BF)[:, 0:3 * P].rearrange("p (c t) -> p c t", c=3), [tok],
                               [("mixB", 2 + c, tl) for c in range(3)])

                        seq_ = [t1, t3, t2, t4, t6, t8, t5, t7]
                        for b in range(nblk):
                            seq_ += [t9(b), t11(b)]
                        seq_ += [t10a, t12a, t10b, t12b, t13, t14(0), t15(0), t14(1), t15(1), t16, t17, t18, t19, t20]
                        return seq_

                    for tl, t in enumerate(tiles):
                        H_th += hg_tile(tl, t)
                    merged = [((i + 0.5) / len(A_th), 0, i, f) for i, f in enumerate(A_th)] + \
                             [((i + 0.5) / len(H_th), 1, i, f) for i, f in enumerate(H_th)]
                    merged.sort(key=lambda x: (x[0], x[1], x[2]))
                    for _, _, _, f in merged:
                        f()

                    pend = tiles

                seq_ = [("out", q) for q in range(NOQ)] + [("ln", t) for t in pend]
                pend_ = pend
                run_blocks(seq_, {"out": lambda bk, sl: outproj_block(bk[1], sl, pend_),
                                  "ln": lambda bk, sl: layernorm_tile(P, bk[1])})

                dst = hp[l, si] if prompt else hs[l]
                dma("sp", dst.rearrange("h d e -> d h e"), Sst[:], "st_s", ["Sst"], [])

                S.fence(MIXER_TOKS)
                dma("sp", lnrow[:, 0, :], ln_g[1][l:l + 1, :].broadcast_to([128, D]), "ld_ln0", [], ["lnrow"])
                dma("sp", lnrow[:, 1, :], ln_b[1][l:l + 1, :].broadcast_to([128, D]), "ld_ln1", [], ["lnrowb"])
                Ltok = (ntile - 1) * 128 + P

                def load_ff(j):
                    s_ = j % 2
                    dma("sp", w1r[:, s_, :, :], wb_f1[l][j], "w1_%d" % s_, [("wbf1", l, j)], [("w1", s_)])
                    dma("sp", w2r[:, s_, :, :], wb_f2[l][j], "w2_%d" % s_, [("wbf2", l, j)], [("w2", s_)])

                load_ff(0)
                for t in range(ntile):
                    make_xT(P, t, x1T, ("x1T", t), t * 128)
                groups = [(g0, min(g0 + 512, Ltok)) for g0 in range(0, Ltok, 512)]

                def ff1(j):
                    s_ = j % 2
                    for c in range(HC):
                        for gi, (g0, g1) in enumerate(groups):
                            n = g1 - g0
                            pb_, ptok = bank()
                            rd = [("x1T", t) for t in range(g0 // 128, (g1 + 127) // 128)] + [("w1", s_)]
                            for kc in range(KC):
                                mm(pb_[:, 0:n], w1r[:, s_, kc, c * 128:(c + 1) * 128], x1T[:, kc, g0:g1], kc == 0, kc == KC - 1, rd, [ptok])
                            rs_ = (c * len(groups) + gi) % 2
                            act(rr[:, rs_, 0:n], pb_[:, 0:n], AF.Relu, [ptok], [("r", rs_)])
                            tt("pool", hT[:, s_, c, g0:g1], rr[:, rs_, 0:n], rr[:, rs_, 0:n], ALU.mult, [("r", rs_)], [("hT", s_, c)])

                def ff2(j, last=False):
                    s_ = j % 2
                    for t in range(ntile):
                        for hf in range(D // 512 if D >= 512 else 1):
                            wcol = min(512, D)
                            pb_, ptok = bank()
                            for c in range(HC):
                                mm(pb_[:P, 0:wcol], hT[:, s_, c, t * 128:t * 128 + P], w2r[:, s_, c, hf * wcol:(hf + 1) * wcol],
                                   c == 0, c == HC - 1, [("hT", s_, c), ("w2", s_)], [ptok])
                            tt("dve", xres[:P, t, hf * wcol:(hf + 1) * wcol], xres[:P, t, hf * wcol:(hf + 1) * wcol], pb_[:P, 0:wcol],
                               ALU.add, [("xres", t), ptok], [("xres", t)])
                        if last:
                            layernorm_tile(P, t)
                            if l == DEPTH - 1:
                                dst = y_dst[t * 128:t * 128 + P, :]
                                dma("sp", dst, xres[:P, t, :], "st_y%d" % (t % 2), [("xres", t)], [])

                ff1(0)
                for j in range(NP_):
                    if j + 1 < NP_:
                        load_ff(j + 1)
                        ff1(j + 1)
                    ff2(j, last=(j == NP_ - 1))

        for si in range(NSEQ):
            run_sequence("p", si)
        run_sequence("s", 0)

        S.emit(nc, st)
    return nc


_CACHE = {}


def _get_nc(cfg_key, cfg):
    if cfg_key not in _CACHE:
        _CACHE[cfg_key] = build(cfg)
    return _CACHE[cfg_key]


def run_cfg(cfg, inputs, ncores):
    nc = build(cfg)
    consts = make_consts()
    f = lambda a: np.ascontiguousarray(np.asarray(a, dtype=np.float32))
    shared = {k: f(inputs[k]) for k in ("w_in", "w_out", "w_ff1", "w_ff2", "mlp_ln_g", "mlp_ln_b", "mlp_ws", "mlp_bs",
                                       "hg_lb_logits", "hg_norm_w", "ln1_g", "ln1_b", "ln2_g", "ln2_b")}
    xp_, xs_ = f(inputs["x_prompt"]), f(inputs["x_sample"])
    ck_, cv_, sh_ = f(inputs["cache_sb_k"]), f(inputs["cache_sb_v"]), f(inputs["state_hgrn"])
    NSEQ = cfg.NSEQ
    in_maps = []
    for c in range(ncores):
        m = dict(shared)
        m["xp"] = np.ascontiguousarray(xp_[c * NSEQ:(c + 1) * NSEQ])
        m["xs"] = np.ascontiguousarray(xs_[c])
        m["ck"] = np.ascontiguousarray(ck_[:, c].reshape(DEPTH, cfg.PAST, AW))
        m["cv"] = np.ascontiguousarray(cv_[:, c].reshape(DEPTH, cfg.PAST, AW))
        m["sh"] = np.ascontiguousarray(sh_[:, c])
        m["consts"] = consts
        in_maps.append(m)
    res = run_bass_kernel_spmd(nc, in_maps, core_ids=list(range(ncores)))
    R = res.results
    B = ncores * NSEQ
    y_prompt = np.concatenate([R[c]["yp"] for c in range(ncores)], axis=0)
    y_sample = np.stack([R[c]["ys"] for c in range(ncores)], axis=0)
    kpo = np.concatenate([R[c]["kp"] for c in range(ncores)], axis=1).reshape(DEPTH, B, cfg.L, NH, HD)
    vpo = np.concatenate([R[c]["vp"] for c in range(ncores)], axis=1).reshape(DEPTH, B, cfg.L, NH, HD)
    hpo = np.concatenate([R[c]["hp"] for c in range(ncores)], axis=1)
    kso = np.stack([R[c]["ks"] for c in range(ncores)], axis=1).reshape(DEPTH, ncores, cfg.LS, NH, HD)
    vso = np.stack([R[c]["vs"] for c in range(ncores)], axis=1).reshape(DEPTH, ncores, cfg.LS, NH, HD)
    hso = np.stack([R[c]["hs"] for c in range(ncores)], axis=1)
    mvo = np.stack([R[c]["mvs"] for c in range(ncores)], axis=1)
    return tuple(np.ascontiguousarray(a, dtype=np.float32) for a in
                 (y_prompt, y_sample, kpo, vpo, hpo, kso, vso, hso, mvo))


def kernel(x_prompt, x_sample, cache_sb_k, cache_sb_v, state_hgrn, w_in, w_out, mlp_ln_g, mlp_ln_b,
           mlp_ws, mlp_bs, hg_lb_logits, hg_norm_w, ln1_g, ln1_b, w_ff1, w_ff2, ln2_g, ln2_b):
    cfg = Cfg()
    inputs = dict(x_prompt=x_prompt, x_sample=x_sample, cache_sb_k=cache_sb_k, cache_sb_v=cache_sb_v,
                  state_hgrn=state_hgrn, w_in=w_in, w_out=w_out, mlp_ln_g=mlp_ln_g, mlp_ln_b=mlp_ln_b,
                  mlp_ws=mlp_ws, mlp_bs=mlp_bs, hg_lb_logits=hg_lb_logits, hg_norm_w=hg_norm_w,
                  ln1_g=ln1_g, ln1_b=ln1_b, w_ff1=w_ff1, w_ff2=w_ff2, ln2_g=ln2_g, ln2_b=ln2_b)
    return run_cfg(cfg, inputs, 8)
```

```python
import contextlib
import math
import numpy as np
import concourse.bass as bass
import concourse.mybir as mybir
from concourse.bass_utils import run_bass_kernel_spmd

F32 = mybir.dt.float32
BF = mybir.dt.bfloat16
AF = mybir.ActivationFunctionType
ALU = mybir.AluOpType
AX = mybir.AxisListType

ENGS = ("pe", "act", "dve", "pool", "sp")
EPOCH = 24000


class Op:
    __slots__ = ("eng", "fn", "deps", "sig", "sigidx", "chan", "chan_idx", "fence")

    def __init__(self, eng, fn):
        self.eng = eng
        self.fn = fn
        self.deps = set()
        self.sig = False
        self.sigidx = 0
        self.chan = None
        self.chan_idx = 0
        self.fence = False


class Sched:
    def __init__(self):
        self.ops = {e: [] for e in ENGS}
        self.lastw = {}
        self.readers = {}
        self.chan_last = {}
        self.chan_cnt = {}
        self.aliases = {}

    def alias(self, ta, tb):
        for a in ta:
            self.aliases.setdefault(a, []).extend(tb)
        for b in tb:
            self.aliases.setdefault(b, []).extend(ta)

    def add(self, eng, fn, reads=(), writes=(), chan=None):
        op = Op(eng, fn)
        if chan is not None:
            op.chan = chan
            prev = self.chan_last.get(chan)
            if prev is not None:
                op.deps.add(prev)
            self.chan_cnt[chan] = self.chan_cnt.get(chan, 0) + 1
            op.chan_idx = self.chan_cnt[chan]
            self.chan_last[chan] = op
        lastw = self.lastw
        readers = self.readers
        for t in reads:
            w = lastw.get(t)
            if w is not None:
                op.deps.add(w)
        k = ("c", chan) if chan is not None else eng
        for t in writes:
            al = self.aliases.get(t)
            tl = (t,) if not al else [t] + al
            for t2 in tl:
                w = lastw.get(t2)
                if w is not None:
                    op.deps.add(w)
                r = readers.get(t2)
                if r:
                    op.deps.update(r.values())
        for t in reads:
            r = readers.get(t)
            if r is None:
                readers[t] = {k: op}
            else:
                r[k] = op
        for t in writes:
            lastw[t] = op
            readers[t] = {}
        op.deps.discard(op)
        self.ops[eng].append(op)
        return op

    def fence(self, tokens, engines=ENGS):
        deps = set()
        for t in tokens:
            w = self.lastw.get(t)
            if w is not None:
                deps.add(w)
            r = self.readers.get(t)
            if r:
                deps.update(r.values())
            self.lastw[t] = None
            self.readers[t] = {}
        for e in engines:
            op = Op(e, None)
            op.fence = True
            op.deps = set(deps)
            self.ops[e].append(op)

    def emit(self, nc, stack):
        for e in ENGS:
            for op in self.ops[e]:
                for d in op.deps:
                    if d.chan is None:
                        if d.eng == "pe" and e == "pe" and not op.fence:
                            continue
                        d.sig = True
        sems = {}
        for e in ENGS:
            c = 0
            for op in self.ops[e]:
                if op.chan is None and op.sig:
                    c += 1
                    op.sigidx = c
            n = (c + EPOCH - 1) // EPOCH
            sems[e] = [stack.enter_context(nc.semaphore(f"s_{e}_{i}")) for i in range(n)]
        csems = {}
        for i, ch in enumerate(self.chan_cnt):
            csems[ch] = stack.enter_context(nc.semaphore(f"c_{i}"))
        block = stack.enter_context(nc.Block())
        handles = {"pe": block.tensor, "act": block.scalar, "dve": block.vector,
                   "pool": block.gpsimd, "sp": block.sync}

        def make_prog(e):
            ops = self.ops[e]

            def prog(h):
                seen = {}
                for op in ops:
                    need = {}
                    for d in op.deps:
                        if d.chan is not None:
                            kk = ("c", d.chan)
                            v = d.chan_idx
                        else:
                            if d.eng == "pe" and e == "pe" and not op.fence:
                                continue
                            kk = d.eng
                            v = d.sigidx
                        if v > need.get(kk, 0):
                            need[kk] = v
                    for kk, v in need.items():
                        if seen.get(kk, 0) >= v:
                            continue
                        seen[kk] = v
                        if isinstance(kk, tuple):
                            h.wait_ge(csems[kk[1]], 16 * v)
                        else:
                            ep, val = divmod(v - 1, EPOCH)
                            h.wait_ge(sems[kk][ep], val + 1)
                    if op.fence:
                        continue
                    ins = op.fn(h)
                    if op.chan is not None:
                        ins.then_inc(csems[op.chan], 16)
                    elif op.sig:
                        ins.then_inc(sems[e][(op.sigidx - 1) // EPOCH], 1)
                for ch, cnt in self.chan_cnt.items():
                    if self.chan_last[ch].eng == e:
                        h.wait_ge(csems[ch], 16 * cnt)
            return prog

        for e in ENGS:
            if self.ops[e]:
                handles[e](make_prog(e))


DEPTH = 2
NH = 6
HD = 64
AW = 384
MW = 256
INW = 3200
MIXW = 1024
ALPHA = (2 * DEPTH) ** 0.25
LN_EPS = 1e-5
RMS_EPS = 1e-6
SECTIONS = [("qa", 0, 384), ("ka", 384, 384), ("va", 768, 384), ("ub", 1152, 256), ("vb", 1408, 256),
            ("qc", 1664, 384), ("fc", 2048, 384), ("ic", 2432, 384), ("gc", 2816, 384)]
OQ = 256
NCH = 8


class Cfg:
    def __init__(self, D=1024, DFF=4096, NSEQ=4, L=2048, LS=16, PAST=1024, HP=512, UT=4):
        self.D, self.DFF, self.NSEQ, self.L, self.LS, self.PAST, self.HP, self.UT = D, DFF, NSEQ, L, LS, PAST, HP, UT
        self.KC = D // 128
        self.NT = L // 128
        self.NP = DFF // HP
        self.HC = HP // 128
        self.NOQ = D // OQ
        self.LK = max(L, PAST + 128)
        self.KT = self.LK // 128


def make_consts():
    c = np.zeros((128, 1312), np.float32)
    s = np.arange(128)[:, None]
    t = np.arange(128)[None, :]
    c[:, 0:128] = (s == t)
    c[:, 128:256] = -(s >= t).astype(np.float32)
    c[:, 256:384] = -1.0
    c[:, 384:512] = (s < t)
    blk_s, blk_t = s // 64, t // 64
    same = (blk_s == blk_t)
    c[:, 512:640] = same & (s <= t)
    mid_t = blk_t * 64 + 31
    c[:, 640:768] = same * ((s <= t).astype(np.float32) - (s <= mid_t).astype(np.float32))
    m16 = np.zeros((128, 128), np.float32)
    m16[:16, :16] = (s[:16] <= t[:, :16])
    c[:, 768:896] = m16
    d16 = np.zeros((128, 128), np.float32)
    d16[:16, :16] = (s[:16] <= t[:, :16]).astype(np.float32) - (s[:16] <= 7).astype(np.float32)
    c[:, 896:1024] = d16
    c[:, 1024:1152] = (t <= s)
    sel = np.zeros((128, 8), np.float32)
    for b in range(2):
        inb = (s[:, 0] // 64 == b)
        mid = 64 * b + 31
        sel[:, 3 * b + 0] = inb & (s[:, 0] <= mid)
        sel[:, 3 * b + 1] = inb
        sel[:, 3 * b + 2] = inb & (s[:, 0] > mid)
    c[:, 1152:1160] = sel
    sel16 = np.zeros((128, 8), np.float32)
    sel16[:16, 0] = (s[:16, 0] <= 7)
    sel16[:16, 1] = 1.0
    sel16[:16, 2] = (s[:16, 0] > 7)
    c[:, 1160:1168] = sel16
    return c


def build(cfg):
    D, DFF, KC, NT, NSEQ, L, LS, PAST, HP, HC, NP_, NOQ = (cfg.D, cfg.DFF, cfg.KC, cfg.NT, cfg.NSEQ, cfg.L, cfg.LS,
                                                          cfg.PAST, cfg.HP, cfg.HC, cfg.NP, cfg.NOQ)
    KT, LK, UT = cfg.KT, cfg.LK, cfg.UT
    NU = UT * 128
    nc = bass.Bass("TRN2", target_bir_lowering=False)
    S = Sched()

    def din(name, shape):
        return nc.dram_tensor(name, list(shape), F32, kind="ExternalInput").ap()

    def dout(name, shape):
        return nc.dram_tensor(name, list(shape), F32, kind="ExternalOutput").ap()

    def dint(name, shape):
        return nc.dram_tensor(name, list(shape), BF, kind="Internal").ap()

    xp = din("xp", [NSEQ, L, D])
    xs = din("xs", [LS, D])
    ck = din("ck", [DEPTH, PAST, AW])
    cv = din("cv", [DEPTH, PAST, AW])
    sh = din("sh", [DEPTH, NH, HD, HD])
    w_in = din("w_in", [DEPTH, D, INW])
    w_out = din("w_out", [DEPTH, MIXW, D])
    w_ff1 = din("w_ff1", [DEPTH, D, DFF])
    w_ff2 = din("w_ff2", [DEPTH, DFF, D])
    mlp_ln_g = din("mlp_ln_g", [DEPTH, MW])
    mlp_ln_b = din("mlp_ln_b", [DEPTH, MW])
    mlp_ws = din("mlp_ws", [DEPTH, 4, 128, 128])
    mlp_bs = din("mlp_bs", [DEPTH, 4, 128])
    hg_lb = din("hg_lb_logits", [DEPTH, AW])
    hg_nw = din("hg_norm_w", [DEPTH, AW])
    ln_g = [din("ln1_g", [DEPTH, D]), din("ln2_g", [DEPTH, D])]
    ln_b = [din("ln1_b", [DEPTH, D]), din("ln2_b", [DEPTH, D])]
    cst = din("consts", [128, 1312])

    yp = dout("yp", [NSEQ, L, D])
    ys = dout("ys", [LS, D])
    kp = dout("kp", [DEPTH, NSEQ, L, AW])
    vp = dout("vp", [DEPTH, NSEQ, L, AW])
    hp = dout("hp", [DEPTH, NSEQ, NH, HD, HD])
    ks = dout("ks", [DEPTH, LS, AW])
    vs = dout("vs", [DEPTH, LS, AW])
    hs = dout("hs", [DEPTH, NH, HD, HD])
    mvs = dout("mvs", [DEPTH, LS, MW])

    wb_in = [[dint(f"wbin{l}_{i}", [128, KC, n]) for i, (_, _, n) in enumerate(SECTIONS)] for l in range(DEPTH)]
    wb_out = [[dint(f"wbout{l}_{q}", [128, NCH, OQ]) for q in range(NOQ)] for l in range(DEPTH)]
    wb_f1 = [[dint(f"wbf1{l}_{j}", [128, KC, HP]) for j in range(NP_)] for l in range(DEPTH)]
    wb_f2 = [[dint(f"wbf2{l}_{j}", [128, HC, D]) for j in range(NP_)] for l in range(DEPTH)]

    st = contextlib.ExitStack()
    with st:
        def sb(name, shape, dt):
            return st.enter_context(nc.sbuf_tensor(name, list(shape), dt))

        xres = sb("xres", [128, NT, D], F32)
        lnrow = sb("lnrow", [128, 2, D], F32)
        mlprow = sb("mlprow", [128, 2, MW], F32)
        hgrow = sb("hgrow", [128, DEPTH, 2, AW], F32)
        nwrow = sb("nwrow", [128, AW], F32)
        Sst = sb("Sst", [64, NH, HD], F32)
        cf = sb("cf", [128, 400], F32)
        cb = sb("cb", [128, 640], BF)
        mh16b = sb("mh16b", [128, 128], BF)
        WTl = sb("WTl", [128, DEPTH, 4, 128], BF)
        bsT = sb("bsT", [128, DEPTH, 4], F32)
        xb2 = sb("xb", [128, 2, D], BF)
        xbc = [0]
        small = sb("small", [128, 64], F32)
        lnsm = sb("lnsm", [128, 4, 40], F32)

        identb = cb[:, 0:128]
        ntrib = cb[:, 128:256]
        nonesb = cb[:, 256:384]
        maskSb = cb[:, 384:512]
        maskHb = cb[:, 512:640]
        Dm = cf[:, 0:128]
        Dm16 = cf[:, 128:256]
        trilf = cf[:, 256:384]
        SelC = cf[:, 384:392]
        SelC16 = cf[:, 392:400]

        off = [0]
        arena_views = []

        def carve(nbytes):
            o = off[0]
            off[0] += (nbytes + 63) // 64 * 64
            return o

        lay = {}
        for name, nb in [("kT", 3 * LK * 2), ("vfull", KT * AW * 2), ("xTmix", NCH * NU * 2),
                         ("qT", 3 * NU * 2), ("u", UT * MW * 2), ("vn", UT * MW * 2), ("logf", UT * AW * 4),
                         ("qc", UT * AW * 2), ("vb", UT * AW * 2), ("gc", UT * AW * 2), ("kk", UT * AW * 2),
                         ("ring", 3 * 6144), ("stage", 4 * AW * 4), ("tmp", 14336 + 18688)]:
            lay[name] = (carve(nb), nb)
        mixer_bytes = off[0]
        off[0] = 0
        flay = {}
        TG = min(512, NT * 128)
        for name, nb in [("x1T", KC * NT * 128 * 2), ("hT", 2 * HC * NT * 128 * 2), ("w1", 2 * KC * HP * 2),
                         ("w2", 2 * HC * D * 2), ("r", 2 * 512 * 2)]:
            flay[name] = (carve(nb), nb)
        ffn_bytes = off[0]
        ARENA = max(mixer_bytes, ffn_bytes)
        arena = sb("arena", [128, ARENA // 2], BF)

        def view(layd, name, dt, pat=None, **kw):
            o, nb = layd[name]
            v = arena[:, o // 2:(o + nb) // 2]
            if dt == F32:
                v = v.bitcast(F32)
            if pat:
                v = v.rearrange(pat, **kw)
            return v

        kT = view(lay, "kT", BF, "p (m k) -> p m k", m=3)
        vfull = view(lay, "vfull", BF, "p (t c) -> p t c", c=AW)
        mixT = view(lay, "xTmix", BF)[:, 0:NCH * NU].rearrange("p (k n) -> p k n", k=NCH)
        qT = view(lay, "qT", BF, "p (m n) -> p m n", m=3)
        ub = view(lay, "u", BF, "p (t c) -> p t c", c=MW)
        vn = view(lay, "vn", BF, "p (t c) -> p t c", c=MW)
        logf = view(lay, "logf", F32, "p (t c) -> p t c", c=AW)
        qcb = view(lay, "qc", BF, "p (t c) -> p t c", c=AW)
        vbb = view(lay, "vb", BF, "p (t c) -> p t c", c=AW)
        gcb = view(lay, "gc", BF, "p (t c) -> p t c", c=AW)
        kkb = view(lay, "kk", BF, "p (t c) -> p t c", c=AW)
        ring = view(lay, "ring", BF, "p (s n) -> p s n", s=3)
        stage = view(lay, "stage", F32, "p (s c) -> p s c", c=AW)
        tmpo = lay["tmp"][0]

        def tview(o, nbytes, dt):
            v = arena[:, (tmpo + o) // 2:(tmpo + o + nbytes) // 2]
            return v.bitcast(F32) if dt == F32 else v

        att = []
        for c in range(2):
            b0 = c * 7168
            att.append(dict(
                ez=[tview(b0, 1024, BF), tview(b0 + 1024, 1024, BF)],
                sp=[tview(b0 + 2048, 1024, BF), tview(b0 + 3072, 1024, BF)],
                w=[tview(b0 + 4096, 1024, BF), tview(b0 + 5120, 1024, BF)],
                rs=tview(b0 + 6144, 1024, BF)))
        assert KC * NU * 2 <= 14336
        xT = tview(0, KC * NU * 2, BF).rearrange("p (k n) -> p k n", k=KC)
        ho = [14336]

        def hv(nbytes, dt):
            o = ho[0]
            ho[0] += nbytes
            return tview(o, nbytes, dt)
        eb = hv(1536, F32)
        enb = hv(1536, F32)
        tq = hv(1536, F32)
        qhat = hv(768, BF)
        khat = hv(768, BF)
        qhT = hv(1536, BF).rearrange("p (h t) -> p h t", h=NH)
        QAB = hv(3072, BF).rearrange("p (h b t) -> p h b t", h=NH, b=2)
        khT = hv(1536, BF).rearrange("p (h t) -> p h t", h=NH)
        sc = hv(1536, BF).rearrange("p (h t) -> p h t", h=NH)
        Stl = hv(1536, BF).rearrange("p (b h e) -> p b h e", b=2, h=NH)
        dtmp = hv(1536, F32)
        sq = eb
        sig = enb
        oc32 = tq
        ocb = hv(768, BF)
        vn32 = eb
        obb = hv(512, BF)
        esel = hv(256, F32)
        assert ho[0] <= 14336 + 18688, ho[0]
        ATT_T = ["ez00", "ez01", "ez10", "ez11", "sp00", "sp01", "sp10", "sp11", "w00", "w01", "w10", "w11", "rs0", "rs1"]
        HG_T = ["eb", "enb", "tq", "qhat", "khat", "qhT", "QAB", "khT", "sc", "Stl", "dtmp",
                "ocb", "obb", "esel"]
        XT_T = [("xT", i) for i in range(UT)]
        MIX_T = [("mixA", h) for h in range(NH)] + [("mixB", c, i) for c in range(5) for i in range(UT)]
        S.alias(XT_T, ATT_T)
        S.alias([("ps", 4)], ["psO0", "psO1"])

        x1T = view(flay, "x1T", BF, "p (k n) -> p k n", k=KC)
        hT = view(flay, "hT", BF, "p (s c n) -> p s c n", s=2, c=HC)
        w1r = view(flay, "w1", BF, "p (s k n) -> p s k n", s=2, k=KC)
        w2r = view(flay, "w2", BF, "p (s c n) -> p s c n", s=2, c=HC)
        rr = view(flay, "r", BF, "p (s n) -> p s n", s=2)

        MIXER_TOKS = (["kT%d_%d" % (m, u) for m in range(3) for u in range(KT)] + [("v", t) for t in range(KT)]
                      + XT_T + MIX_T + [("qT", m) for m in range(3)]
                      + [(nm, i) for nm in ("u", "vn", "logf", "qc", "vb", "gc", "kk") for i in range(UT)]
                      + [("ring", s_) for s_ in range(3)] + [("ringb", s_) for s_ in range(3)] + [("stage", s_) for s_ in range(4)] + ATT_T + HG_T)
        FFN_TOKS = ([("x1T", t) for t in range(NT)] + [("hT", s_, c) for s_ in range(2) for c in range(HC)]
                    + [("w1", s_) for s_ in range(2)] + [("w2", s_) for s_ in range(2)] + [("r", s_) for s_ in range(2)])

        pbanks = [st.enter_context(nc.psum_tensor(f"pb{i}", [128, 512], F32)) for i in range(8)]
        pcount = [0]

        def bank(i=None):
            if i is None:
                i = pcount[0] % 8
                pcount[0] += 1
            return pbanks[i], ("ps", i)

        def mm(out, lhsT, rhs, start, stop, reads, writes, sgc=False):
            if sgc:
                S.add("pe", lambda h: h.matmul(out, lhsT=lhsT, rhs=rhs, start=start, stop=stop, skip_group_check=True),
                      reads, writes)
            else:
                S.add("pe", lambda h: h.matmul(out, lhsT=lhsT, rhs=rhs, start=start, stop=stop), reads, writes)

        def tr(out, in_, ident, reads, writes):
            S.add("pe", lambda h: h.transpose(out=out, in_=in_, identity=ident), reads, writes)

        def act(out, in_, func, reads, writes, scale=None, bias=None):
            kw = {}
            if scale is not None:
                kw["scale"] = scale
            if bias is not None:
                kw["bias"] = bias
            S.add("act", lambda h: h.activation(out=out, in_=in_, func=func, **kw), reads, writes)

        def tt(eng, out, in0, in1, op, reads, writes):
            S.add(eng, lambda h: h.tensor_tensor(out=out, in0=in0, in1=in1, op=op), reads, writes)

        def tsc(eng, out, in0, s1, s2, op0, op1, reads, writes):
            if op1 is None:
                S.add(eng, lambda h: h.tensor_scalar(out=out, in0=in0, scalar1=s1, scalar2=None, op0=op0), reads, writes)
            else:
                S.add(eng, lambda h: h.tensor_scalar(out=out, in0=in0, scalar1=s1, scalar2=s2, op0=op0, op1=op1),
                      reads, writes)

        def stt(out, in0, scalar, in1, op0, op1, reads, writes):
            S.add("dve", lambda h: h.scalar_tensor_tensor(out=out, in0=in0, scalar=scalar, in1=in1, op0=op0, op1=op1),
                  reads, writes)

        def cp(eng, out, in_, reads, writes):
            if eng == "act":
                S.add("act", lambda h: h.activation(out=out, in_=in_, func=AF.Copy), reads, writes)
            else:
                S.add(eng, lambda h: h.tensor_copy(out=out, in_=in_), reads, writes)

        def recip(out, in_, reads, writes):
            S.add("dve", lambda h: h.reciprocal(out=out, in_=in_), reads, writes)

        def dma(eng, out, in_, chan, reads, writes, **kw):
            S.add(eng, lambda h: h.dma_start(out=out, in_=in_, **kw), reads, writes, chan=chan)

        dma("sp", cf[:, 0:128], cst[:, 640:768], "ld_c", [], ["cf"])
        dma("sp", cf[:, 128:256], cst[:, 896:1024], "ld_c", [], ["cf"])
        dma("sp", cf[:, 256:384], cst[:, 1024:1152], "ld_c", [], ["cf"])
        dma("sp", cf[:, 384:400], cst[:, 1152:1168], "ld_c", [], ["cf"])
        dma("pool", cb[:], cst[:, 0:640], "cv0", [], ["cb"])
        dma("pool", mh16b[:], cst[:, 768:896], "cv1", [], ["mh16b"])
        for l in range(DEPTH):
            dma("sp", bsT[:, l, :], mlp_bs[l].rearrange("g t -> t g"), "ld_c", [], ["bsT"],
                allow_slow_non_contiguous=True)
        L0 = eb[:, 0:AW]
        L1 = enb[:, 0:AW]
        T0 = tq[:, 0:AW]
        T1 = dtmp[:, 0:AW]
        dma("sp", L0, hg_lb[0:1, :].broadcast_to([128, AW]), "ld_c", [], ["eb"])
        dma("sp", L1, hg_lb[1:2, :].broadcast_to([128, AW]), "ld_c", [], ["enb"])
        tt("dve", T0, L0, L1, ALU.max, ["eb", "enb"], ["tq"])
        tt("dve", L0, L0, T0, ALU.subtract, ["eb", "tq"], ["eb"])
        tt("dve", L1, L1, T0, ALU.subtract, ["enb", "tq"], ["enb"])
        act(L0, L0, AF.Exp, ["eb"], ["eb"])
        act(L1, L1, AF.Exp, ["enb"], ["enb"])
        tt("dve", T0, L0, L1, ALU.add, ["eb", "enb"], ["tq"])
        recip(T0, T0, ["tq"], ["tq"])
        tt("dve", L0, L0, T0, ALU.mult, ["eb", "tq"], ["eb"])
        tt("dve", L1, L1, T0, ALU.mult, ["enb", "tq"], ["enb"])
        tt("dve", T1, L0, L1, ALU.add, ["eb", "enb"], ["dtmp"])
        tt("dve", hgrow[:, 0, 0, :], L0, L0, ALU.subtract, ["eb"], ["hgrow"])
        tt("dve", hgrow[:, 1, 0, :], T1, L0, ALU.subtract, ["dtmp", "eb"], ["hgrow"])
        for l in range(DEPTH):
            tsc("dve", hgrow[:, l, 1, :], hgrow[:, l, 0, :], -1.0, 1.0, ALU.mult, ALU.add, ["hgrow"], ["hgrow"])
        for l in range(DEPTH):
            wsrc = tview(14336, 2048, F32).rearrange("p (g s) -> p g s", g=4)
            dma("sp", wsrc, mlp_ws[l].rearrange("g t s -> t g s"), "ld_c", [], ["eb", "enb"])
            wsm = tview(14336 + 4608, 1024, BF).rearrange("p (g s) -> p g s", g=4)
            tt("dve", wsm, wsrc, trilf.unsqueeze(1).broadcast_to([128, 4, 128]), ALU.mult,
               ["eb", "enb", "cf"], ["qhat", "khat"])
            pb_, ptok = bank()
            pbb = pb_[:].bitcast(BF)
            for g in range(4):
                tr(pbb[:, g * 128:(g + 1) * 128], wsm[:, g, :], identb, ["qhat", "khat", "cb"], [ptok])
            cp("dve", WTl[:, l, :, :], pbb[:, 0:512].rearrange("p (g t) -> p g t", g=4), [ptok], ["WTl"])

        cvi = [0]

        def conv(out, in_, tok):
            dma("pool", out, in_, "cv%d" % (cvi[0] % 4), [], [tok])
            cvi[0] += 1

        def conv_layer_in(l):
            for i, (_, c0, n) in enumerate(SECTIONS):
                conv(wb_in[l][i], w_in[l][:, c0:c0 + n].rearrange("(k p) n -> p k n", p=128), ("wbin", l, i))
            for q in range(NOQ):
                conv(wb_out[l][q], w_out[l][:, q * OQ:(q + 1) * OQ].rearrange("(c p) n -> p c n", p=128), ("wbout", l, q))

        def conv_layer_ff(l):
            for j in range(NP_):
                conv(wb_f1[l][j], w_ff1[l][:, j * HP:(j + 1) * HP].rearrange("(k p) n -> p k n", p=128), ("wbf1", l, j))
                conv(wb_f2[l][j], w_ff2[l][j * HP:(j + 1) * HP, :].rearrange("(c p) n -> p c n", p=128), ("wbf2", l, j))

        conv_layer_in(0)
        conv_layer_ff(0)
        conv_layer_in(1)
        conv_layer_ff(1)

        NCK = (D + 511) // 512
        CW = D // NCK

        def layernorm_tile(P, t):
            xt_ = ("xres", t)
            sl_ = t % 4
            sm = lnsm[:, sl_, :]
            k1, k2 = ("lns", sl_), ("lns2", sl_)
            for i in range(NCK):
                S.add("dve", lambda h, i=i: h.bn_stats(out=sm[:P, 6 * i:6 * i + 6], in_=xres[:P, t, i * CW:(i + 1) * CW]),
                      [xt_], [k1])
            S.add("dve", lambda h: h.bn_aggr(out=sm[:P, 32:34], in_=sm[:P, 0:6 * NCK]), [k1], [k1])
            act(sm[:P, 34:35], sm[:P, 33:34], AF.Ln, [k1, "epsc"], [k1], bias=epsln[:P, 0:1])
            act(sm[:P, 35:36], sm[:P, 34:35], AF.Exp, [k1], [k1], scale=-0.5)
            tsc("dve", sm[:P, 36:37], sm[:P, 32:33], sm[:P, 35:36], -1.0, ALU.mult, ALU.mult, [k1], [k2])
            act(xres[:P, t, :], xres[:P, t, :], AF.Identity, [xt_, k1, k2], [xt_],
                scale=sm[:P, 35:36], bias=sm[:P, 36:37])
            tt("dve", xres[:P, t, :], xres[:P, t, :], lnrow[:P, 0, :], ALU.mult, [xt_, "lnrow"], [xt_])
            tt("pool", xres[:P, t, :], xres[:P, t, :], lnrow[:P, 1, :], ALU.add, [xt_, "lnrowb"], [xt_])

        epsc = sb("epsc", [128, 2], F32)
        epsln = epsc[:, 0:1]
        epsrms = epsc[:, 1:2]
        S.add("pool", lambda h: h.memset(epsc[:, 0:1], LN_EPS), [], ["epsc"])
        S.add("pool", lambda h: h.memset(epsc[:, 1:2], RMS_EPS), ["epsc"], ["epsc"])

        def make_xT(P, t, dst, dst_tok, col0):
            xt_ = ("xres", t)
            xs_ = xbc[0] % 2
            xbc[0] += 1
            xb = xb2[:, xs_, :]
            xbt = ("xb", xs_)
            cp("dve", xb[:P, :], xres[:P, t, :], [xt_], [xbt])
            S.add("act", lambda h: h.mul(out=xres[:P, t, :], in_=xres[:P, t, :], mul=ALPHA), [xt_], [xt_])
            pb_, ptok = bank()
            pbb = pb_[:].bitcast(BF)
            for kc in range(KC):
                tr(pbb[:, kc * P:(kc + 1) * P], xb[:P, kc * 128:(kc + 1) * 128], identb[:P, :P], [xbt, "cb"], [ptok])
            cp("dve", dst[:, :, col0:col0 + P], pbb[:, 0:KC * P].rearrange("p (k t) -> p k t", k=KC), [ptok], [dst_tok])

        def run_sequence(kind, si):
            prompt = kind == "p"
            P = 128 if prompt else LS
            ntile = NT if prompt else 1
            kt0 = 0 if prompt else PAST // 128
            nblk, BL = (2, 64) if prompt else (1, LS)
            Dm_ = Dm if prompt else Dm16
            Sel_ = SelC if prompt else SelC16
            mH_ = maskHb if prompt else mh16b
            units = [list(range(u, min(u + UT, ntile))) for u in range(0, ntile, UT)]
            x_src = xp[si] if prompt else xs
            y_dst = yp[si] if prompt else ys

            for tiles in units:
                t0 = tiles[0]
                if prompt:
                    dma("sp", xres[:, t0:t0 + len(tiles), :],
                        x_src[t0 * 128:(t0 + len(tiles)) * 128, :].rearrange("(t p) d -> p t d", p=128),
                        "ld_x", [], [("xres", t) for t in tiles])
                else:
                    dma("sp", xres[:P, 0, :], x_src, "ld_x", [], [("xres", 0)])

            for l in range(DEPTH):
                S.fence(FFN_TOKS)
                dma("sp", lnrow[:, 0, :], ln_g[0][l:l + 1, :].broadcast_to([128, D]), "ld_ln0", [], ["lnrow"])
                dma("sp", lnrow[:, 1, :], ln_b[0][l:l + 1, :].broadcast_to([128, D]), "ld_ln1", [], ["lnrowb"])
                dma("sp", mlprow[:, 0, :], mlp_ln_g[l:l + 1, :].broadcast_to([128, MW]), "ld_ln2", [], ["mlprow"])
                dma("sp", mlprow[:, 1, :], mlp_ln_b[l:l + 1, :].broadcast_to([128, MW]), "ld_ln3", [], ["mlprowb"])
                dma("sp", nwrow[:, :], hg_nw[l:l + 1, :].broadcast_to([128, AW]), "ld_ln4", [], ["nwrow"])
                if prompt:
                    S.add("pool", lambda h: h.memset(Sst[:], 0.0), [], ["Sst"])
                else:
                    dma("sp", Sst[:], sh[l].rearrange("h d e -> d h e"), "ld_s", [], ["Sst"])
                    npt = PAST // 128
                    dma("pool", vfull[:, 0:npt, :], cv[l].rearrange("(t p) c -> p t c", p=128), "ld_cv", [],
                        [("v", t) for t in range(npt)])
                    kc_tmp = tview(0, npt * AW * 2, BF).rearrange("p (t c) -> p t c", c=AW)
                    dma("pool", kc_tmp, ck[l].rearrange("(t p) c -> p t c", p=128), "ld_ck", [], ATT_T)
                    for t in range(npt):
                        pb_, ptok = bank()
                        pbb = pb_[:].bitcast(BF)
                        for m in range(3):
                            tr(pbb[:, m * 128:(m + 1) * 128], kc_tmp[:, t, m * 128:(m + 1) * 128], identb,
                               ATT_T + ["cb"], [ptok])
                        cp("dve", kT[:, :, t * 128:(t + 1) * 128], pbb[:, 0:384].rearrange("p (m k) -> p m k", m=3),
                           [ptok], ["kT%d_%d" % (m, t) for m in range(3)])

                ringcnt = [0]

                def load_block(blk):
                    slot = ringcnt[0] % 3
                    ringcnt[0] += 1
                    if blk[0] == "in":
                        i = blk[1]
                        n = SECTIONS[i][2]
                        dst = ring[:, slot, 0:KC * n].rearrange("p (k n) -> p k n", k=KC)
                        dma("sp", dst, wb_in[l][i], "ring%d" % slot, [("wbin", l, i)], [("ring", slot), ("ringb", slot)])
                    else:
                        i = blk[1]
                        dst = ring[:, slot, 0:NCH * OQ].rearrange("p (k n) -> p k n", k=NCH)
                        dma("sp", dst, wb_out[l][i], "ring%d" % slot, [("wbout", l, i)], [("ring", slot), ("ringb", slot)])
                    return slot

                def outproj_block(q, slot, tiles_o):
                    Wo = ring[:, slot, 0:NCH * OQ].rearrange("p (k n) -> p k n", k=NCH)
                    rtok = ("ring", slot)
                    for tl, t in enumerate(tiles_o):
                        col = tl * 128
                        pb_, ptok = bank()
                        for pr in range(3):
                            mm(pb_[:P, 0:OQ], mixT[:, pr, col:col + P], Wo[:, pr, :], pr == 0, False,
                               [("mixA", 2 * pr), ("mixA", 2 * pr + 1), rtok], [ptok])
                        for c in range(5):
                            mm(pb_[:P, 0:OQ], mixT[:, 3 + c, col:col + P], Wo[:, 3 + c, :], False, c == 4,
                               [("mixB", c, tl), ("ringb", slot)], [ptok])
                        tt("dve", xres[:P, t, q * OQ:(q + 1) * OQ], xres[:P, t, q * OQ:(q + 1) * OQ], pb_[:P, 0:OQ], ALU.add,
                           [("xres", t), ptok], [("xres", t)])

                def run_blocks(seq_, fns):
                    pcs = [i for i, bk in enumerate(seq_) if bk[0] in ("in", "out")]
                    slot_of = {}
                    nl = [0]

                    def ensure(upto_rank):
                        while nl[0] <= upto_rank and nl[0] < len(pcs):
                            slot_of[pcs[nl[0]]] = load_block(seq_[pcs[nl[0]]])
                            nl[0] += 1
                    rank = 0
                    for i, bk in enumerate(seq_):
                        if bk[0] in ("in", "out"):
                            ensure(rank + 2)
                            fns[bk[0]](bk, slot_of[i])
                            rank += 1
                        else:
                            fns[bk[0]](bk, None)

                pend = None
                for tiles in units:
                    nt_u = len(tiles)
                    N = (nt_u - 1) * 128 + P
                    kcol0 = (kt0 + tiles[0]) * 128
                    for tl, t in enumerate(tiles):
                        make_xT(P, t, xT, ("xT", tl), tl * 128)
                    xT_reads = [("xT", tl) for tl in range(nt_u)]

                    def inproj_block(pi, slot):
                        sname, c0_, n = SECTIONS[pi]
                        W = ring[:, slot, 0:KC * n].rearrange("p (k n) -> p k n", k=KC)
                        rtok = ("ring", slot)
                        rtok2 = ("ringb", slot)
                        if sname in ("qa", "ka"):
                            for m in range(3):
                                pb_, ptok = bank()
                                for kc in range(KC):
                                    mm(pb_[:, 0:N], W[:, kc, m * 128:(m + 1) * 128], xT[:, kc, 0:N], kc == 0, kc == KC - 1,
                                       xT_reads + [rtok, rtok2], [ptok])
                                if sname == "qa":
                                    act(qT[:, m, 0:N], pb_[:, 0:N], AF.Copy, [ptok], [("qT", m)], scale=0.125)
                                else:
                                    cp("dve", kT[:, m, kcol0:kcol0 + N], pb_[:, 0:N], [ptok],
                                       ["kT%d_%d" % (m, kt0 + t) for t in tiles])
                        if sname != "qa":
                            for tl, t in enumerate(tiles):
                                pb_, ptok = bank()
                                for kc in range(KC):
                                    mm(pb_[:P, 0:n], xT[:, kc, tl * 128:tl * 128 + P], W[:, kc, 0:n], kc == 0, kc == KC - 1,
                                       [("xT", tl), rtok, rtok2], [ptok])
                                ps_ = pb_[:P, 0:n]
                                tok_lo, tok_hi = t * 128, t * 128 + P
                                if sname == "ka":
                                    ss = (t % 2)
                                    cp("dve", stage[:P, ss, :], ps_, [ptok], [("stage", ss)])
                                    dst = kp[l, si, tok_lo:tok_hi, :] if prompt else ks[l]
                                    dma("sp", dst, stage[:P, ss, :], "st_k%d" % ss, [("stage", ss)], [])
                                elif sname == "va":
                                    ss = 2 + (t % 2)
                                    cp("act", stage[:P, ss, :], ps_, [ptok], [("stage", ss)])
                                    dst = vp[l, si, tok_lo:tok_hi, :] if prompt else vs[l]
                                    dma("sp", dst, stage[:P, ss, :], "st_v%d" % (ss - 2), [("stage", ss)], [])
                                    cp("pool", vfull[:P, kt0 + t, :], stage[:P, ss, :], [("stage", ss)], [("v", kt0 + t)])
                                elif sname == "ub":
                                    cp("act", ub[:P, tl, :], ps_, [ptok], [("u", tl)])
                                elif sname == "vb":
                                    S.add("dve", lambda h, ps_=ps_: h.bn_stats(out=small[:P, 40:46], in_=ps_), [ptok], ["smallg"])
                                    S.add("dve", lambda h: h.bn_aggr(out=small[:P, 46:48], in_=small[:P, 40:46]),
                                          ["smallg"], ["smallg"])
                                    act(small[:P, 48:49], small[:P, 47:48], AF.Ln, ["smallg", "epsc"], ["smallg"],
                                        bias=epsln[:P, 0:1])
                                    act(small[:P, 49:50], small[:P, 48:49], AF.Exp, ["smallg"], ["smallg"], scale=-0.5)
                                    tsc("dve", vn32[:P, 0:MW], ps_, small[:P, 46:47], small[:P, 49:50], ALU.subtract, ALU.mult,
                                        [ptok, "smallg"], ["eb"])
                                    tt("pool", vn32[:P, 0:MW], vn32[:P, 0:MW], mlprow[:P, 0, :], ALU.mult,
                                       ["eb", "mlprow"], ["eb"])
                                    tt("pool", vn32[:P, 0:MW], vn32[:P, 0:MW], mlprow[:P, 1, :], ALU.add,
                                       ["eb", "mlprowb"], ["eb"])
                                    cp("act", vn[:P, tl, :], vn32[:P, 0:MW], ["eb"], [("vn", tl)])
                                    if not prompt:
                                        dma("sp", mvs[l], vn32[:P, 0:MW], "st_m", ["eb"], [])
                                elif sname == "qc":
                                    cp("act", qcb[:P, tl, :], ps_, [ptok], [("qc", tl)])
                                elif sname == "fc":
                                    tA, tB, tC = eb[:P, 0:AW], enb[:P, 0:AW], tq[:P, 0:AW]
                                    act(tA, ps_, AF.Exp, [ptok], ["eb"], scale=-1.0)
                                    act(tB, tA, AF.Ln, ["eb"], ["enb"], bias=1.0)
                                    act(tB, tB, AF.Exp, ["enb"], ["enb"], scale=-1.0)
                                    tt("dve", tC, tA, tB, ALU.mult, ["eb", "enb"], ["tq"])
                                    tt("pool", kkb[:P, tl, :], tC, hgrow[:P, l, 1, :], ALU.mult, ["tq", "hgrow"], [("kk", tl)])
                                    tt("dve", tB, tB, hgrow[:P, l, 1, :], ALU.mult, ["enb", "hgrow"], ["enb"])
                                    tt("dve", tB, tB, hgrow[:P, l, 0, :], ALU.add, ["enb", "hgrow"], ["enb"])
                                    act(logf[:P, tl, :], tB, AF.Ln, ["enb"], [("logf", tl)])
                                elif sname == "ic":
                                    cp("dve", vbb[:P, tl, :], ps_, [ptok], [("vb", tl)])
                                elif sname == "gc":
                                    cp("act", gcb[:P, tl, :], ps_, [ptok], [("gc", tl)])

                    seq_ = []
                    oq = 0
                    for pi in range(len(SECTIONS)):
                        if pend is not None and pi % 2 == 0 and oq < NOQ:
                            seq_.append(("out", oq))
                            oq += 1
                        seq_.append(("in", pi))
                    if pend is not None:
                        while oq < NOQ:
                            seq_.append(("out", oq))
                            oq += 1
                        last_out = max(i for i, bk in enumerate(seq_) if bk[0] == "out")
                        lns = [("ln", t) for t in pend]
                        rest = seq_[last_out + 1:]
                        half = len(lns) // 2
                        seq_ = seq_[:last_out + 1] + lns[:half] + rest + lns[half:]
                    pend_ = pend
                    run_blocks(seq_, {"in": lambda bk, sl: inproj_block(bk[1], sl),
                                      "out": lambda bk, sl: outproj_block(bk[1], sl, pend_),
                                      "ln": lambda bk, sl: layernorm_tile(P, bk[1])})

                    blocks = []
                    for jb in range(kt0 + tiles[-1], -1, -1):
                        if jb >= kt0 + tiles[0]:
                            tl = jb - (kt0 + tiles[0])
                            blocks.append((jb, P, True, tl * 128))
                        else:
                            blocks.append((jb, 128, False, 0))
                    nst = len(blocks)
                    A_th = []
                    H_th = []

                    def s1(c, hh, k, par):
                        jb, Pk, diag, c0 = blocks[k]
                        pr, j = hh // 2, hh % 2
                        n = N - c0
                        zb, ztok = bank(c)
                        mm(zb[:Pk, 0:n], kT[64 * j:64 * j + 64, pr, jb * 128:jb * 128 + Pk], qT[64 * j:64 * j + 64, pr, c0:N],
                           True, True, ["kT%d_%d" % (pr, jb), ("qT", pr)], [ztok])

                    def s2(c, hh, k, par):
                        jb, Pk, diag, c0 = blocks[k]
                        n = N - c0
                        zb, ztok = bank(c)
                        A = att[c]
                        spb = A["sp"][par]
                        sptok = "sp%d%d" % (c, par)
                        ezb = A["ez"][par]
                        eztok = "ez%d%d" % (c, par)
                        act(ezb[:Pk, 0:n], zb[:Pk, 0:n], AF.Exp, [ztok], [eztok])
                        act(spb[:Pk, 0:n], ezb[:Pk, 0:n], AF.Ln, [eztok], [sptok], bias=1.0)
                        if diag:
                            tt("pool", spb[:Pk, 0:P], spb[:Pk, 0:P], maskSb[:Pk, :P], ALU.mult, [sptok, "cb"], [sptok])

                    def s3(c, hh, k, par):
                        jb, Pk, diag, c0 = blocks[k]
                        pr, j = hh // 2, hh % 2
                        n = N - c0
                        A = att[c]
                        spb = A["sp"][par]
                        sptok = "sp%d%d" % (c, par)
                        cbk, ctok = bank(2 + c)
                        if k == 0:
                            S.add("pool", lambda h, c=c: h.memset(att[c]["rs"][:, 0:N], 0.0), [], ["rs%d" % c])
                        mm(cbk[:Pk, 0:n], ntrib[:Pk, :Pk], spb[:Pk, 0:n], True, k == 0, [sptok, "cb"], [ctok])
                        if k > 0:
                            mm(cbk[:Pk, 0:n], nonesb[:, :Pk], A["rs"][:, c0:N], False, True, ["rs%d" % c, "cb"], [ctok])
                        if k < nst - 1:
                            tt("dve", A["rs"][:Pk, c0:N], A["rs"][:Pk, c0:N], spb[:Pk, 0:n], ALU.add,
                               ["rs%d" % c, sptok], ["rs%d" % c])

                    def s4(c, hh, k, par):
                        jb, Pk, diag, c0 = blocks[k]
                        n = N - c0
                        A = att[c]
                        wb_ = A["w"][par]
                        wtok = "w%d%d" % (c, par)
                        cbk, ctok = bank(2 + c)
                        act(wb_[:Pk, 0:n], cbk[:Pk, 0:n], AF.Exp, [ctok], [wtok])
                        tt("dve", wb_[:Pk, 0:n], wb_[:Pk, 0:n], A["ez"][par][:Pk, 0:n], ALU.mult,
                           [wtok, "ez%d%d" % (c, par)], [wtok])
                        if diag:
                            tt("pool", wb_[:Pk, 0:P], wb_[:Pk, 0:P], maskSb[:Pk, :P], ALU.mult, [wtok, "cb"], [wtok])

                    def s5(c, hh, k, par):
                        jb, Pk, diag, c0 = blocks[k]
                        n = N - c0
                        A = att[c]
                        wb_ = A["w"][par]
                        wtok = "w%d%d" % (c, par)
                        ob_, otok = pbanks[4], "psO%d" % c
                        mm(ob_[64 * c:64 * c + 64, c0:N], vfull[:Pk, jb, hh * 64:(hh + 1) * 64], wb_[:Pk, 0:n], k == 0, k == nst - 1,
                           [("v", jb), wtok], [otok], sgc=True)
                        if k == nst - 1:
                            cp("dve", mixT[64 * c:64 * c + 64, hh // 2, 0:N], ob_[64 * c:64 * c + 64, 0:N], [otok], [("mixA", hh)])

                    items = [[(hh, k) for hh in range(c, NH, 2) for k in range(nst)] for c in range(2)]
                    nit = len(items[0])
                    for i_ in range(nit + 2):
                        for fn_, ii in ((s1, i_), (s2, i_), (s3, i_ - 1), (s4, i_ - 1), (s5, i_ - 2)):
                            if 0 <= ii < nit:
                                A_th.append(lambda fn_=fn_, ii=ii: [fn_(c, items[c][ii][0], items[c][ii][1], ii % 2) for c in range(2)])

                    def hg_tile(tl, t):
                        col = tl * 128
                        X = {}
                        es3 = esel[0:64, 0:48].rearrange("p (h c) -> p h c", h=NH)
                        bc = lambda col_: es3[:, :, col_:col_ + 1].broadcast_to([64, NH, HD])
                        d3 = lambda ap_: ap_.rearrange("p (h e) -> p h e", h=NH)

                        def t1():
                            pg, X["pg"] = bank(5)
                            for g in range(4):
                                mm(pg[:P, g * 64:(g + 1) * 64], WTl[:P, l, g, :P], vn[:P, tl, g * 64:(g + 1) * 64], True, True,
                                   ["WTl", ("vn", tl)], [X["pg"]])

                        def t3():
                            pg = pbanks[5]
                            for g in range(4):
                                stt(obb[:P, g * 64:(g + 1) * 64], pg[:P, g * 64:(g + 1) * 64], bsT[:P, l, g:g + 1],
                                    ub[:P, tl, g * 64:(g + 1) * 64], ALU.add, ALU.mult, [X["pg"], "bsT", ("u", tl)], ["obb"])

                        def t2():
                            pbr, tok = bank(5)
                            mm(pbr[:P, 0:AW], Dm_[:P, :P], logf[:P, tl, :], True, True, ["cf", ("logf", tl)], [tok])
                            for hh in range(NH):
                                mm(pbr[0:64, AW + hh * 8:AW + hh * 8 + 8], logf[:P, tl, hh * 64:(hh + 1) * 64], Sel_[:P, 0:8], True, True,
                                   ["cf", ("logf", tl)], [tok])

                        def t4():
                            pbr, tok = bank(5)
                            act(eb[:P, 0:AW], pbr[:P, 0:AW], AF.Exp, [tok], ["eb"])
                            act(enb[:P, 0:AW], pbr[:P, 0:AW], AF.Exp, [tok], ["enb"], scale=-1.0)
                            act(esel[0:64, 0:48], pbr[0:64, AW:AW + 48], AF.Exp, [tok], ["esel"])

                        def t6():
                            pt, tok = bank(5)
                            ptb = pt[:].bitcast(BF)
                            for c in range(2):
                                tr(ptb[:, c * P:(c + 1) * P], obb[:P, c * 128:(c + 1) * 128], identb[:P, :P], ["obb", "cb"], [tok])

                        def t8():
                            pt, tok = bank(5)
                            ptb = pt[:].bitcast(BF)
                            cp("dve", mixT[:, 3:5, col:col + P], ptb[:, 0:2 * P].rearrange("p (c t) -> p c t", c=2), [tok],
                               [("mixB", 0, tl), ("mixB", 1, tl)])

                        def t5():
                            act(tq[:P, 0:AW], qcb[:P, tl, :], AF.Exp, [("qc", tl)], ["tq"], scale=-1.0)
                            act(tq[:P, 0:AW], tq[:P, 0:AW], AF.Ln, ["tq"], ["tq"], bias=1.0)
                            act(tq[:P, 0:AW], tq[:P, 0:AW], AF.Exp, ["tq"], ["tq"], scale=-1.0)

                        def t7():
                            tt("pool", khat[:P, 0:AW], enb[:P, 0:AW], kkb[:P, tl, :], ALU.mult, ["enb", ("kk", tl)], ["khat"])
                            tt("dve", tq[:P, 0:AW], tq[:P, 0:AW], qcb[:P, tl, :], ALU.mult, ["tq", ("qc", tl)], ["tq"])
                            tt("dve", qhat[:P, 0:AW], tq[:P, 0:AW], eb[:P, 0:AW], ALU.mult, ["tq", "eb"], ["qhat"])

                        def t9(b):
                            def f():
                                pd, tok = bank(7 - b)
                                for hh in range(NH):
                                    mm(pd[0:64, hh * 64:(hh + 1) * 64], khat[b * BL:(b + 1) * BL, hh * 64:(hh + 1) * 64],
                                       vbb[b * BL:(b + 1) * BL, tl, hh * 64:(hh + 1) * 64], True, True, ["khat", ("vb", tl)], [tok])
                            return f

                        def t11(b):
                            def f():
                                pd, tok = bank(7 - b)
                                tt("dve", Stl[0:64, b, :, :], Sst[:, :, :], bc(3 * b + 0), ALU.mult, ["Sst", "esel"], ["Stl"])
                                tt("dve", d3(dtmp[0:64, 0:AW]), d3(pd[0:64, 0:AW]), bc(3 * b + 2), ALU.mult, [tok, "esel"], ["dtmp"])
                                tt("dve", Sst[:, :, :], Sst[:, :, :], bc(3 * b + 1), ALU.mult, ["Sst", "esel"], ["Sst"])
                                tt("dve", Sst[:, :, :], Sst[:, :, :], d3(dtmp[0:64, 0:AW]), ALU.add, ["Sst", "dtmp"], ["Sst"])
                            return f

                        def t10a():
                            pq, tok = bank(7)
                            pqb = pq[:].bitcast(BF)
                            for hh in range(NH):
                                tr(pqb[0:64, hh * P:(hh + 1) * P], qhat[:P, hh * 64:(hh + 1) * 64], identb[:P, :P], ["qhat", "cb"], [tok])

                        def t12a():
                            pq, tok = bank(7)
                            pq3 = pq[:].bitcast(BF)[0:64, 0:NH * P].rearrange("p (h t) -> p h t", h=NH)
                            cp("dve", qhT[0:64, :, 0:P], pq3, [tok], ["qhT"])
                            S.add("pool", lambda h: h.memset(QAB[0:64, :, :, :], 0.0), [], ["QAB"])
                            for b in range(nblk):
                                cp("dve", QAB[0:64, :, b, b * BL:(b + 1) * BL], pq3[:, :, b * BL:(b + 1) * BL], [tok], ["QAB"])

                        def t10b():
                            pk_, tok = bank(6)
                            pkb = pk_[:].bitcast(BF)
                            for hh in range(NH):
                                tr(pkb[0:64, hh * P:(hh + 1) * P], khat[:P, hh * 64:(hh + 1) * 64], identb[:P, :P], ["khat", "cb"], [tok])

                        def t12b():
                            pk_, tok = bank(6)
                            cp("act", khT[0:64, :, 0:P], pk_[:].bitcast(BF)[0:64, 0:NH * P].rearrange("p (h t) -> p h t", h=NH),
                               [tok], ["khT"])

                        def t13():
                            act(sig[:P, 0:AW], gcb[:P, tl, :], AF.Exp, [("gc", tl)], ["enb"], scale=-1.0)
                            act(sig[:P, 0:AW], sig[:P, 0:AW], AF.Ln, ["enb"], ["enb"], bias=1.0)
                            act(sig[:P, 0:AW], sig[:P, 0:AW], AF.Exp, ["enb"], ["enb"], scale=-1.0)

                        def t14(g):
                            def f():
                                pscb, tok = bank(7 - g)
                                for hh in range(3 * g, 3 * g + 3):
                                    mm(pscb[:P, (hh % 3) * 128:(hh % 3) * 128 + P], khT[0:64, hh, 0:P], qhT[0:64, hh, 0:P], True, True,
                                       ["khT", "qhT"], [tok])
                            return f

                        def t15(g):
                            def f():
                                pscb, tok = bank(7 - g)
                                tt("dve", sc[:P, 3 * g:3 * g + 3, 0:P],
                                   pscb[:P, 0:384].rearrange("p (h t) -> p h t", h=3)[:, :, 0:P],
                                   mH_[:P, 0:P].unsqueeze(1).broadcast_to([P, 3, P]), ALU.mult, [tok, "cb", "mh16b"], ["sc"])
                            return f

                        def t16():
                            po, tok = bank(7)
                            for hh in range(NH):
                                mm(po[:P, hh * 64:(hh + 1) * 64], sc[:P, hh, 0:P], vbb[:P, tl, hh * 64:(hh + 1) * 64], True, False,
                                   ["sc", ("vb", tl)], [tok])
                                for b in range(nblk):
                                    mm(po[:P, hh * 64:(hh + 1) * 64], QAB[0:64, hh, b, 0:P], Stl[0:64, b, hh, :], False, b == nblk - 1,
                                       ["QAB", "Stl"], [tok])

                        def t17():
                            po, tok = bank(7)
                            act(sq[:P, 0:AW], po[:P, 0:AW], AF.Square, [tok], ["eb"])
                            S.add("dve", lambda h: h.tensor_reduce(out=small[:P, 52:58], in_=d3(sq[:P, 0:AW]),
                                                                   axis=AX.X, op=ALU.add), ["eb"], ["smallh"])
                            act(small[:P, 52:58], small[:P, 52:58], AF.Ln, ["smallh", "epsc"], ["smallh"], scale=1.0 / HD,
                                bias=epsrms[:P, 0:1])
                            act(small[:P, 52:58], small[:P, 52:58], AF.Exp, ["smallh"], ["smallh"], scale=-0.5)

                        def t18():
                            po, tok = bank(7)
                            tt("dve", d3(oc32[:P, 0:AW]), d3(po[:P, 0:AW]),
                               small[:P, 52:58].unsqueeze(2).broadcast_to([P, NH, HD]), ALU.mult, [tok, "smallh"], ["tq"])
                            tt("pool", oc32[:P, 0:AW], oc32[:P, 0:AW], nwrow[:P, :], ALU.mult, ["tq", "nwrow"], ["tq"])
                            tt("pool", ocb[:P, 0:AW], oc32[:P, 0:AW], sig[:P, 0:AW], ALU.mult, ["tq", "enb"], ["ocb"])

                        def t19():
                            pt2, tok = bank(6)
                            pt2b = pt2[:].bitcast(BF)
                            for c in range(3):
                                tr(pt2b[:, c * P:(c + 1) * P], ocb[:P, c * 128:(c + 1) * 128], identb[:P, :P], ["ocb", "cb"], [tok])

                        def t20():
                            pt2, tok = bank(6)
                            cp("dve", mixT[:, 5:8, col:col + P], pt2[:].bitcast(BF)[:, 0:3 * P].rearrange("p (c t) -> p c t", c=3), [tok],
                               [("mixB", 2 + c, tl) for c in range(3)])

                        seq_ = [t1, t3, t2, t4, t6, t8, t5, t7]
                        for b in range(nblk):
                            seq_ += [t9(b), t11(b)]
                        seq_ += [t10a, t12a, t10b, t12b, t13, t14(0), t15(0), t14(1), t15(1), t16, t17, t18, t19, t20]
                        return seq_

                    for tl, t in enumerate(tiles):
                        H_th += hg_tile(tl, t)
                    merged = [((i + 0.5) / len(A_th), 0, i, f) for i, f in enumerate(A_th)] + \
                             [((i + 0.5) / len(H_th), 1, i, f) for i, f in enumerate(H_th)]
                    merged.sort(key=lambda x: (x[0], x[1], x[2]))
                    for _, _, _, f in merged:
                        f()

                    pend = tiles

                seq_ = [("out", q) for q in range(NOQ)] + [("ln", t) for t in pend]
                pend_ = pend
                run_blocks(seq_, {"out": lambda bk, sl: outproj_block(bk[1], sl, pend_),
                                  "ln": lambda bk, sl: layernorm_tile(P, bk[1])})

                dst = hp[l, si] if prompt else hs[l]
                dma("sp", dst.rearrange("h d e -> d h e"), Sst[:], "st_s", ["Sst"], [])

                S.fence(MIXER_TOKS)
                dma("sp", lnrow[:, 0, :], ln_g[1][l:l + 1, :].broadcast_to([128, D]), "ld_ln0", [], ["lnrow"])
                dma("sp", lnrow[:, 1, :], ln_b[1][l:l + 1, :].broadcast_to([128, D]), "ld_ln1", [], ["lnrowb"])
                Ltok = (ntile - 1) * 128 + P

                def load_ff(j):
                    s_ = j % 2
                    dma("sp", w1r[:, s_, :, :], wb_f1[l][j], "w1_%d" % s_, [("wbf1", l, j)], [("w1", s_)])
                    dma("sp", w2r[:, s_, :, :], wb_f2[l][j], "w2_%d" % s_, [("wbf2", l, j)], [("w2", s_)])

                load_ff(0)
                for t in range(ntile):
                    make_xT(P, t, x1T, ("x1T", t), t * 128)
                groups = [(g0, min(g0 + 512, Ltok)) for g0 in range(0, Ltok, 512)]

                def ff1(j):
                    s_ = j % 2
                    for c in range(HC):
                        for gi, (g0, g1) in enumerate(groups):
                            n = g1 - g0
                            pb_, ptok = bank()
                            rd = [("x1T", t) for t in range(g0 // 128, (g1 + 127) // 128)] + [("w1", s_)]
                            for kc in range(KC):
                                mm(pb_[:, 0:n], w1r[:, s_, kc, c * 128:(c + 1) * 128], x1T[:, kc, g0:g1], kc == 0, kc == KC - 1, rd, [ptok])
                            rs_ = (c * len(groups) + gi) % 2
                            act(rr[:, rs_, 0:n], pb_[:, 0:n], AF.Relu, [ptok], [("r", rs_)])
                            tt("pool", hT[:, s_, c, g0:g1], rr[:, rs_, 0:n], rr[:, rs_, 0:n], ALU.mult, [("r", rs_)], [("hT", s_, c)])

                def ff2(j, last=False):
                    s_ = j % 2
                    for t in range(ntile):
                        for hf in range(D // 512 if D >= 512 else 1):
                            wcol = min(512, D)
                            pb_, ptok = bank()
                            for c in range(HC):
                                mm(pb_[:P, 0:wcol], hT[:, s_, c, t * 128:t * 128 + P], w2r[:, s_, c, hf * wcol:(hf + 1) * wcol],
                                   c == 0, c == HC - 1, [("hT", s_, c), ("w2", s_)], [ptok])
                            tt("dve", xres[:P, t, hf * wcol:(hf + 1) * wcol], xres[:P, t, hf * wcol:(hf + 1) * wcol], pb_[:P, 0:wcol],
                               ALU.add, [("xres", t), ptok], [("xres", t)])
                        if last:
                            layernorm_tile(P, t)
                            if l == DEPTH - 1:
                                dst = y_dst[t * 128:t * 128 + P, :]
                                dma("sp", dst, xres[:P, t, :], "st_y%d" % (t % 2), [("xres", t)], [])

                ff1(0)
                for j in range(NP_):
                    if j + 1 < NP_:
                        load_ff(j + 1)
                        ff1(j + 1)
                    ff2(j, last=(j == NP_ - 1))

        for si in range(NSEQ):
            run_sequence("p", si)
        run_sequence("s", 0)

        S.emit(nc, st)
    return nc


_CACHE = {}


def _get_nc(cfg_key, cfg):
    if cfg_key not in _CACHE:
        _CACHE[cfg_key] = build(cfg)
    return _CACHE[cfg_key]


def run_cfg(cfg, inputs, ncores):
    nc = build(cfg)
    consts = make_consts()
    f = lambda a: np.ascontiguousarray(np.asarray(a, dtype=np.float32))
    shared = {k: f(inputs[k]) for k in ("w_in", "w_out", "w_ff1", "w_ff2", "mlp_ln_g", "mlp_ln_b", "mlp_ws", "mlp_bs",
                                       "hg_lb_logits", "hg_norm_w", "ln1_g", "ln1_b", "ln2_g", "ln2_b")}
    xp_, xs_ = f(inputs["x_prompt"]), f(inputs["x_sample"])
    ck_, cv_, sh_ = f(inputs["cache_sb_k"]), f(inputs["cache_sb_v"]), f(inputs["state_hgrn"])
    NSEQ = cfg.NSEQ
    in_maps = []
    for c in range(ncores):
        m = dict(shared)
        m["xp"] = np.ascontiguousarray(xp_[c * NSEQ:(c + 1) * NSEQ])
        m["xs"] = np.ascontiguousarray(xs_[c])
        m["ck"] = np.ascontiguousarray(ck_[:, c].reshape(DEPTH, cfg.PAST, AW))
        m["cv"] = np.ascontiguousarray(cv_[:, c].reshape(DEPTH, cfg.PAST, AW))
        m["sh"] = np.ascontiguousarray(sh_[:, c])
        m["consts"] = consts
        in_maps.append(m)
    res = run_bass_kernel_spmd(nc, in_maps, core_ids=list(range(ncores)))
    R = res.results
    B = ncores * NSEQ
    y_prompt = np.concatenate([R[c]["yp"] for c in range(ncores)], axis=0)
    y_sample = np.stack([R[c]["ys"] for c in range(ncores)], axis=0)
    kpo = np.concatenate([R[c]["kp"] for c in range(ncores)], axis=1).reshape(DEPTH, B, cfg.L, NH, HD)
    vpo = np.concatenate([R[c]["vp"] for c in range(ncores)], axis=1).reshape(DEPTH, B, cfg.L, NH, HD)
    hpo = np.concatenate([R[c]["hp"] for c in range(ncores)], axis=1)
    kso = np.stack([R[c]["ks"] for c in range(ncores)], axis=1).reshape(DEPTH, ncores, cfg.LS, NH, HD)
    vso = np.stack([R[c]["vs"] for c in range(ncores)], axis=1).reshape(DEPTH, ncores, cfg.LS, NH, HD)
    hso = np.stack([R[c]["hs"] for c in range(ncores)], axis=1)
    mvo = np.stack([R[c]["mvs"] for c in range(ncores)], axis=1)
    return tuple(np.ascontiguousarray(a, dtype=np.float32) for a in
                 (y_prompt, y_sample, kpo, vpo, hpo, kso, vso, hso, mvo))


def kernel(x_prompt, x_sample, cache_sb_k, cache_sb_v, state_hgrn, w_in, w_out, mlp_ln_g, mlp_ln_b,
           mlp_ws, mlp_bs, hg_lb_logits, hg_norm_w, ln1_g, ln1_b, w_ff1, w_ff2, ln2_g, ln2_b):
    cfg = Cfg()
    inputs = dict(x_prompt=x_prompt, x_sample=x_sample, cache_sb_k=cache_sb_k, cache_sb_v=cache_sb_v,
                  state_hgrn=state_hgrn, w_in=w_in, w_out=w_out, mlp_ln_g=mlp_ln_g, mlp_ln_b=mlp_ln_b,
                  mlp_ws=mlp_ws, mlp_bs=mlp_bs, hg_lb_logits=hg_lb_logits, hg_norm_w=hg_norm_w,
                  ln1_g=ln1_g, ln1_b=ln1_b, w_ff1=w_ff1, w_ff2=w_ff2, ln2_g=ln2_g, ln2_b=ln2_b)
    return run_cfg(cfg, inputs, 8)
```

```python
import contextlib
import math
import numpy as np
import concourse.bass as bass
import concourse.mybir as mybir
from concourse.bass_utils import run_bass_kernel_spmd

F32 = mybir.dt.float32
BF = mybir.dt.bfloat16
AF = mybir.ActivationFunctionType
ALU = mybir.AluOpType
AX = mybir.AxisListType

ENGS = ("pe", "act", "dve", "pool", "sp")
EPOCH = 24000


class Op:
    __slots__ = ("eng", "fn", "deps", "sig", "sigidx", "chan", "chan_idx", "fence")

    def __init__(self, eng, fn):
        self.eng = eng
        self.fn = fn
        self.deps = set()
        self.sig = False
        self.sigidx = 0
        self.chan = None
        self.chan_idx = 0
        self.fence = False


class Sched:
    def __init__(self):
        self.ops = {e: [] for e in ENGS}
        self.lastw = {}
        self.readers = {}
        self.chan_last = {}
        self.chan_cnt = {}
        self.aliases = {}

    def alias(self, ta, tb):
        for a in ta:
            self.aliases.setdefault(a, []).extend(tb)
        for b in tb:
            self.aliases.setdefault(b, []).extend(ta)

    def add(self, eng, fn, reads=(), writes=(), chan=None):
        op = Op(eng, fn)
        if chan is not None:
            op.chan = chan
            prev = self.chan_last.get(chan)
            if prev is not None:
                op.deps.add(prev)
            self.chan_cnt[chan] = self.chan_cnt.get(chan, 0) + 1
            op.chan_idx = self.chan_cnt[chan]
            self.chan_last[chan] = op
        lastw = self.lastw
        readers = self.readers
        for t in reads:
            w = lastw.get(t)
            if w is not None:
                op.deps.add(w)
        k = ("c", chan) if chan is not None else eng
        for t in writes:
            al = self.aliases.get(t)
            tl = (t,) if not al else [t] + al
            for t2 in tl:
                w = lastw.get(t2)
                if w is not None:
                    op.deps.add(w)
                r = readers.get(t2)
                if r:
                    op.deps.update(r.values())
        for t in reads:
            r = readers.get(t)
            if r is None:
                readers[t] = {k: op}
            else:
                r[k] = op
        for t in writes:
            lastw[t] = op
            readers[t] = {}
        op.deps.discard(op)
        self.ops[eng].append(op)
        return op

    def fence(self, tokens, engines=ENGS):
        deps = set()
        for t in tokens:
            w = self.lastw.get(t)
            if w is not None:
                deps.add(w)
            r = self.readers.get(t)
            if r:
                deps.update(r.values())
            self.lastw[t] = None
            self.readers[t] = {}
        for e in engines:
            op = Op(e, None)
            op.fence = True
            op.deps = set(deps)
            self.ops[e].append(op)

    def emit(self, nc, stack):
        for e in ENGS:
            for op in self.ops[e]:
                for d in op.deps:
                    if d.chan is None:
                        if d.eng == "pe" and e == "pe" and not op.fence:
                            continue
                        d.sig = True
        sems = {}
        for e in ENGS:
            c = 0
            for op in self.ops[e]:
                if op.chan is None and op.sig:
                    c += 1
                    op.sigidx = c
            n = (c + EPOCH - 1) // EPOCH
            sems[e] = [stack.enter_context(nc.semaphore(f"s_{e}_{i}")) for i in range(n)]
        csems = {}
        for i, ch in enumerate(self.chan_cnt):
            csems[ch] = stack.enter_context(nc.semaphore(f"c_{i}"))
        block = stack.enter_context(nc.Block())
        handles = {"pe": block.tensor, "act": block.scalar, "dve": block.vector,
                   "pool": block.gpsimd, "sp": block.sync}

        def make_prog(e):
            ops = self.ops[e]

            def prog(h):
                seen = {}
                for op in ops:
                    need = {}
                    for d in op.deps:
                        if d.chan is not None:
                            kk = ("c", d.chan)
                            v = d.chan_idx
                        else:
                            if d.eng == "pe" and e == "pe" and not op.fence:
                                continue
                            kk = d.eng
                            v = d.sigidx
                        if v > need.get(kk, 0):
                            need[kk] = v
                    for kk, v in need.items():
                        if seen.get(kk, 0) >= v:
                            continue
                        seen[kk] = v
                        if isinstance(kk, tuple):
                            h.wait_ge(csems[kk[1]], 16 * v)
                        else:
                            ep, val = divmod(v - 1, EPOCH)
                            h.wait_ge(sems[kk][ep], val + 1)
                    if op.fence:
                        continue
                    ins = op.fn(h)
                    if op.chan is not None:
                        ins.then_inc(csems[op.chan], 16)
                    elif op.sig:
                        ins.then_inc(sems[e][(op.sigidx - 1) // EPOCH], 1)
                for ch, cnt in self.chan_cnt.items():
                    if self.chan_last[ch].eng == e:
                        h.wait_ge(csems[ch], 16 * cnt)
            return prog

        for e in ENGS:
            if self.ops[e]:
                handles[e](make_prog(e))


DEPTH = 2
NH = 6
HD = 64
AW = 384
MW = 256
INW = 3200
MIXW = 1024
ALPHA = (2 * DEPTH) ** 0.25
LN_EPS = 1e-5
RMS_EPS = 1e-6
SECTIONS = [("qa", 0, 384), ("ka", 384, 384), ("va", 768, 384), ("ub", 1152, 256), ("vb", 1408, 256),
            ("qc", 1664, 384), ("fc", 2048, 384), ("ic", 2432, 384), ("gc", 2816, 384)]
OQ = 256
NCH = 8


class Cfg:
    def __init__(self, D=1024, DFF=4096, NSEQ=4, L=2048, LS=16, PAST=1024, HP=512, UT=4):
        self.D, self.DFF, self.NSEQ, self.L, self.LS, self.PAST, self.HP, self.UT = D, DFF, NSEQ, L, LS, PAST, HP, UT
        self.KC = D // 128
        self.NT = L // 128
        self.NP = DFF // HP
        self.HC = HP // 128
        self.NOQ = D // OQ
        self.LK = max(L, PAST + 128)
        self.KT = self.LK // 128


def make_consts():
    c = np.zeros((128, 1312), np.float32)
    s = np.arange(128)[:, None]
    t = np.arange(128)[None, :]
    c[:, 0:128] = (s == t)
    c[:, 128:256] = -(s >= t).astype(np.float32)
    c[:, 256:384] = -1.0
    c[:, 384:512] = (s < t)
    blk_s, blk_t = s // 64, t // 64
    same = (blk_s == blk_t)
    c[:, 512:640] = same & (s <= t)
    mid_t = blk_t * 64 + 31
    c[:, 640:768] = same * ((s <= t).astype(np.float32) - (s <= mid_t).astype(np.float32))
    m16 = np.zeros((128, 128), np.float32)
    m16[:16, :16] = (s[:16] <= t[:, :16])
    c[:, 768:896] = m16
    d16 = np.zeros((128, 128), np.float32)
    d16[:16, :16] = (s[:16] <= t[:, :16]).astype(np.float32) - (s[:16] <= 7).astype(np.float32)
    c[:, 896:1024] = d16
    c[:, 1024:1152] = (t <= s)
    sel = np.zeros((128, 8), np.float32)
    for b in range(2):
        inb = (s[:, 0] // 64 == b)
        mid = 64 * b + 31
        sel[:, 3 * b + 0] = inb & (s[:, 0] <= mid)
        sel[:, 3 * b + 1] = inb
        sel[:, 3 * b + 2] = inb & (s[:, 0] > mid)
    c[:, 1152:1160] = sel
    sel16 = np.zeros((128, 8), np.float32)
    sel16[:16, 0] = (s[:16, 0] <= 7)
    sel16[:16, 1] = 1.0
    sel16[:16, 2] = (s[:16, 0] > 7)
    c[:, 1160:1168] = sel16
    return c


def build(cfg):
    D, DFF, KC, NT, NSEQ, L, LS, PAST, HP, HC, NP_, NOQ = (cfg.D, cfg.DFF, cfg.KC, cfg.NT, cfg.NSEQ, cfg.L, cfg.LS,
                                                          cfg.PAST, cfg.HP, cfg.HC, cfg.NP, cfg.NOQ)
    KT, LK, UT = cfg.KT, cfg.LK, cfg.UT
    NU = UT * 128
    nc = bass.Bass("TRN2", target_bir_lowering=False)
    S = Sched()

    def din(name, shape):
        return nc.dram_tensor(name, list(shape), F32, kind="ExternalInput").ap()

    def dout(name, shape):
        return nc.dram_tensor(name, list(shape), F32, kind="ExternalOutput").ap()

    def dint(name, shape):
        return nc.dram_tensor(name, list(shape), BF, kind="Internal").ap()

    xp = din("xp", [NSEQ, L, D])
    xs = din("xs", [LS, D])
    ck = din("ck", [DEPTH, PAST, AW])
    cv = din("cv", [DEPTH, PAST, AW])
    sh = din("sh", [DEPTH, NH, HD, HD])
    w_in = din("w_in", [DEPTH, D, INW])
    w_out = din("w_out", [DEPTH, MIXW, D])
    w_ff1 = din("w_ff1", [DEPTH, D, DFF])
    w_ff2 = din("w_ff2", [DEPTH, DFF, D])
    mlp_ln_g = din("mlp_ln_g", [DEPTH, MW])
    mlp_ln_b = din("mlp_ln_b", [DEPTH, MW])
    mlp_ws = din("mlp_ws", [DEPTH, 4, 128, 128])
    mlp_bs = din("mlp_bs", [DEPTH, 4, 128])
    hg_lb = din("hg_lb_logits", [DEPTH, AW])
    hg_nw = din("hg_norm_w", [DEPTH, AW])
    ln_g = [din("ln1_g", [DEPTH, D]), din("ln2_g", [DEPTH, D])]
    ln_b = [din("ln1_b", [DEPTH, D]), din("ln2_b", [DEPTH, D])]
    cst = din("consts", [128, 1312])

    yp = dout("yp", [NSEQ, L, D])
    ys = dout("ys", [LS, D])
    kp = dout("kp", [DEPTH, NSEQ, L, AW])
    vp = dout("vp", [DEPTH, NSEQ, L, AW])
    hp = dout("hp", [DEPTH, NSEQ, NH, HD, HD])
    ks = dout("ks", [DEPTH, LS, AW])
    vs = dout("vs", [DEPTH, LS, AW])
    hs = dout("hs", [DEPTH, NH, HD, HD])
    mvs = dout("mvs", [DEPTH, LS, MW])

    wb_in = [[dint(f"wbin{l}_{i}", [128, KC, n]) for i, (_, _, n) in enumerate(SECTIONS)] for l in range(DEPTH)]
    wb_out = [[dint(f"wbout{l}_{q}", [128, NCH, OQ]) for q in range(NOQ)] for l in range(DEPTH)]
    wb_f1 = [[dint(f"wbf1{l}_{j}", [128, KC, HP]) for j in range(NP_)] for l in range(DEPTH)]
    wb_f2 = [[dint(f"wbf2{l}_{j}", [128, HC, D]) for j in range(NP_)] for l in range(DEPTH)]

    st = contextlib.ExitStack()
    with st:
        def sb(name, shape, dt):
            return st.enter_context(nc.sbuf_tensor(name, list(shape), dt))

        xres = sb("xres", [128, NT, D], F32)
        lnrow = sb("lnrow", [128, 2, D], F32)
        mlprow = sb("mlprow", [128, 2, MW], F32)
        hgrow = sb("hgrow", [128, DEPTH, 2, AW], F32)
        nwrow = sb("nwrow", [128, AW], F32)
        Sst = sb("Sst", [64, NH, HD], F32)
        cf = sb("cf", [128, 400], F32)
        cb = sb("cb", [128, 640], BF)
        mh16b = sb("mh16b", [128, 128], BF)
        WTl = sb("WTl", [128, DEPTH, 4, 128], BF)
        bsT = sb("bsT", [128, DEPTH, 4], F32)
        xb2 = sb("xb", [128, 2, D], BF)
        xbc = [0]
        small = sb("small", [128, 64], F32)
        lnsm = sb("lnsm", [128, 4, 40], F32)

        identb = cb[:, 0:128]
        ntrib = cb[:, 128:256]
        nonesb = cb[:, 256:384]
        maskSb = cb[:, 384:512]
        maskHb = cb[:, 512:640]
        Dm = cf[:, 0:128]
        Dm16 = cf[:, 128:256]
        trilf = cf[:, 256:384]
        SelC = cf[:, 384:392]
        SelC16 = cf[:, 392:400]

        off = [0]
        arena_views = []

        def carve(nbytes):
            o = off[0]
            off[0] += (nbytes + 63) // 64 * 64
            return o

        lay = {}
        for name, nb in [("kT", 3 * LK * 2), ("vfull", KT * AW * 2), ("xTmix", NCH * NU * 2),
                         ("qT", 3 * NU * 2), ("u", UT * MW * 2), ("vn", UT * MW * 2), ("logf", UT * AW * 4),
                         ("qc", UT * AW * 2), ("vb", UT * AW * 2), ("gc", UT * AW * 2), ("kk", UT * AW * 2),
                         ("ring", 3 * 6144), ("stage", 4 * AW * 4), ("tmp", 14336 + 18688)]:
            lay[name] = (carve(nb), nb)
        mixer_bytes = off[0]
        off[0] = 0
        flay = {}
        TG = min(512, NT * 128)
        for name, nb in [("x1T", KC * NT * 128 * 2), ("hT", 2 * HC * NT * 128 * 2), ("w1", 2 * KC * HP * 2),
                         ("w2", 2 * HC * D * 2), ("r", 2 * 512 * 2)]:
            flay[name] = (carve(nb), nb)
        ffn_bytes = off[0]
        ARENA = max(mixer_bytes, ffn_bytes)
        arena = sb("arena", [128, ARENA // 2], BF)

        def view(layd, name, dt, pat=None, **kw):
            o, nb = layd[name]
            v = arena[:, o // 2:(o + nb) // 2]
            if dt == F32:
                v = v.bitcast(F32)
            if pat:
                v = v.rearrange(pat, **kw)
            return v

        kT = view(lay, "kT", BF, "p (m k) -> p m k", m=3)
        vfull = view(lay, "vfull", BF, "p (t c) -> p t c", c=AW)
        mixT = view(lay, "xTmix", BF)[:, 0:NCH * NU].rearrange("p (k n) -> p k n", k=NCH)
        qT = view(lay, "qT", BF, "p (m n) -> p m n", m=3)
        ub = view(lay, "u", BF, "p (t c) -> p t c", c=MW)
        vn = view(lay, "vn", BF, "p (t c) -> p t c", c=MW)
        logf = view(lay, "logf", F32, "p (t c) -> p t c", c=AW)
        qcb = view(lay, "qc", BF, "p (t c) -> p t c", c=AW)
        vbb = view(lay, "vb", BF, "p (t c) -> p t c", c=AW)
        gcb = view(lay, "gc", BF, "p (t c) -> p t c", c=AW)
        kkb = view(lay, "kk", BF, "p (t c) -> p t c", c=AW)
        ring = view(lay, "ring", BF, "p (s n) -> p s n", s=3)
        stage = view(lay, "stage", F32, "p (s c) -> p s c", c=AW)
        tmpo = lay["tmp"][0]

        def tview(o, nbytes, dt):
            v = arena[:, (tmpo + o) // 2:(tmpo + o + nbytes) // 2]
            return v.bitcast(F32) if dt == F32 else v

        att = []
        for c in range(2):
            b0 = c * 7168
            att.append(dict(
                ez=[tview(b0, 1024, BF), tview(b0 + 1024, 1024, BF)],
                sp=[tview(b0 + 2048, 1024, BF), tview(b0 + 3072, 1024, BF)],
                w=[tview(b0 + 4096, 1024, BF), tview(b0 + 5120, 1024, BF)],
                rs=tview(b0 + 6144, 1024, BF)))
        assert KC * NU * 2 <= 14336
        xT = tview(0, KC * NU * 2, BF).rearrange("p (k n) -> p k n", k=KC)
        ho = [14336]

        def hv(nbytes, dt):
            o = ho[0]
            ho[0] += nbytes
            return tview(o, nbytes, dt)
        eb = hv(1536, F32)
        enb = hv(1536, F32)
        tq = hv(1536, F32)
        qhat = hv(768, BF)
        khat = hv(768, BF)
        qhT = hv(1536, BF).rearrange("p (h t) -> p h t", h=NH)
        QAB = hv(3072, BF).rearrange("p (h b t) -> p h b t", h=NH, b=2)
        khT = hv(1536, BF).rearrange("p (h t) -> p h t", h=NH)
        sc = hv(1536, BF).rearrange("p (h t) -> p h t", h=NH)
        Stl = hv(1536, BF).rearrange("p (b h e) -> p b h e", b=2, h=NH)
        dtmp = hv(1536, F32)
        sq = eb
        sig = enb
        oc32 = tq
        ocb = hv(768, BF)
        vn32 = eb
        obb = hv(512, BF)
        esel = hv(256, F32)
        assert ho[0] <= 14336 + 18688, ho[0]
        ATT_T = ["ez00", "ez01", "ez10", "ez11", "sp00", "sp01", "sp10", "sp11", "w00", "w01", "w10", "w11", "rs0", "rs1"]
        HG_T = ["eb", "enb", "tq", "qhat", "khat", "qhT", "QAB", "khT", "sc", "Stl", "dtmp",
                "ocb", "obb", "esel"]
        XT_T = [("xT", i) for i in range(UT)]
        MIX_T = [("mixA", h) for h in range(NH)] + [("mixB", c, i) for c in range(5) for i in range(UT)]
        S.alias(XT_T, ATT_T)
        S.alias([("ps", 4)], ["psO0", "psO1"])

        x1T = view(flay, "x1T", BF, "p (k n) -> p k n", k=KC)
        hT = view(flay, "hT", BF, "p (s c n) -> p s c n", s=2, c=HC)
        w1r = view(flay, "w1", BF, "p (s k n) -> p s k n", s=2, k=KC)
        w2r = view(flay, "w2", BF, "p (s c n) -> p s c n", s=2, c=HC)
        rr = view(flay, "r", BF, "p (s n) -> p s n", s=2)

        MIXER_TOKS = (["kT%d_%d" % (m, u) for m in range(3) for u in range(KT)] + [("v", t) for t in range(KT)]
                      + XT_T + MIX_T + [("qT", m) for m in range(3)]
                      + [(nm, i) for nm in ("u", "vn", "logf", "qc", "vb", "gc", "kk") for i in range(UT)]
                      + [("ring", s_) for s_ in range(3)] + [("ringb", s_) for s_ in range(3)] + [("stage", s_) for s_ in range(4)] + ATT_T + HG_T)
        FFN_TOKS = ([("x1T", t) for t in range(NT)] + [("hT", s_, c) for s_ in range(2) for c in range(HC)]
                    + [("w1", s_) for s_ in range(2)] + [("w2", s_) for s_ in range(2)] + [("r", s_) for s_ in range(2)])

        pbanks = [st.enter_context(nc.psum_tensor(f"pb{i}", [128, 512], F32)) for i in range(8)]
        pcount = [0]

        def bank(i=None):
            if i is None:
                i = pcount[0] % 8
                pcount[0] += 1
            return pbanks[i], ("ps", i)

        def mm(out, lhsT, rhs, start, stop, reads, writes, sgc=False):
            if sgc:
                S.add("pe", lambda h: h.matmul(out, lhsT=lhsT, rhs=rhs, start=start, stop=stop, skip_group_check=True),
                      reads, writes)
            else:
                S.add("pe", lambda h: h.matmul(out, lhsT=lhsT, rhs=rhs, start=start, stop=stop), reads, writes)

        def tr(out, in_, ident, reads, writes):
            S.add("pe", lambda h: h.transpose(out=out, in_=in_, identity=ident), reads, writes)

        def act(out, in_, func, reads, writes, scale=None, bias=None):
            kw = {}
            if scale is not None:
                kw["scale"] = scale
            if bias is not None:
                kw["bias"] = bias
            S.add("act", lambda h: h.activation(out=out, in_=in_, func=func, **kw), reads, writes)

        def tt(eng, out, in0, in1, op, reads, writes):
            S.add(eng, lambda h: h.tensor_tensor(out=out, in0=in0, in1=in1, op=op), reads, writes)

        def tsc(eng, out, in0, s1, s2, op0, op1, reads, writes):
            if op1 is None:
                S.add(eng, lambda h: h.tensor_scalar(out=out, in0=in0, scalar1=s1, scalar2=None, op0=op0), reads, writes)
            else:
                S.add(eng, lambda h: h.tensor_scalar(out=out, in0=in0, scalar1=s1, scalar2=s2, op0=op0, op1=op1),
                      reads, writes)

        def stt(out, in0, scalar, in1, op0, op1, reads, writes):
            S.add("dve", lambda h: h.scalar_tensor_tensor(out=out, in0=in0, scalar=scalar, in1=in1, op0=op0, op1=op1),
                  reads, writes)

        def cp(eng, out, in_, reads, writes):
            if eng == "act":
                S.add("act", lambda h: h.activation(out=out, in_=in_, func=AF.Copy), reads, writes)
            else:
                S.add(eng, lambda h: h.tensor_copy(out=out, in_=in_), reads, writes)

        def recip(out, in_, reads, writes):
            S.add("dve", lambda h: h.reciprocal(out=out, in_=in_), reads, writes)

        def dma(eng, out, in_, chan, reads, writes, **kw):
            S.add(eng, lambda h: h.dma_start(out=out, in_=in_, **kw), reads, writes, chan=chan)

        dma("sp", cf[:, 0:128], cst[:, 640:768], "ld_c", [], ["cf"])
        dma("sp", cf[:, 128:256], cst[:, 896:1024], "ld_c", [], ["cf"])
        dma("sp", cf[:, 256:384], cst[:, 1024:1152], "ld_c", [], ["cf"])
        dma("sp", cf[:, 384:400], cst[:, 1152:1168], "ld_c", [], ["cf"])
        dma("pool", cb[:], cst[:, 0:640], "cv0", [], ["cb"])
        dma("pool", mh16b[:], cst[:, 768:896], "cv1", [], ["mh16b"])
        for l in range(DEPTH):
            dma("sp", bsT[:, l, :], mlp_bs[l].rearrange("g t -> t g"), "ld_c", [], ["bsT"],
                allow_slow_non_contiguous=True)
        L0 = eb[:, 0:AW]
        L1 = enb[:, 0:AW]
        T0 = tq[:, 0:AW]
        T1 = dtmp[:, 0:AW]
        dma("sp", L0, hg_lb[0:1, :].broadcast_to([128, AW]), "ld_c", [], ["eb"])
        dma("sp", L1, hg_lb[1:2, :].broadcast_to([128, AW]), "ld_c", [], ["enb"])
        tt("dve", T0, L0, L1, ALU.max, ["eb", "enb"], ["tq"])
        tt("dve", L0, L0, T0, ALU.subtract, ["eb", "tq"], ["eb"])
        tt("dve", L1, L1, T0, ALU.subtract, ["enb", "tq"], ["enb"])
        act(L0, L0, AF.Exp, ["eb"], ["eb"])
        act(L1, L1, AF.Exp, ["enb"], ["enb"])
        tt("dve", T0, L0, L1, ALU.add, ["eb", "enb"], ["tq"])
        recip(T0, T0, ["tq"], ["tq"])
        tt("dve", L0, L0, T0, ALU.mult, ["eb", "tq"], ["eb"])
        tt("dve", L1, L1, T0, ALU.mult, ["enb", "tq"], ["enb"])
        tt("dve", T1, L0, L1, ALU.add, ["eb", "enb"], ["dtmp"])
        tt("dve", hgrow[:, 0, 0, :], L0, L0, ALU.subtract, ["eb"], ["hgrow"])
        tt("dve", hgrow[:, 1, 0, :], T1, L0, ALU.subtract, ["dtmp", "eb"], ["hgrow"])
        for l in range(DEPTH):
            tsc("dve", hgrow[:, l, 1, :], hgrow[:, l, 0, :], -1.0, 1.0, ALU.mult, ALU.add, ["hgrow"], ["hgrow"])
        for l in range(DEPTH):
            wsrc = tview(14336, 2048, F32).rearrange("p (g s) -> p g s", g=4)
            dma("sp", wsrc, mlp_ws[l].rearrange("g t s -> t g s"), "ld_c", [], ["eb", "enb"])
            wsm = tview(14336 + 4608, 1024, BF).rearrange("p (g s) -> p g s", g=4)
            tt("dve", wsm, wsrc, trilf.unsqueeze(1).broadcast_to([128, 4, 128]), ALU.mult,
               ["eb", "enb", "cf"], ["qhat", "khat"])
            pb_, ptok = bank()
            pbb = pb_[:].bitcast(BF)
            for g in range(4):
                tr(pbb[:, g * 128:(g + 1) * 128], wsm[:, g, :], identb, ["qhat", "khat", "cb"], [ptok])
            cp("dve", WTl[:, l, :, :], pbb[:, 0:512].rearrange("p (g t) -> p g t", g=4), [ptok], ["WTl"])

        cvi = [0]

        def conv(out, in_, tok):
            dma("pool", out, in_, "cv%d" % (cvi[0] % 4), [], [tok])
            cvi[0] += 1

        def conv_layer_in(l):
            for i, (_, c0, n) in enumerate(SECTIONS):
                conv(wb_in[l][i], w_in[l][:, c0:c0 + n].rearrange("(k p) n -> p k n", p=128), ("wbin", l, i))
            for q in range(NOQ):
                conv(wb_out[l][q], w_out[l][:, q * OQ:(q + 1) * OQ].rearrange("(c p) n -> p c n", p=128), ("wbout", l, q))

        def conv_layer_ff(l):
            for j in range(NP_):
                conv(wb_f1[l][j], w_ff1[l][:, j * HP:(j + 1) * HP].rearrange("(k p) n -> p k n", p=128), ("wbf1", l, j))
                conv(wb_f2[l][j], w_ff2[l][j * HP:(j + 1) * HP, :].rearrange("(c p) n -> p c n", p=128), ("wbf2", l, j))

        conv_layer_in(0)
        conv_layer_ff(0)
        conv_layer_in(1)
        conv_layer_ff(1)

        NCK = (D + 511) // 512
        CW = D // NCK

        def layernorm_tile(P, t):
            xt_ = ("xres", t)
            sl_ = t % 4
            sm = lnsm[:, sl_, :]
            k1, k2 = ("lns", sl_), ("lns2", sl_)
            for i in range(NCK):
                S.add("dve", lambda h, i=i: h.bn_stats(out=sm[:P, 6 * i:6 * i + 6], in_=xres[:P, t, i * CW:(i + 1) * CW]),
                      [xt_], [k1])
            S.add("dve", lambda h: h.bn_aggr(out=sm[:P, 32:34], in_=sm[:P, 0:6 * NCK]), [k1], [k1])
            act(sm[:P, 34:35], sm[:P, 33:34], AF.Ln, [k1, "epsc"], [k1], bias=epsln[:P, 0:1])
            act(sm[:P, 35:36], sm[:P, 34:35], AF.Exp, [k1], [k1], scale=-0.5)
            tsc("dve", sm[:P, 36:37], sm[:P, 32:33], sm[:P, 35:36], -1.0, ALU.mult, ALU.mult, [k1], [k2])
            act(xres[:P, t, :], xres[:P, t, :], AF.Identity, [xt_, k1, k2], [xt_],
                scale=sm[:P, 35:36], bias=sm[:P, 36:37])
            tt("pool", xres[:P, t, :], xres[:P, t, :], lnrow[:P, 0, :], ALU.mult, [xt_, "lnrow"], [xt_])
            tt("pool", xres[:P, t, :], xres[:P, t, :], lnrow[:P, 1, :], ALU.add, [xt_, "lnrowb"], [xt_])

        epsc = sb("epsc", [128, 2], F32)
        epsln = epsc[:, 0:1]
        epsrms = epsc[:, 1:2]
        S.add("pool", lambda h: h.memset(epsc[:, 0:1], LN_EPS), [], ["epsc"])
        S.add("pool", lambda h: h.memset(epsc[:, 1:2], RMS_EPS), ["epsc"], ["epsc"])

        def make_xT(P, t, dst, dst_tok, col0):
            xt_ = ("xres", t)
            xs_ = xbc[0] % 2
            xbc[0] += 1
            xb = xb2[:, xs_, :]
            xbt = ("xb", xs_)
            cp("dve", xb[:P, :], xres[:P, t, :], [xt_], [xbt])
            S.add("act", lambda h: h.mul(out=xres[:P, t, :], in_=xres[:P, t, :], mul=ALPHA), [xt_], [xt_])
            pb_, ptok = bank()
            pbb = pb_[:].bitcast(BF)
            for kc in range(KC):
                tr(pbb[:, kc * P:(kc + 1) * P], xb[:P, kc * 128:(kc + 1) * 128], identb[:P, :P], [xbt, "cb"], [ptok])
            cp("dve", dst[:, :, col0:col0 + P], pbb[:, 0:KC * P].rearrange("p (k t) -> p k t", k=KC), [ptok], [dst_tok])

        def run_sequence(kind, si):
            prompt = kind == "p"
            P = 128 if prompt else LS
            ntile = NT if prompt else 1
            kt0 = 0 if prompt else PAST // 128
            nblk, BL = (2, 64) if prompt else (1, LS)
            Dm_ = Dm if prompt else Dm16
            Sel_ = SelC if prompt else SelC16
            mH_ = maskHb if prompt else mh16b
            units = [list(range(u, min(u + UT, ntile))) for u in range(0, ntile, UT)]
            x_src = xp[si] if prompt else xs
            y_dst = yp[si] if prompt else ys

            for tiles in units:
                t0 = tiles[0]
                if prompt:
                    dma("sp", xres[:, t0:t0 + len(tiles), :],
                        x_src[t0 * 128:(t0 + len(tiles)) * 128, :].rearrange("(t p) d -> p t d", p=128),
                        "ld_x", [], [("xres", t) for t in tiles])
                else:
                    dma("sp", xres[:P, 0, :], x_src, "ld_x", [], [("xres", 0)])

            for l in range(DEPTH):
                S.fence(FFN_TOKS)
                dma("sp", lnrow[:, 0, :], ln_g[0][l:l + 1, :].broadcast_to([128, D]), "ld_ln0", [], ["lnrow"])
                dma("sp", lnrow[:, 1, :], ln_b[0][l:l + 1, :].broadcast_to([128, D]), "ld_ln1", [], ["lnrowb"])
                dma("sp", mlprow[:, 0, :], mlp_ln_g[l:l + 1, :].broadcast_to([128, MW]), "ld_ln2", [], ["mlprow"])
                dma("sp", mlprow[:, 1, :], mlp_ln_b[l:l + 1, :].broadcast_to([128, MW]), "ld_ln3", [], ["mlprowb"])
                dma("sp", nwrow[:, :], hg_nw[l:l + 1, :].broadcast_to([128, AW]), "ld_ln4", [], ["nwrow"])
                if prompt:
                    S.add("pool", lambda h: h.memset(Sst[:], 0.0), [], ["Sst"])
                else:
                    dma("sp", Sst[:], sh[l].rearrange("h d e -> d h e"), "ld_s", [], ["Sst"])
                    npt = PAST // 128
                    dma("pool", vfull[:, 0:npt, :], cv[l].rearrange("(t p) c -> p t c", p=128), "ld_cv", [],
                        [("v", t) for t in range(npt)])
                    kc_tmp = tview(0, npt * AW * 2, BF).rearrange("p (t c) -> p t c", c=AW)
                    dma("pool", kc_tmp, ck[l].rearrange("(t p) c -> p t c", p=128), "ld_ck", [], ATT_T)
                    for t in range(npt):
                        pb_, ptok = bank()
                        pbb = pb_[:].bitcast(BF)
                        for m in range(3):
                            tr(pbb[:, m * 128:(m + 1) * 128], kc_tmp[:, t, m * 128:(m + 1) * 128], identb,
                               ATT_T + ["cb"], [ptok])
                        cp("dve", kT[:, :, t * 128:(t + 1) * 128], pbb[:, 0:384].rearrange("p (m k) -> p m k", m=3),
                           [ptok], ["kT%d_%d" % (m, t) for m in range(3)])

                ringcnt = [0]

                def load_block(blk):
                    slot = ringcnt[0] % 3
                    ringcnt[0] += 1
                    if blk[0] == "in":
                        i = blk[1]
                        n = SECTIONS[i][2]
                        dst = ring[:, slot, 0:KC * n].rearrange("p (k n) -> p k n", k=KC)
                        dma("sp", dst, wb_in[l][i], "ring%d" % slot, [("wbin", l, i)], [("ring", slot), ("ringb", slot)])
                    else:
                        i = blk[1]
                        dst = ring[:, slot, 0:NCH * OQ].rearrange("p (k n) -> p k n", k=NCH)
                        dma("sp", dst, wb_out[l][i], "ring%d" % slot, [("wbout", l, i)], [("ring", slot), ("ringb", slot)])
                    return slot

                def outproj_block(q, slot, tiles_o):
                    Wo = ring[:, slot, 0:NCH * OQ].rearrange("p (k n) -> p k n", k=NCH)
                    rtok = ("ring", slot)
                    for tl, t in enumerate(tiles_o):
                        col = tl * 128
                        pb_, ptok = bank()
                        for pr in range(3):
                            mm(pb_[:P, 0:OQ], mixT[:, pr, col:col + P], Wo[:, pr, :], pr == 0, False,
                               [("mixA", 2 * pr), ("mixA", 2 * pr + 1), rtok], [ptok])
                        for c in range(5):
                            mm(pb_[:P, 0:OQ], mixT[:, 3 + c, col:col + P], Wo[:, 3 + c, :], False, c == 4,
                               [("mixB", c, tl), ("ringb", slot)], [ptok])
                        tt("dve", xres[:P, t, q * OQ:(q + 1) * OQ], xres[:P, t, q * OQ:(q + 1) * OQ], pb_[:P, 0:OQ], ALU.add,
                           [("xres", t), ptok], [("xres", t)])

                def run_blocks(seq_, fns):
                    pcs = [i for i, bk in enumerate(seq_) if bk[0] in ("in", "out")]
                    slot_of = {}
                    nl = [0]

                    def ensure(upto_rank):
                        while nl[0] <= upto_rank and nl[0] < len(pcs):
                            slot_of[pcs[nl[0]]] = load_block(seq_[pcs[nl[0]]])
                            nl[0] += 1
                    rank = 0
                    for i, bk in enumerate(seq_):
                        if bk[0] in ("in", "out"):
                            ensure(rank + 2)
                            fns[bk[0]](bk, slot_of[i])
                            rank += 1
                        else:
                            fns[bk[0]](bk, None)

                pend = None
                for tiles in units:
                    nt_u = len(tiles)
                    N = (nt_u - 1) * 128 + P
                    kcol0 = (kt0 + tiles[0]) * 128
                    for tl, t in enumerate(tiles):
                        make_xT(P, t, xT, ("xT", tl), tl * 128)
                    xT_reads = [("xT", tl) for tl in range(nt_u)]

                    def inproj_block(pi, slot):
                        sname, c0_, n = SECTIONS[pi]
                        W = ring[:, slot, 0:KC * n].rearrange("p (k n) -> p k n", k=KC)
                        rtok = ("ring", slot)
                        rtok2 = ("ringb", slot)
                        if sname in ("qa", "ka"):
                            for m in range(3):
                                pb_, ptok = bank()
                                for kc in range(KC):
                                    mm(pb_[:, 0:N], W[:, kc, m * 128:(m + 1) * 128], xT[:, kc, 0:N], kc == 0, kc == KC - 1,
                                       xT_reads + [rtok, rtok2], [ptok])
                                if sname == "qa":
                                    act(qT[:, m, 0:N], pb_[:, 0:N], AF.Copy, [ptok], [("qT", m)], scale=0.125)
                                else:
                                    cp("dve", kT[:, m, kcol0:kcol0 + N], pb_[:, 0:N], [ptok],
                                       ["kT%d_%d" % (m, kt0 + t) for t in tiles])
                        if sname != "qa":
                            for tl, t in enumerate(tiles):
                                pb_, ptok = bank()
                                for kc in range(KC):
                                    mm(pb_[:P, 0:n], xT[:, kc, tl * 128:tl * 128 + P], W[:, kc, 0:n], kc == 0, kc == KC - 1,
                                       [("xT", tl), rtok, rtok2], [ptok])
                                ps_ = pb_[:P, 0:n]
                                tok_lo, tok_hi = t * 128, t * 128 + P
                                if sname == "ka":
                                    ss = (t % 2)
                                    cp("dve", stage[:P, ss, :], ps_, [ptok], [("stage", ss)])
                                    dst = kp[l, si, tok_lo:tok_hi, :] if prompt else ks[l]
                                    dma("sp", dst, stage[:P, ss, :], "st_k%d" % ss, [("stage", ss)], [])
                                elif sname == "va":
                                    ss = 2 + (t % 2)
                                    cp("act", stage[:P, ss, :], ps_, [ptok], [("stage", ss)])
                                    dst = vp[l, si, tok_lo:tok_hi, :] if prompt else vs[l]
                                    dma("sp", dst, stage[:P, ss, :], "st_v%d" % (ss - 2), [("stage", ss)], [])
                                    cp("pool", vfull[:P, kt0 + t, :], stage[:P, ss, :], [("stage", ss)], [("v", kt0 + t)])
                                elif sname == "ub":
                                    cp("act", ub[:P, tl, :], ps_, [ptok], [("u", tl)])
                                elif sname == "vb":
                                    S.add("dve", lambda h, ps_=ps_: h.bn_stats(out=small[:P, 40:46], in_=ps_), [ptok], ["smallg"])
                                    S.add("dve", lambda h: h.bn_aggr(out=small[:P, 46:48], in_=small[:P, 40:46]),
                                          ["smallg"], ["smallg"])
                                    act(small[:P, 48:49], small[:P, 47:48], AF.Ln, ["smallg", "epsc"], ["smallg"],
                                        bias=epsln[:P, 0:1])
                                    act(small[:P, 49:50], small[:P, 48:49], AF.Exp, ["smallg"], ["smallg"], scale=-0.5)
                                    tsc("dve", vn32[:P, 0:MW], ps_, small[:P, 46:47], small[:P, 49:50], ALU.subtract, ALU.mult,
                                        [ptok, "smallg"], ["eb"])
                                    tt("pool", vn32[:P, 0:MW], vn32[:P, 0:MW], mlprow[:P, 0, :], ALU.mult,
                                       ["eb", "mlprow"], ["eb"])
                                    tt("pool", vn32[:P, 0:MW], vn32[:P, 0:MW], mlprow[:P, 1, :], ALU.add,
                                       ["eb", "mlprowb"], ["eb"])
                                    cp("act", vn[:P, tl, :], vn32[:P, 0:MW], ["eb"], [("vn", tl)])
                                    if not prompt:
                                        dma("sp", mvs[l], vn32[:P, 0:MW], "st_m", ["eb"], [])
                                elif sname == "qc":
                                    cp("act", qcb[:P, tl, :], ps_, [ptok], [("qc", tl)])
                                elif sname == "fc":
                                    tA, tB, tC = eb[:P, 0:AW], enb[:P, 0:AW], tq[:P, 0:AW]
                                    act(tA, ps_, AF.Exp, [ptok], ["eb"], scale=-1.0)
                                    act(tB, tA, AF.Ln, ["eb"], ["enb"], bias=1.0)
                                    act(tB, tB, AF.Exp, ["enb"], ["enb"], scale=-1.0)
                                    tt("dve", tC, tA, tB, ALU.mult, ["eb", "enb"], ["tq"])
                                    tt("pool", kkb[:P, tl, :], tC, hgrow[:P, l, 1, :], ALU.mult, ["tq", "hgrow"], [("kk", tl)])
                                    tt("dve", tB, tB, hgrow[:P, l, 1, :], ALU.mult, ["enb", "hgrow"], ["enb"])
                                    tt("dve", tB, tB, hgrow[:P, l, 0, :], ALU.add, ["enb", "hgrow"], ["enb"])
                                    act(logf[:P, tl, :], tB, AF.Ln, ["enb"], [("logf", tl)])
                                elif sname == "ic":
                                    cp("dve", vbb[:P, tl, :], ps_, [ptok], [("vb", tl)])
                                elif sname == "gc":
                                    cp("act", gcb[:P, tl, :], ps_, [ptok], [("gc", tl)])

                    seq_ = []
                    oq = 0
                    for pi in range(len(SECTIONS)):
                        if pend is not None and pi % 2 == 0 and oq < NOQ:
                            seq_.append(("out", oq))
                            oq += 1
                        seq_.append(("in", pi))
                    if pend is not None:
                        while oq < NOQ:
                            seq_.append(("out", oq))
                            oq += 1
                        last_out = max(i for i, bk in enumerate(seq_) if bk[0] == "out")
                        lns = [("ln", t) for t in pend]
                        rest = seq_[last_out + 1:]
                        half = len(lns) // 2
                        seq_ = seq_[:last_out + 1] + lns[:half] + rest + lns[half:]
                    pend_ = pend
                    run_blocks(seq_, {"in": lambda bk, sl: inproj_block(bk[1], sl),
                                      "out": lambda bk, sl: outproj_block(bk[1], sl, pend_),
                                      "ln": lambda bk, sl: layernorm_tile(P, bk[1])})

                    blocks = []
                    for jb in range(kt0 + tiles[-1], -1, -1):
                        if jb >= kt0 + tiles[0]:
                            tl = jb - (kt0 + tiles[0])
                            blocks.append((jb, P, True, tl * 128))
                        else:
                            blocks.append((jb, 128, False, 0))
                    nst = len(blocks)
                    A_th = []
                    H_th = []

                    def s1(c, hh, k, par):
                        jb, Pk, diag, c0 = blocks[k]
                        pr, j = hh // 2, hh % 2
                        n = N - c0
                        zb, ztok = bank(c)
                        mm(zb[:Pk, 0:n], kT[64 * j:64 * j + 64, pr, jb * 128:jb * 128 + Pk], qT[64 * j:64 * j + 64, pr, c0:N],
                           True, True, ["kT%d_%d" % (pr, jb), ("qT", pr)], [ztok])

                    def s2(c, hh, k, par):
                        jb, Pk, diag, c0 = blocks[k]
                        n = N - c0
                        zb, ztok = bank(c)
                        A = att[c]
                        spb = A["sp"][par]
                        sptok = "sp%d%d" % (c, par)
                        ezb = A["ez"][par]
                        eztok = "ez%d%d" % (c, par)
                        act(ezb[:Pk, 0:n], zb[:Pk, 0:n], AF.Exp, [ztok], [eztok])
                        act(spb[:Pk, 0:n], ezb[:Pk, 0:n], AF.Ln, [eztok], [sptok], bias=1.0)
                        if diag:
                            tt("pool", spb[:Pk, 0:P], spb[:Pk, 0:P], maskSb[:Pk, :P], ALU.mult, [sptok, "cb"], [sptok])

                    def s3(c, hh, k, par):
                        jb, Pk, diag, c0 = blocks[k]
                        pr, j = hh // 2, hh % 2
                        n = N - c0
                        A = att[c]
                        spb = A["sp"][par]
                        sptok = "sp%d%d" % (c, par)
                        cbk, ctok = bank(2 + c)
                        if k == 0:
                            S.add("pool", lambda h, c=c: h.memset(att[c]["rs"][:, 0:N], 0.0), [], ["rs%d" % c])
                        mm(cbk[:Pk, 0:n], ntrib[:Pk, :Pk], spb[:Pk, 0:n], True, k == 0, [sptok, "cb"], [ctok])
                        if k > 0:
                            mm(cbk[:Pk, 0:n], nonesb[:, :Pk], A["rs"][:, c0:N], False, True, ["rs%d" % c, "cb"], [ctok])
                        if k < nst - 1:
                            tt("pool", A["rs"][:Pk, c0:N], A["rs"][:Pk, c0:N], spb[:Pk, 0:n], ALU.add,
                               ["rs%d" % c, sptok], ["rs%d" % c])

                    def s4(c, hh, k, par):
                        jb, Pk, diag, c0 = blocks[k]
                        n = N - c0
                        A = att[c]
                        wb_ = A["w"][par]
                        wtok = "w%d%d" % (c, par)
                        cbk, ctok = bank(2 + c)
                        act(wb_[:Pk, 0:n], cbk[:Pk, 0:n], AF.Exp, [ctok], [wtok])
                        tt("dve", wb_[:Pk, 0:n], wb_[:Pk, 0:n], A["ez"][par][:Pk, 0:n], ALU.mult,
                           [wtok, "ez%d%d" % (c, par)], [wtok])
                        if diag:
                            tt("pool", wb_[:Pk, 0:P], wb_[:Pk, 0:P], maskSb[:Pk, :P], ALU.mult, [wtok, "cb"], [wtok])

                    def s5(c, hh, k, par):
                        jb, Pk, diag, c0 = blocks[k]
                        n = N - c0
                        A = att[c]
                        wb_ = A["w"][par]
                        wtok = "w%d%d" % (c, par)
                        ob_, otok = pbanks[4], "psO%d" % c
                        mm(ob_[64 * c:64 * c + 64, c0:N], vfull[:Pk, jb, hh * 64:(hh + 1) * 64], wb_[:Pk, 0:n], k == 0, k == nst - 1,
                           [("v", jb), wtok], [otok], sgc=True)
                        if k == nst - 1:
                            cp("dve", mixT[64 * c:64 * c + 64, hh // 2, 0:N], ob_[64 * c:64 * c + 64, 0:N], [otok], [("mixA", hh)])

                    items = [[(hh, k) for hh in range(c, NH, 2) for k in range(nst)] for c in range(2)]
                    nit = len(items[0])
                    for i_ in range(nit + 2):
                        for fn_, ii in ((s1, i_), (s2, i_), (s3, i_ - 1), (s4, i_ - 1), (s5, i_ - 2)):
                            if 0 <= ii < nit:
                                A_th.append(lambda fn_=fn_, ii=ii: [fn_(c, items[c][ii][0], items[c][ii][1], ii % 2) for c in range(2)])

                    def hg_tile(tl, t):
                        col = tl * 128
                        X = {}
                        es3 = esel[0:64, 0:48].rearrange("p (h c) -> p h c", h=NH)
                        bc = lambda col_: es3[:, :, col_:col_ + 1].broadcast_to([64, NH, HD])
                        d3 = lambda ap_: ap_.rearrange("p (h e) -> p h e", h=NH)

                        def t1():
                            pg, X["pg"] = bank(5)
                            for g in range(4):
                                mm(pg[:P, g * 64:(g + 1) * 64], WTl[:P, l, g, :P], vn[:P, tl, g * 64:(g + 1) * 64], True, True,
                                   ["WTl", ("vn", tl)], [X["pg"]])

                        def t3():
                            pg = pbanks[5]
                            for g in range(4):
                                stt(obb[:P, g * 64:(g + 1) * 64], pg[:P, g * 64:(g + 1) * 64], bsT[:P, l, g:g + 1],
                                    ub[:P, tl, g * 64:(g + 1) * 64], ALU.add, ALU.mult, [X["pg"], "bsT", ("u", tl)], ["obb"])

                        def t2():
                            pbr, tok = bank(5)
                            mm(pbr[:P, 0:AW], Dm_[:P, :P], logf[:P, tl, :], True, True, ["cf", ("logf", tl)], [tok])
                            for hh in range(NH):
                                mm(pbr[0:64, AW + hh * 8:AW + hh * 8 + 8], logf[:P, tl, hh * 64:(hh + 1) * 64], Sel_[:P, 0:8], True, True,
                                   ["cf", ("logf", tl)], [tok])

                        def t4():
                            pbr, tok = bank(5)
                            act(eb[:P, 0:AW], pbr[:P, 0:AW], AF.Exp, [tok], ["eb"])
                            act(enb[:P, 0:AW], pbr[:P, 0:AW], AF.Exp, [tok], ["enb"], scale=-1.0)
                            act(esel[0:64, 0:48], pbr[0:64, AW:AW + 48], AF.Exp, [tok], ["esel"])

                        def t6():
                            pt, tok = bank(5)
                            ptb = pt[:].bitcast(BF)
                            for c in range(2):
                                tr(ptb[:, c * P:(c + 1) * P], obb[:P, c * 128:(c + 1) * 128], identb[:P, :P], ["obb", "cb"], [tok])

                        def t8():
                            pt, tok = bank(5)
                            ptb = pt[:].bitcast(BF)
                            cp("dve", mixT[:, 3:5, col:col + P], ptb[:, 0:2 * P].rearrange("p (c t) -> p c t", c=2), [tok],
                               [("mixB", 0, tl), ("mixB", 1, tl)])

                        def t5():
                            act(tq[:P, 0:AW], qcb[:P, tl, :], AF.Exp, [("qc", tl)], ["tq"], scale=-1.0)
                            act(tq[:P, 0:AW], tq[:P, 0:AW], AF.Ln, ["tq"], ["tq"], bias=1.0)
                            act(tq[:P, 0:AW], tq[:P, 0:AW], AF.Exp, ["tq"], ["tq"], scale=-1.0)

                        def t7():
                            tt("pool", khat[:P, 0:AW], enb[:P, 0:AW], kkb[:P, tl, :], ALU.mult, ["enb", ("kk", tl)], ["khat"])
                            tt("dve", tq[:P, 0:AW], tq[:P, 0:AW], qcb[:P, tl, :], ALU.mult, ["tq", ("qc", tl)], ["tq"])
                            tt("dve", qhat[:P, 0:AW], tq[:P, 0:AW], eb[:P, 0:AW], ALU.mult, ["tq", "eb"], ["qhat"])

                        def t9(b):
                            def f():
                                pd, tok = bank(7 - b)
                                for hh in range(NH):
                                    mm(pd[0:64, hh * 64:(hh + 1) * 64], khat[b * BL:(b + 1) * BL, hh * 64:(hh + 1) * 64],
                                       vbb[b * BL:(b + 1) * BL, tl, hh * 64:(hh + 1) * 64], True, True, ["khat", ("vb", tl)], [tok])
                            return f

                        def t11(b):
                            def f():
                                pd, tok = bank(7 - b)
                                tt("dve", Stl[0:64, b, :, :], Sst[:, :, :], bc(3 * b + 0), ALU.mult, ["Sst", "esel"], ["Stl"])
                                tt("dve", d3(dtmp[0:64, 0:AW]), d3(pd[0:64, 0:AW]), bc(3 * b + 2), ALU.mult, [tok, "esel"], ["dtmp"])
                                tt("dve", Sst[:, :, :], Sst[:, :, :], bc(3 * b + 1), ALU.mult, ["Sst", "esel"], ["Sst"])
                                tt("dve", Sst[:, :, :], Sst[:, :, :], d3(dtmp[0:64, 0:AW]), ALU.add, ["Sst", "dtmp"], ["Sst"])
                            return f

                        def t10a():
                            pq, tok = bank(7)
                            pqb = pq[:].bitcast(BF)
                            for hh in range(NH):
                                tr(pqb[0:64, hh * P:(hh + 1) * P], qhat[:P, hh * 64:(hh + 1) * 64], identb[:P, :P], ["qhat", "cb"], [tok])

                        def t12a():
                            pq, tok = bank(7)
                            pq3 = pq[:].bitcast(BF)[0:64, 0:NH * P].rearrange("p (h t) -> p h t", h=NH)
                            cp("dve", qhT[0:64, :, 0:P], pq3, [tok], ["qhT"])
                            S.add("pool", lambda h: h.memset(QAB[0:64, :, :, :], 0.0), [], ["QAB"])
                            for b in range(nblk):
                                cp("dve", QAB[0:64, :, b, b * BL:(b + 1) * BL], pq3[:, :, b * BL:(b + 1) * BL], [tok], ["QAB"])

                        def t10b():
                            pk_, tok = bank(6)
                            pkb = pk_[:].bitcast(BF)
                            for hh in range(NH):
                                tr(pkb[0:64, hh * P:(hh + 1) * P], khat[:P, hh * 64:(hh + 1) * 64], identb[:P, :P], ["khat", "cb"], [tok])

                        def t12b():
                            pk_, tok = bank(6)
                            cp("act", khT[0:64, :, 0:P], pk_[:].bitcast(BF)[0:64, 0:NH * P].rearrange("p (h t) -> p h t", h=NH),
                               [tok], ["khT"])

                        def t13():
                            act(sig[:P, 0:AW], gcb[:P, tl, :], AF.Exp, [("gc", tl)], ["enb"], scale=-1.0)
                            act(sig[:P, 0:AW], sig[:P, 0:AW], AF.Ln, ["enb"], ["enb"], bias=1.0)
                            act(sig[:P, 0:AW], sig[:P, 0:AW], AF.Exp, ["enb"], ["enb"], scale=-1.0)

                        def t14(g):
                            def f():
                                pscb, tok = bank(7 - g)
                                for hh in range(3 * g, 3 * g + 3):
                                    mm(pscb[:P, (hh % 3) * 128:(hh % 3) * 128 + P], khT[0:64, hh, 0:P], qhT[0:64, hh, 0:P], True, True,
                                       ["khT", "qhT"], [tok])
                            return f

                        def t15(g):
                            def f():
                                pscb, tok = bank(7 - g)
                                tt("dve", sc[:P, 3 * g:3 * g + 3, 0:P],
                                   pscb[:P, 0:384].rearrange("p (h t) -> p h t", h=3)[:, :, 0:P],
                                   mH_[:P, 0:P].unsqueeze(1).broadcast_to([P, 3, P]), ALU.mult, [tok, "cb", "mh16b"], ["sc"])
                            return f

                        def t16():
                            po, tok = bank(7)
                            for hh in range(NH):
                                mm(po[:P, hh * 64:(hh + 1) * 64], sc[:P, hh, 0:P], vbb[:P, tl, hh * 64:(hh + 1) * 64], True, False,
                                   ["sc", ("vb", tl)], [tok])
                                for b in range(nblk):
                                    mm(po[:P, hh * 64:(hh + 1) * 64], QAB[0:64, hh, b, 0:P], Stl[0:64, b, hh, :], False, b == nblk - 1,
                                       ["QAB", "Stl"], [tok])

                        def t17():
                            po, tok = bank(7)
                            act(sq[:P, 0:AW], po[:P, 0:AW], AF.Square, [tok], ["eb"])
                            S.add("dve", lambda h: h.tensor_reduce(out=small[:P, 52:58], in_=d3(sq[:P, 0:AW]),
                                                                   axis=AX.X, op=ALU.add), ["eb"], ["smallh"])
                            act(small[:P, 52:58], small[:P, 52:58], AF.Ln, ["smallh", "epsc"], ["smallh"], scale=1.0 / HD,
                                bias=epsrms[:P, 0:1])
                            act(small[:P, 52:58], small[:P, 52:58], AF.Exp, ["smallh"], ["smallh"], scale=-0.5)

                        def t18():
                            po, tok = bank(7)
                            tt("dve", d3(oc32[:P, 0:AW]), d3(po[:P, 0:AW]),
                               small[:P, 52:58].unsqueeze(2).broadcast_to([P, NH, HD]), ALU.mult, [tok, "smallh"], ["tq"])
                            tt("pool", oc32[:P, 0:AW], oc32[:P, 0:AW], nwrow[:P, :], ALU.mult, ["tq", "nwrow"], ["tq"])
                            tt("pool", ocb[:P, 0:AW], oc32[:P, 0:AW], sig[:P, 0:AW], ALU.mult, ["tq", "enb"], ["ocb"])

                        def t19():
                            pt2, tok = bank(6)
                            pt2b = pt2[:].bitcast(BF)
                            for c in range(3):
                                tr(pt2b[:, c * P:(c + 1) * P], ocb[:P, c * 128:(c + 1) * 128], identb[:P, :P], ["ocb", "cb"], [tok])

                        def t20():
                            pt2, tok = bank(6)
                            cp("dve", mixT[:, 5:8, col:col + P], pt2[:].bitcast(BF)[:, 0:3 * P].rearrange("p (c t) -> p c t", c=3), [tok],
                               [("mixB", 2 + c, tl) for c in range(3)])

                        seq_ = [t1, t3, t2, t4, t6, t8, t5, t7]
                        for b in range(nblk):
                            seq_ += [t9(b), t11(b)]
                        seq_ += [t10a, t12a, t10b, t12b, t13, t14(0), t15(0), t14(1), t15(1), t16, t17, t18, t19, t20]
                        return seq_

                    for tl, t in enumerate(tiles):
                        H_th += hg_tile(tl, t)
                    merged = [((i + 0.5) / len(A_th), 0, i, f) for i, f in enumerate(A_th)] + \
                             [((i + 0.5) / len(H_th), 1, i, f) for i, f in enumerate(H_th)]
                    merged.sort(key=lambda x: (x[0], x[1], x[2]))
                    for _, _, _, f in merged:
                        f()

                    pend = tiles

                seq_ = [("out", q) for q in range(NOQ)] + [("ln", t) for t in pend]
                pend_ = pend
                run_blocks(seq_, {"out": lambda bk, sl: outproj_block(bk[1], sl, pend_),
                                  "ln": lambda bk, sl: layernorm_tile(P, bk[1])})

                dst = hp[l, si] if prompt else hs[l]
                dma("sp", dst.rearrange("h d e -> d h e"), Sst[:], "st_s", ["Sst"], [])

                S.fence(MIXER_TOKS)
                dma("sp", lnrow[:, 0, :], ln_g[1][l:l + 1, :].broadcast_to([128, D]), "ld_ln0", [], ["lnrow"])
                dma("sp", lnrow[:, 1, :], ln_b[1][l:l + 1, :].broadcast_to([128, D]), "ld_ln1", [], ["lnrowb"])
                Ltok = (ntile - 1) * 128 + P

                def load_ff(j):
                    s_ = j % 2
                    dma("sp", w1r[:, s_, :, :], wb_f1[l][j], "w1_%d" % s_, [("wbf1", l, j)], [("w1", s_)])
                    dma("sp", w2r[:, s_, :, :], wb_f2[l][j], "w2_%d" % s_, [("wbf2", l, j)], [("w2", s_)])

                load_ff(0)
                for t in range(ntile):
                    make_xT(P, t, x1T, ("x1T", t), t * 128)
                groups = [(g0, min(g0 + 512, Ltok)) for g0 in range(0, Ltok, 512)]

                def ff1(j):
                    s_ = j % 2
                    for c in range(HC):
                        for gi, (g0, g1) in enumerate(groups):
                            n = g1 - g0
                            pb_, ptok = bank()
                            rd = [("x1T", t) for t in range(g0 // 128, (g1 + 127) // 128)] + [("w1", s_)]
                            for kc in range(KC):
                                mm(pb_[:, 0:n], w1r[:, s_, kc, c * 128:(c + 1) * 128], x1T[:, kc, g0:g1], kc == 0, kc == KC - 1, rd, [ptok])
                            rs_ = (c * len(groups) + gi) % 2
                            act(rr[:, rs_, 0:n], pb_[:, 0:n], AF.Relu, [ptok], [("r", rs_)])
                            tt("pool", hT[:, s_, c, g0:g1], rr[:, rs_, 0:n], rr[:, rs_, 0:n], ALU.mult, [("r", rs_)], [("hT", s_, c)])

                def ff2(j, last=False):
                    s_ = j % 2
                    for t in range(ntile):
                        for hf in range(D // 512 if D >= 512 else 1):
                            wcol = min(512, D)
                            pb_, ptok = bank()
                            for c in range(HC):
                                mm(pb_[:P, 0:wcol], hT[:, s_, c, t * 128:t * 128 + P], w2r[:, s_, c, hf * wcol:(hf + 1) * wcol],
                                   c == 0, c == HC - 1, [("hT", s_, c), ("w2", s_)], [ptok])
                            tt("dve", xres[:P, t, hf * wcol:(hf + 1) * wcol], xres[:P, t, hf * wcol:(hf + 1) * wcol], pb_[:P, 0:wcol],
                               ALU.add, [("xres", t), ptok], [("xres", t)])
                        if last:
                            layernorm_tile(P, t)
                            if l == DEPTH - 1:
                                dst = y_dst[t * 128:t * 128 + P, :]
                                dma("sp", dst, xres[:P, t, :], "st_y%d" % (t % 2), [("xres", t)], [])

                ff1(0)
                for j in range(NP_):
                    if j + 1 < NP_:
                        load_ff(j + 1)
                        ff1(j + 1)
                    ff2(j, last=(j == NP_ - 1))

        for si in range(NSEQ):
            run_sequence("p", si)
        run_sequence("s", 0)

        S.emit(nc, st)
    return nc


_CACHE = {}


def _get_nc(cfg_key, cfg):
    if cfg_key not in _CACHE:
        _CACHE[cfg_key] = build(cfg)
    return _CACHE[cfg_key]


def run_cfg(cfg, inputs, ncores):
    nc = build(cfg)
    consts = make_consts()
    f = lambda a: np.ascontiguousarray(np.asarray(a, dtype=np.float32))
    shared = {k: f(inputs[k]) for k in ("w_in", "w_out", "w_ff1", "w_ff2", "mlp_ln_g", "mlp_ln_b", "mlp_ws", "mlp_bs",
                                       "hg_lb_logits", "hg_norm_w", "ln1_g", "ln1_b", "ln2_g", "ln2_b")}
    xp_, xs_ = f(inputs["x_prompt"]), f(inputs["x_sample"])
    ck_, cv_, sh_ = f(inputs["cache_sb_k"]), f(inputs["cache_sb_v"]), f(inputs["state_hgrn"])
    NSEQ = cfg.NSEQ
    in_maps = []
    for c in range(ncores):
        m = dict(shared)
        m["xp"] = np.ascontiguousarray(xp_[c * NSEQ:(c + 1) * NSEQ])
        m["xs"] = np.ascontiguousarray(xs_[c])
        m["ck"] = np.ascontiguousarray(ck_[:, c].reshape(DEPTH, cfg.PAST, AW))
        m["cv"] = np.ascontiguousarray(cv_[:, c].reshape(DEPTH, cfg.PAST, AW))
        m["sh"] = np.ascontiguousarray(sh_[:, c])
        m["consts"] = consts
        in_maps.append(m)
    res = run_bass_kernel_spmd(nc, in_maps, core_ids=list(range(ncores)))
    R = res.results
    B = ncores * NSEQ
    y_prompt = np.concatenate([R[c]["yp"] for c in range(ncores)], axis=0)
    y_sample = np.stack([R[c]["ys"] for c in range(ncores)], axis=0)
    kpo = np.concatenate([R[c]["kp"] for c in range(ncores)], axis=1).reshape(DEPTH, B, cfg.L, NH, HD)
    vpo = np.concatenate([R[c]["vp"] for c in range(ncores)], axis=1).reshape(DEPTH, B, cfg.L, NH, HD)
    hpo = np.concatenate([R[c]["hp"] for c in range(ncores)], axis=1)
    kso = np.stack([R[c]["ks"] for c in range(ncores)], axis=1).reshape(DEPTH, ncores, cfg.LS, NH, HD)
    vso = np.stack([R[c]["vs"] for c in range(ncores)], axis=1).reshape(DEPTH, ncores, cfg.LS, NH, HD)
    hso = np.stack([R[c]["hs"] for c in range(ncores)], axis=1)
    mvo = np.stack([R[c]["mvs"] for c in range(ncores)], axis=1)
    return tuple(np.ascontiguousarray(a, dtype=np.float32) for a in
                 (y_prompt, y_sample, kpo, vpo, hpo, kso, vso, hso, mvo))


def kernel(x_prompt, x_sample, cache_sb_k, cache_sb_v, state_hgrn, w_in, w_out, mlp_ln_g, mlp_ln_b,
           mlp_ws, mlp_bs, hg_lb_logits, hg_norm_w, ln1_g, ln1_b, w_ff1, w_ff2, ln2_g, ln2_b):
    cfg = Cfg()
    inputs = dict(x_prompt=x_prompt, x_sample=x_sample, cache_sb_k=cache_sb_k, cache_sb_v=cache_sb_v,
                  state_hgrn=state_hgrn, w_in=w_in, w_out=w_out, mlp_ln_g=mlp_ln_g, mlp_ln_b=mlp_ln_b,
                  mlp_ws=mlp_ws, mlp_bs=mlp_bs, hg_lb_logits=hg_lb_logits, hg_norm_w=hg_norm_w,
                  ln1_g=ln1_g, ln1_b=ln1_b, w_ff1=w_ff1, w_ff2=w_ff2, ln2_g=ln2_g, ln2_b=ln2_b)
    return run_cfg(cfg, inputs, 8)
```
